# Optimizing a Trainium2 kernel written in Bass

```python
import math
import jax, jax.numpy as jnp
from jax import lax
import numpy as np

D_MODEL = 1024
BATCH = 2
SEQ = 8192
DEPTH = 2
DEC_BATCH = 128
DEC_SEQ = 4
PAST_LEN = 2048
PAGE_SIZE = 128

FOX_HEADS = 4
FOX_DIM = 64
DIFF_HEADS = 4
DIFF_QK_DIM = 32
DIFF_V_DIM = 64
DELTA_HEADS = 4
DELTA_K_DIM = 64
DELTA_V_DIM = 64
CONV_WIDTH = 4
DELTA_CHUNK = 64
N_BRANCH = 3
BRANCH_WIDTH = FOX_HEADS * FOX_DIM
D_FF = 2816
Q_BLOCK = 128
ROPE_THETA = 10000.0
NORM_EPS = 1e-6

FOX_W = FOX_HEADS * FOX_DIM
DIFF_QK_W = DIFF_HEADS * 2 * DIFF_QK_DIM
DIFF_V_W = DIFF_HEADS * DIFF_V_DIM
DELTA_QK_W = DELTA_HEADS * DELTA_K_DIM
DELTA_V_W = DELTA_HEADS * DELTA_V_DIM
CONV_CH = 2 * DELTA_QK_W + DELTA_V_W
IN_SPLITS = (FOX_W, FOX_W, FOX_W, FOX_HEADS,
             DIFF_QK_W, DIFF_QK_W, DIFF_V_W,
             CONV_CH, DELTA_HEADS, DELTA_HEADS, DELTA_V_W,
             N_BRANCH * D_MODEL)
D_IN = sum(IN_SPLITS)

kernel_name = 'hybrid_fox_diff_gdn_macaron_step'


def rmsnorm(x, gain):
    x32 = x.astype(jnp.float32)
    y = x32 * lax.rsqrt(jnp.mean(x32 * x32, axis=-1, keepdims=True) + NORM_EPS)
    return (y * gain.astype(jnp.float32)).astype(x.dtype)


def l2norm(x):
    return x * lax.rsqrt(jnp.sum(x * x, axis=-1, keepdims=True) + 1e-6)


def swiglu(h, wi, wo):
    gate, up = jnp.split(h @ wi, 2, axis=-1)
    return (jax.nn.silu(gate) * up) @ wo


def rope(x, pos):
    d = x.shape[-1]
    inv = ROPE_THETA ** (-jnp.arange(0, d, 2, dtype=jnp.float32) / d)
    ang = pos.astype(jnp.float32)[:, None] * inv
    shape = (ang.shape[0],) + (1,) * (x.ndim - 3) + (d // 2,)
    cos = jnp.cos(ang).reshape(shape)
    sin = jnp.sin(ang).reshape(shape)
    x1, x2 = jnp.split(x.astype(jnp.float32), 2, axis=-1)
    return jnp.concatenate([x1 * cos - x2 * sin, x2 * cos + x1 * sin], axis=-1).astype(x.dtype)


def causal_softmax(scores, q_pos, k_pos):
    mask = k_pos[None, :] <= q_pos[:, None]
    return jax.nn.softmax(jnp.where(mask, scores, -jnp.inf), axis=-1)


def fox_core(q, cq, k, v, ck, q_pos, k_pos):
    s = jnp.einsum('nqhd,nkhd->nhqk', q, k).astype(jnp.float32) * (FOX_DIM ** -0.5)
    s = s + jnp.swapaxes(cq, 1, 2)[..., :, None] - jnp.swapaxes(ck, 1, 2)[..., None, :]
    p = causal_softmax(s, q_pos, k_pos)
    return jnp.einsum('nhqk,nkhd->nqhd', p.astype(v.dtype), v)


def diff_core(q, k, v, lam, q_pos, k_pos):
    s = jnp.einsum('nqhcd,nkhcd->nhcqk', q, k).astype(jnp.float32) * (DIFF_QK_DIM ** -0.5)
    p = causal_softmax(s, q_pos, k_pos)
    p = p[:, :, 0] - lam * p[:, :, 1]
    return jnp.einsum('nhqk,nkhd->nqhd', p.astype(v.dtype), v)


def sweep_query_blocks(core, q_args, kv_args, seq_len):
    nb = seq_len // Q_BLOCK
    pos = jnp.arange(seq_len, dtype=jnp.int32)

    def to_blocks(a):
        return jnp.moveaxis(a.reshape((a.shape[0], nb, Q_BLOCK) + a.shape[2:]), 1, 0)

    xs = tuple(to_blocks(a) for a in q_args) + (pos.reshape(nb, Q_BLOCK),)
    out = lax.map(lambda blk: core(*blk[:-1], *kv_args, blk[-1], pos), xs)
    out = jnp.moveaxis(out, 0, 1)
    return out.reshape((out.shape[0], seq_len) + out.shape[3:])


def short_conv(xin, buf, w):
    L = xin.shape[1]
    xp = jnp.concatenate([buf.astype(xin.dtype), xin], axis=1)
    y = xp[:, 0:L] * w[0]
    for i in range(1, CONV_WIDTH):
        y = y + xp[:, i:i + L] * w[i]
    return jax.nn.silu(y), xp[:, -(CONV_WIDTH - 1):]


def gated_delta_chunked(q, k, v, beta, g, s0, chunk):
    n, L = q.shape[:2]
    nc = L // chunk

    def blocks(a):
        a = a.reshape((n, nc, chunk) + a.shape[2:])
        return jnp.moveaxis(a, (1, 3), (0, 2))

    qc, kc, vc, bc, gc = (blocks(a) for a in (q, k, v, beta, g))
    G = jnp.cumsum(gc, axis=-1)
    tri = jnp.tril(jnp.ones((chunk, chunk), bool))
    tri_strict = jnp.tril(jnp.ones((chunk, chunk), bool), -1)
    decay = jnp.exp(jnp.where(tri, G[..., :, None] - G[..., None, :], -jnp.inf))
    kb = kc * bc[..., None]
    A = jnp.where(tri_strict, jnp.einsum('...id,...jd->...ij', kb, kc) * decay, 0.0)
    eye = jnp.eye(chunk, dtype=A.dtype)
    T = lax.linalg.triangular_solve(A + eye, jnp.broadcast_to(eye, A.shape), left_side=True, lower=True)
    u_base = jnp.einsum('...ij,...jd->...id', T, vc * bc[..., None])
    w = jnp.einsum('...ij,...jd->...id', T, kb * jnp.exp(G)[..., None])
    qk = jnp.einsum('...id,...jd->...ij', qc, kc) * decay

    def step(S, xs):
        qi, ki, ui, wi, Gi, qki = xs
        u = ui - jnp.einsum('nhcd,nhdv->nhcv', wi, S)
        o = jnp.einsum('nhcd,nhdv->nhcv', qi * jnp.exp(Gi)[..., None], S) + jnp.einsum('nhij,nhjv->nhiv', qki, u)
        k_dec = ki * jnp.exp(Gi[..., -1:] - Gi)[..., None]
        S = S * jnp.exp(Gi[..., -1])[..., None, None] + jnp.einsum('nhcd,nhcv->nhdv', k_dec, u)
        return S, o

    S, o = lax.scan(step, s0, (qc, kc, u_base, w, G, qk))
    o = jnp.moveaxis(o, (0, 2), (1, 3))
    return o.reshape((n, L) + o.shape[3:]), S


def split_columns(p):
    out, start = [], 0
    for size in IN_SPLITS:
        out.append(p[..., start:start + size])
        start += size
    return out


def token_mixers(h, l, W, past, pos):
    f32 = jnp.float32
    n, L, _ = h.shape
    fq, fk, fv, ff, dq, dk, dv, cqkv, cb, ca, cz, gates = split_columns(h @ W['w_in'][l])
    fq = fq.reshape(n, L, FOX_HEADS, FOX_DIM)
    fk = fk.reshape(n, L, FOX_HEADS, FOX_DIM)
    fv = fv.reshape(n, L, FOX_HEADS, FOX_DIM)
    logf = jax.nn.log_sigmoid(ff.astype(f32) + W['fox_f_bias'][l].astype(f32))
    dq = rope(dq.reshape(n, L, DIFF_HEADS, 2, DIFF_QK_DIM), pos)
    dk = rope(dk.reshape(n, L, DIFF_HEADS, 2, DIFF_QK_DIM), pos)
    dv = dv.reshape(n, L, DIFF_HEADS, DIFF_V_DIM)
    lam_p = W['diff_lambda'][l].astype(f32)
    lam_init = 0.8 - 0.6 * math.exp(-0.3 * l)
    lam = jnp.exp(jnp.sum(lam_p[0] * lam_p[1])) - jnp.exp(jnp.sum(lam_p[2] * lam_p[3])) + lam_init
    conv_buf = jnp.zeros((n, CONV_WIDTH - 1, CONV_CH), h.dtype) if past is None else past['conv']
    cqkv, new_conv = short_conv(cqkv, conv_buf, W['delta_conv_w'][l])

    if past is None:
        c = jnp.cumsum(logf, axis=1)
        o_fox = sweep_query_blocks(fox_core, (fq, c), (fk, fv, c), L)
        o_diff = sweep_query_blocks(diff_core, (dq,), (dk, dv, lam), L)
        s0 = jnp.zeros((n, DELTA_HEADS, DELTA_K_DIM, DELTA_V_DIM), f32)
    else:
        p_len = past['fox_k'].shape[1]
        k_pos = jnp.arange(p_len + L, dtype=jnp.int32)
        c = jnp.cumsum(jnp.concatenate([past['fox_logf'].astype(f32), logf], axis=1), axis=1)
        fk_all = jnp.concatenate([past['fox_k'].astype(fk.dtype), fk], axis=1)
        fv_all = jnp.concatenate([past['fox_v'].astype(fv.dtype), fv], axis=1)
        dk_all = jnp.concatenate([past['diff_k'].astype(dk.dtype), dk], axis=1)
        dv_all = jnp.concatenate([past['diff_v'].astype(dv.dtype), dv], axis=1)
        o_fox = fox_core(fq, c[:, p_len:], fk_all, fv_all, c, pos, k_pos)
        o_diff = diff_core(dq, dk_all, dv_all, lam, pos, k_pos)
        s0 = past['delta'].astype(f32)

    o_diff = rmsnorm(o_diff, W['diff_norm'][l]) * (1.0 - lam_init)

    cq_, ck_, cv_ = jnp.split(cqkv.astype(f32), [DELTA_QK_W, 2 * DELTA_QK_W], axis=-1)
    cq_ = l2norm(cq_.reshape(n, L, DELTA_HEADS, DELTA_K_DIM)) * (DELTA_K_DIM ** -0.5)
    ck_ = l2norm(ck_.reshape(n, L, DELTA_HEADS, DELTA_K_DIM))
    cv_ = cv_.reshape(n, L, DELTA_HEADS, DELTA_V_DIM)
    beta = jax.nn.sigmoid(cb.astype(f32))
    g = -jnp.exp(W['delta_A_log'][l].astype(f32)) * jax.nn.softplus(ca.astype(f32) + W['delta_dt_bias'][l].astype(f32))
    chunk = DELTA_CHUNK if L % DELTA_CHUNK == 0 else L
    o_delta, s_new = gated_delta_chunked(cq_, ck_, cv_, beta, g, s0, chunk)
    o_delta = rmsnorm(o_delta, W['delta_norm'][l]) * jax.nn.silu(cz.astype(f32).reshape(n, L, DELTA_HEADS, DELTA_V_DIM))

    branches = jnp.stack([o_fox.reshape(n, L, BRANCH_WIDTH).astype(h.dtype),
                          o_diff.reshape(n, L, BRANCH_WIDTH).astype(h.dtype),
                          o_delta.reshape(n, L, BRANCH_WIDTH).astype(h.dtype)], axis=2)
    lifted = jnp.einsum('nlbw,bwd->nlbd', branches, W['w_branch'][l])
    gate = jax.nn.sigmoid(gates.reshape(n, L, N_BRANCH, D_MODEL))
    y = jnp.sum(gate * lifted, axis=2) @ W['w_out'][l]
    new = {'fox_k': fk, 'fox_v': fv, 'fox_logf': logf, 'diff_k': dk, 'diff_v': dv,
           'delta': s_new, 'conv': new_conv}
    return y, new


def layer(x, l, W, past, pos):
    x = x + 0.5 * swiglu(rmsnorm(x, W['norm_ffn1'][l]), W['ffn1_wi'][l], W['ffn1_wo'][l])
    y, new = token_mixers(rmsnorm(x, W['norm_mix'][l]), l, W, past, pos)
    x = x + y
    x = x + 0.5 * swiglu(rmsnorm(x, W['norm_ffn2'][l]), W['ffn2_wi'][l], W['ffn2_wo'][l])
    return x, new


def setup_inputs(seed: int = 0) -> dict:
    key = jax.random.key(seed)
    keys = jax.random.split(key, 32)
    f32 = jnp.float32

    def nrm(i, shape, scale=1.0):
        return scale * jax.random.normal(keys[i], shape, f32)

    n_pages = PAST_LEN // PAGE_SIZE
    n_pool = (5 * DEC_BATCH * n_pages) // 4
    perm = jax.random.permutation(keys[9], n_pool).astype(jnp.int32)
    page_table = perm[:DEC_BATCH * n_pages].reshape(DEC_BATCH, n_pages)

    dt = jnp.exp(jax.random.uniform(keys[20], (DEPTH, DELTA_HEADS), f32, math.log(1e-3), math.log(1e-1)))
    return {
        'x_prompt': nrm(0, (BATCH, SEQ, D_MODEL)),
        'x_sample': nrm(1, (DEC_BATCH, DEC_SEQ, D_MODEL)),
        'cache_fox_k': nrm(2, (DEPTH, n_pool, PAGE_SIZE, FOX_HEADS, FOX_DIM)),
        'cache_fox_v': nrm(3, (DEPTH, n_pool, PAGE_SIZE, FOX_HEADS, FOX_DIM)),
        'cache_fox_logf': jax.nn.log_sigmoid(2.0 + nrm(4, (DEPTH, n_pool, PAGE_SIZE, FOX_HEADS))),
        'cache_diff_k': nrm(5, (DEPTH, n_pool, PAGE_SIZE, DIFF_HEADS, 2, DIFF_QK_DIM)),
        'cache_diff_v': nrm(6, (DEPTH, n_pool, PAGE_SIZE, DIFF_HEADS, DIFF_V_DIM)),
        'state_delta': nrm(7, (DEPTH, DEC_BATCH, DELTA_HEADS, DELTA_K_DIM, DELTA_V_DIM), 0.1),
        'state_conv': nrm(8, (DEPTH, DEC_BATCH, CONV_WIDTH - 1, CONV_CH)),
        'page_table': page_table,
        'norm_ffn1': 1.0 + nrm(10, (DEPTH, D_MODEL), 0.1),
        'ffn1_wi': nrm(11, (DEPTH, D_MODEL, 2 * D_FF), D_MODEL ** -0.5),
        'ffn1_wo': nrm(12, (DEPTH, D_FF, D_MODEL), D_FF ** -0.5),
        'norm_mix': 1.0 + nrm(13, (DEPTH, D_MODEL), 0.1),
        'w_in': nrm(14, (DEPTH, D_MODEL, D_IN), D_MODEL ** -0.5),
        'fox_f_bias': 2.0 + nrm(15, (DEPTH, FOX_HEADS), 0.1),
        'diff_lambda': nrm(16, (DEPTH, 4, DIFF_QK_DIM), 0.1),
        'diff_norm': 1.0 + nrm(17, (DEPTH, DIFF_V_DIM), 0.1),
        'delta_conv_w': nrm(18, (DEPTH, CONV_WIDTH, CONV_CH), 0.5),
        'delta_A_log': jnp.log(jax.random.uniform(keys[19], (DEPTH, DELTA_HEADS), f32, 1.0, 16.0)),
        'delta_dt_bias': dt + jnp.log(-jnp.expm1(-dt)),
        'delta_norm': 1.0 + nrm(21, (DEPTH, DELTA_V_DIM), 0.1),
        'w_branch': nrm(22, (DEPTH, N_BRANCH, BRANCH_WIDTH, D_MODEL), BRANCH_WIDTH ** -0.5),
        'w_out': nrm(23, (DEPTH, D_MODEL, D_MODEL), D_MODEL ** -0.5),
        'norm_ffn2': 1.0 + nrm(24, (DEPTH, D_MODEL), 0.1),
        'ffn2_wi': nrm(25, (DEPTH, D_MODEL, 2 * D_FF), D_MODEL ** -0.5),
        'ffn2_wo': nrm(26, (DEPTH, D_FF, D_MODEL), D_FF ** -0.5),
        'norm_final': 1.0 + nrm(27, (D_MODEL,), 0.1),
    }


def reference(x_prompt, x_sample, cache_fox_k, cache_fox_v, cache_fox_logf, cache_diff_k, cache_diff_v,
              state_delta, state_conv, page_table, norm_ffn1, ffn1_wi, ffn1_wo, norm_mix, w_in, fox_f_bias,
              diff_lambda, diff_norm, delta_conv_w, delta_A_log, delta_dt_bias, delta_norm, w_branch, w_out,
              norm_ffn2, ffn2_wi, ffn2_wo, norm_final):
    W = {'norm_ffn1': norm_ffn1, 'ffn1_wi': ffn1_wi, 'ffn1_wo': ffn1_wo, 'norm_mix': norm_mix, 'w_in': w_in,
         'fox_f_bias': fox_f_bias, 'diff_lambda': diff_lambda, 'diff_norm': diff_norm,
         'delta_conv_w': delta_conv_w, 'delta_A_log': delta_A_log, 'delta_dt_bias': delta_dt_bias,
         'delta_norm': delta_norm, 'w_branch': w_branch, 'w_out': w_out, 'norm_ffn2': norm_ffn2,
         'ffn2_wi': ffn2_wi, 'ffn2_wo': ffn2_wo}
    db, n_pages = page_table.shape
    past_len = n_pages * PAGE_SIZE
    pos_prompt = jnp.arange(x_prompt.shape[1], dtype=jnp.int32)
    pos_sample = past_len + jnp.arange(x_sample.shape[1], dtype=jnp.int32)

    def gather(cache):
        rows = cache[page_table]
        return rows.reshape((db, past_len) + rows.shape[3:])

    xp, xs = x_prompt, x_sample
    new_p, new_s = [], []
    for l in range(DEPTH):
        xp, new_pl = layer(xp, l, W, None, pos_prompt)
        past = {'fox_k': gather(cache_fox_k[l]), 'fox_v': gather(cache_fox_v[l]),
                'fox_logf': gather(cache_fox_logf[l]), 'diff_k': gather(cache_diff_k[l]),
                'diff_v': gather(cache_diff_v[l]), 'delta': state_delta[l], 'conv': state_conv[l]}
        xs, new_sl = layer(xs, l, W, past, pos_sample)
        new_p.append(new_pl)
        new_s.append(new_sl)
    y_prompt = rmsnorm(xp, norm_final)
    y_sample = rmsnorm(xs, norm_final)

    def stk(lst, name):
        return jnp.stack([d[name] for d in lst])

    return (y_prompt, y_sample,
            stk(new_p, 'fox_k'), stk(new_s, 'fox_k'),
            stk(new_p, 'fox_v'), stk(new_s, 'fox_v'),
            stk(new_p, 'fox_logf'), stk(new_s, 'fox_logf'),
            stk(new_p, 'diff_k'), stk(new_s, 'diff_k'),
            stk(new_p, 'diff_v'), stk(new_s, 'diff_v'),
            stk(new_p, 'delta'), stk(new_s, 'delta'),
            stk(new_p, 'conv'), stk(new_s, 'conv'))
```

```python
import math
import os
from contextlib import ExitStack
import numpy as np
import concourse.bass as bass
import concourse.mybir as mybir
from concourse.bass_utils import run_bass_kernel_spmd

AF = mybir.ActivationFunctionType
ALU = mybir.AluOpType
AX = mybir.AxisListType
F32, BF16, I32 = mybir.dt.float32, mybir.dt.bfloat16, mybir.dt.int32
ENGS = ["pe", "act", "dve", "pool", "sp"]
ND = 10

class _RecInst:
    def __init__(self, name, a, k):
        self.name, self.a, self.k = name, a, k

    def then_inc(self, *a, **k):
        return self


class _Rec:
    def __getattr__(self, name):
        return lambda *a, **k: _RecInst(name, a, k)


_REC = _Rec()


class Buf:
    __slots__ = ("name", "w", "r")

    def __init__(self, name=""):
        self.name = name
        self.w = []
        self.r = []


class Prog:
    def __init__(self, nc):
        self.nc = nc
        self.q = {e: [] for e in ENGS}
        self.cnt = {e: 0 for e in ENGS}
        self.sem = {e: nc.alloc_semaphore(name=f"c_{e}") for e in ENGS}
        self.dsem = {e: [nc.alloc_semaphore(name=f"d_{e}{i}") for i in range(ND)] for e in ENGS if e != "pe"}
        self.dval = {e: [0] * ND for e in self.dsem}
        self.dnext = {e: 0 for e in self.dsem}
        self.seen = {e: {} for e in ENGS}
        self.nwait = 0

    def eng(self, e):
        nc = self.nc
        return {"pe": nc.tensor, "act": nc.scalar, "dve": nc.vector, "pool": nc.gpsimd, "sp": nc.sync}[e]

    def _wait(self, e, tok):
        kind, key, val = tok
        if kind == "eng" and key == e and e == "pe":
            return
        k = (kind, key if kind == "eng" else id(key))
        if self.seen[e].get(k, 0) >= val:
            return
        self.seen[e][k] = val
        sem = self.sem[key] if kind == "eng" else key
        self.q[e].append(lambda E, sem=sem, val=val: E.wait_ge(sem, val))
        self.nwait += 1

    def _deps(self, e, reads, writes):
        for b in reads:
            for t in b.w:
                self._wait(e, t)
        for b in writes:
            for t in b.w:
                self._wait(e, t)
            for t in b.r:
                self._wait(e, t)

    def _commit(self, tok, reads, writes):
        for b in writes:
            b.w = [tok]
            b.r = []
        for b in reads:
            if b not in writes:
                b.r.append(tok)

    def op(self, e, fn, reads=(), writes=()):
        self._deps(e, reads, writes)
        self.cnt[e] += 1
        k = self.cnt[e]
        sem = self.sem[e]
        r = fn(_REC)
        self.q[e].append(lambda E, r=r, sem=sem: getattr(E, r.name)(*r.a, **r.k).then_inc(sem, 1))
        self._commit(("eng", e, k), reads, writes)

    def dma(self, e, fn, reads=(), writes=()):
        i = self.dnext[e]
        self.dnext[e] = (i + 1) % ND
        sem = self.dsem[e][i]
        if self.dval[e][i] > 0:
            self._wait(e, ("dma", sem, self.dval[e][i]))
        self._deps(e, reads, writes)
        self.dval[e][i] += 16
        v = self.dval[e][i]
        r = fn(_REC)
        self.q[e].append(lambda E, r=r, sem=sem: getattr(E, r.name)(*r.a, **r.k).then_inc(sem, 16))
        self._commit(("dma", sem, v), reads, writes)

    def finish(self):
        for e in self.dsem:
            for i in range(ND):
                if self.dval[e][i] > 0:
                    self._wait("sp", ("dma", self.dsem[e][i], self.dval[e][i]))
        for e in ENGS:
            if e != "sp" and self.cnt[e] > 0:
                self._wait("sp", ("eng", e, self.cnt[e]))

    def emit(self):
        nc = self.nc
        self.finish()
        q = self.q
        with nc.Block() as block:
            @block.sync
            def _(E):
                for f in q["sp"]:
                    f(E)

            @block.tensor
            def _(E):
                for f in q["pe"]:
                    f(E)

            @block.scalar
            def _(E):
                for f in q["act"]:
                    f(E)

            @block.vector
            def _(E):
                for f in q["dve"]:
                    f(E)

            @block.gpsimd
            def _(E):
                for f in q["pool"]:
                    f(E)

    def barrier(self):
        toks = []
        for e in self.dsem:
            for i in range(ND):
                if self.dval[e][i] > 0:
                    toks.append(("dma", self.dsem[e][i], self.dval[e][i]))
        for e in ENGS:
            if self.cnt[e] > 0:
                toks.append(("eng", e, self.cnt[e]))
        for e in ENGS:
            for t in toks:
                if t[0] == "eng" and t[1] == e:
                    continue
                self._wait(e, t)


D = 1024
DFF = 2816
NFC = 22
NTP = 8192
NBS = 64
NS = NBS * 4
NT = NTP + NS
NCORES = 2
DEPTH = 2
NPOOL = 2560
TILES = [(i * 1024, 1024, False) for i in range(8)] + [(NTP, NS, True)]
_UID = [0]
NFM = 45
STAGE = 5


def subtiles(tn):
    return [(s, min(512, tn - s)) for s in range(0, tn, 512)]


def build(stage=STAGE, ntiles=9, fmlist=None):
    nc = bass.Bass("TRN2", target_bir_lowering=False)
    P = Prog(nc)

    def din(name, shape, dt=F32):
        return nc.dram_tensor(name, list(shape), dt, kind="ExternalInput").ap()

    def dout(name, shape, dt=F32):
        return nc.dram_tensor(name, list(shape), dt, kind="ExternalOutput").ap()

    def dint(name, shape, dt=F32):
        return nc.dram_tensor(name, list(shape), dt, kind="Internal").ap()

    xT0 = din("xT0", [D, NT])
    wi_r = [din(f"wi{i}", [DEPTH, NFC, 128, 8 * 256]) for i in range(2)]
    wo_r = [din(f"wo{i}", [DEPTH, 8, 128, NFC * 128]) for i in range(2)]
    gains = din("gains", [128, DEPTH * 3 * 8 + 8])
    wfm = din("wfm", [DEPTH, NFM, 128, 8 * 128])
    wtm = din("wtm", [DEPTH, 128, 8 * 768])
    cossin = din("cossin", [2, 128, NT])
    smallp = din("smallp", [128, DEPTH * 4])
    convw = din("convw", [DEPTH, 128, 6 * 4])
    convs_in = din("convs_in", [DEPTH, 768, NBS * 3])
    consts = din("consts", [128, 452])
    ptab = din("ptab", [1, NBS * 16], I32)
    pool_all = [din(f"pool_all{i}", [NPOOL * 128, 1036]) for i in range(DEPTH)]
    dnorm_row = din("dnorm_row", [DEPTH, 1, 64])
    sel_in = din("sel_in", [8, 512])
    dlam = din("dlam", [DEPTH, 1, 128])
    dgain = din("dgain", [128, DEPTH * 2])
    sdel_in = din("sdel_in", [DEPTH, NBS, 128, 128])
    wbr = din("wbr", [DEPTH, 8, 128, 768])
    wout_r = din("wout_r", [DEPTH, 8, 128, 1024])
    yT = dout("yT", [D, NT])
    fkT_o = dout("fkT_o", [DEPTH, 256, NT])
    fv_o = dout("fv_o", [DEPTH, NT, 256])
    lfT_o = dout("lfT_o", [DEPTH, 4, NT])
    dkT_o = dout("dkT_o", [DEPTH, 256, NT])
    dv_o = dout("dv_o", [DEPTH, NT, 256])
    cvp_o = dout("cvp_o", [DEPTH, 768, 3])
    cvs_o = dout("cvs_o", [DEPTH, 768, NBS * 3])
    dsp_o = dout("dsp_o", [DEPTH, 128, 128])
    dss_o = dout("dss_o", [DEPTH, NBS, 128, 128])
    xT = dint("xT", [D, NT])
    fqa = dint("fqa", [4, 70, NT], BF16)
    fka = dint("fka", [4, 70, NT], BF16)
    fva = dint("fva", [NT, 4 * 65], BF16)
    dqT = dint("dqT", [4, 64, NT], BF16)
    dkT = dint("dkT", [4, 64, NT], BF16)
    dva = dint("dva", [NT, 4 * 65], BF16)
    cqkvT = dint("cqkvT", [768, NT])
    bgT = dint("bgT", [8, NT])
    zs = dint("zs", [NT, 256])
    gatesT = dint("gatesT", [3 * D, NT], BF16)
    lfs = dint("lfs", [4, NS])
    zT = dint("zT", [256, NT])
    ofT = dint("ofT", [256, NT], BF16)
    odT = dint("odT", [256, NT], BF16)
    ogT = dint("ogT", [256, NT], BF16)

    sb = nc.alloc_sbuf_tensor
    gains_sb = sb("gains_sb", [128, DEPTH * 3 * 8 + 8], F32)
    ones_bf = sb("ones_bf", [128, 128], BF16)
    smallp_sb = sb("smallp_sb", [128, DEPTH * 4], F32)
    B_const = Buf("const")
    P.dma("sp", lambda E: E.dma_start(out=gains_sb[:, :], in_=gains), writes=[B_const])
    P.dma("sp", lambda E: E.dma_start(out=smallp_sb[:, :], in_=smallp), writes=[B_const])
    P.op("dve", lambda E: E.memset(ones_bf[:, :], 1.0), writes=[B_const])
    dgain_sb = sb("dgain_sb", [128, DEPTH * 2], F32)
    P.dma("sp", lambda E: E.dma_start(out=dgain_sb[:, :], in_=dgain), writes=[B_const])

    ps = [nc.alloc_psum_tensor(f"ps{i}", [128, 512], F32) for i in range(8)]
    Bps = [Buf(f"ps{i}") for i in range(8)]
    consts_sb = sb("consts_sb", [128, 452], F32)
    tri_bf = sb("tri_bf", [128, 128], BF16)
    bd_bf = sb("bd_bf", [128, 128], BF16)
    onesf = sb("onesf", [128, 128], F32)
    P.dma("sp", lambda E: E.dma_start(out=consts_sb[:, :], in_=consts), writes=[B_const])
    P.op("dve", lambda E: E.tensor_copy(tri_bf[:, :], consts_sb[:, 0:128]), reads=[B_const], writes=[B_const])
    P.op("dve", lambda E: E.tensor_copy(bd_bf[:, :], consts_sb[:, 192:320]), reads=[B_const], writes=[B_const])
    P.op("dve", lambda E: E.memset(onesf[:, :], 1.0), writes=[B_const])
    ident2 = consts_sb[:, 128:192]

    def pqv(pq, h):
        return pq[:, (h % 2) * 256:(h % 2) * 256 + 256]

    def psOv(psO):
        return psO[0:4, 0:780] if False else psO[0:4, 0:512]

    def mmx(out, lhsT, rhs, start, stop, reads, writes):
        P.op("pe", lambda E: E.matmul(out, lhsT=lhsT, rhs=rhs, start=start, stop=stop), reads=reads, writes=writes)

    def mixers(l):
        with ExitStack() as es:
            attn_prompt(es, l)
        P.barrier()
        with ExitStack() as es:
            deltanet(es, l)
        P.barrier()
        with ExitStack() as es:
            attn_sample(es, l)

    def attn_sample(es, l):
        def sb(name, shape, dt):
            _UID[0] += 1
            return es.enter_context(nc.sbuf_tensor(f"{name}_{_UID[0]}", shape, dt))
        lam_init = 0.8 - 0.6 * math.exp(-0.3 * l)
        FK, DK, FV, DV, LF, RW = 0, 256, 512, 772, 1032, 1036
        Gall = sb("Gall", [128, 16, RW], F32)
        BG = [Buf() for _ in range(16)]
        idx = sb("idx", [128, NBS * 16], I32)
        ptf = sb("ptf", [128, NBS * 16], F32)
        iot = sb("iot", [128, 1], F32)
        Bidx = Buf()
        qs = [sb(f"qs{k}", [128, 2, NS], BF16) for k in range(4)]
        qsm = [sb(f"qsm{c}", [128, 2, NS], BF16) for c in range(2)]
        qsf = [sb(f"qsf{k}", [128, 2, NS], F32) for k in range(2)]
        Bqs = Buf()
        vn = [sb(f"vn{k}", [4, NBS, 260], BF16) for k in range(2)]
        lfn = sb("lfn", [4, NBS * 4], F32)
        en = sb("en", [4, NBS, 4], F32)
        Bn = Buf()
        Rq = sb("Rq", [128, 4, 4, 64], F32)
        BRq = Buf()
        qb = [sb(f"qb{k}", [128, 4, 4, 64], F32) for k in range(2)]
        Bqb = [Buf(), Buf()]
        prod = [sb(f"prod{k}", [128, 4, 4, 64], F32) for k in range(2)]
        Bprod = [Buf(), Buf()]
        sc = [sb("scf", [128, 16, 4, 4], F32), sb("scd", [128, 16, 4, 4, 2], F32)]
        Bsc = [Buf(), Buf()]
        lfc = sb("lfc", [128, 16, 4], F32)
        tot = sb("tot", [128, 16, 4], F32)
        pg = sb("pg", [128, 16, 4], F32)
        t1 = sb("t1", [128, 16, 4], F32)
        Br = Buf()
        snw = sb("snw", [4, 3, 4, 4], F32)
        pnb = sb("pnb", [4, 3, 4, 4], F32)
        vnf = sb("vnf", [4, 2, 260], F32)
        Bsn = Buf()
        osb = sb("osb_s", [4, 3, 4, 65], F32)
        rec = sb("rec_s", [4, 3, 4], F32)
        of = sb("of_s", [4, 3, 4, 64], F32)
        o2s = sb("o2_s", [4, 4, 64], F32)
        ssq = sb("ssq_s", [4, 4], F32)
        Bo = Buf()
        grow = sb("grow", [4, 64], F32)
        oc = [sb(f"oc{k}", [128, 2, NS], BF16) for k in range(2)]
        Boc = [Buf(), Buf()]
        lsb2 = sb("lsb2", [4, 2], F32)
        P.dma("pool", lambda E: E.dma_start(out=idx[:, :], in_=ptab.partition_broadcast(128)), writes=[Bidx])
        P.op("pool", lambda E: E.iota(iot[:, :], [[0, 1]], base=0, channel_multiplier=1, allow_small_or_imprecise_dtypes=True), writes=[Bidx])
        P.op("pool", lambda E: E.tensor_copy(ptf[:, :], idx[:, :]), reads=[Bidx], writes=[Bidx])
        P.op("pool", lambda E: E.tensor_scalar(idx[:, :], ptf[:, :], 128.0, iot[:, 0:1], ALU.mult, ALU.add), reads=[Bidx], writes=[Bidx])
        for k, src in enumerate((fqa, fka, dqT, dkT)):
            for h in range(4):
                P.dma("sp", lambda E, k=k, src=src, h=h: E.dma_start(out=qs[k][64 * (h % 2):64 * (h % 2) + 64, h // 2, :], in_=src[h, 0:64, NTP:NT]), writes=[Bqs])
        for c in range(2):
            P.op("dve", lambda E, c=c: E.tensor_scalar(out=qsm[c][:, :, :], in0=qs[2][:, :, :], scalar1=consts_sb[:, 320 + c:321 + c], scalar2=None, op0=ALU.mult), reads=[Bqs, B_const], writes=[Bqs])
        qj = [[sb(f"qj{m}{j}", [128, 2, NS], BF16) for j in range(2)] for m in range(3)]
        for m in range(3):
            srcq = qs[0] if m == 0 else qsm[m - 1]
            for j in range(2):
                P.op("dve", lambda E, m=m, j=j, srcq=srcq: E.tensor_scalar(out=qj[m][j][:, :, :], in0=srcq[:, :, :], scalar1=consts_sb[:, 450 + j:451 + j], scalar2=None, op0=ALU.mult), reads=[Bqs, B_const], writes=[Bqs])
        P.op("dve", lambda E: E.tensor_copy(qsf[0][:, :, :], qs[0][:, :, :]), reads=[Bqs], writes=[Bqs])
        P.op("dve", lambda E: E.tensor_copy(qsf[1][:, :, :], qs[2][:, :, :]), reads=[Bqs], writes=[Bqs])
        for k, src in enumerate((fva, dva)):
            P.dma("sp", lambda E, k=k, src=src: E.dma_start(out=vn[k][:, :, :], in_=src[NTP:NT, :].rearrange("(b k) d -> k b d", k=4)), writes=[Bn])
        for h in range(4):
            P.dma("sp", lambda E, h=h: E.dma_start(out=lfn[:, :].rearrange("k (b h) -> k b h", h=4)[:, :, h], in_=lfs[h].rearrange("(b k) -> k b", k=4), allow_slow_non_contiguous=True), writes=[Bn])
        mmx(ps[7][0:4, 0:NBS * 4], consts_sb[0:4, 0:4], lfn[0:4, :], True, True, [Bn, B_const], [Bps[7]])
        P.op("act", lambda E: E.activation(out=en[:, :, :].rearrange("k b h -> k (b h)"), in_=ps[7][0:4, 0:NBS * 4], func=AF.Copy), reads=[Bps[7]], writes=[Bn])
        P.dma("sp", lambda E: E.dma_start(out=grow[:, :], in_=dnorm_row[l].partition_broadcast(4)), writes=[Bn])
        P.op("dve", lambda E: E.tensor_scalar(out=grow[:, :], in0=grow[:, :], scalar1=1.0 - lam_init, scalar2=None, op0=ALU.mult), reads=[Bn], writes=[Bn])
        dl = sb("dl_s", [4, 128], F32)
        P.dma("sp", lambda E: E.dma_start(out=dl[:, :], in_=dlam[l].partition_broadcast(4)), writes=[Bn])
        P.op("dve", lambda E: E.tensor_tensor(out=dl[:, 0:32], in0=dl[:, 0:32], in1=dl[:, 32:64], op=ALU.mult), reads=[Bn], writes=[Bn])
        P.op("dve", lambda E: E.tensor_tensor(out=dl[:, 64:96], in0=dl[:, 64:96], in1=dl[:, 96:128], op=ALU.mult), reads=[Bn], writes=[Bn])
        P.op("dve", lambda E: E.tensor_reduce(out=lsb2[:, 0:1], in_=dl[:, 0:32], axis=AX.X, op=ALU.add), reads=[Bn], writes=[Bn])
        P.op("dve", lambda E: E.tensor_reduce(out=lsb2[:, 1:2], in_=dl[:, 64:96], axis=AX.X, op=ALU.add), reads=[Bn], writes=[Bn])
        P.op("act", lambda E: E.activation(out=lsb2[:, 0:2], in_=lsb2[:, 0:2], func=AF.Exp), reads=[Bn], writes=[Bn])
        P.op("dve", lambda E: E.tensor_tensor(out=lsb2[:, 0:1], in0=lsb2[:, 1:2], in1=lsb2[:, 0:1], op=ALU.subtract), reads=[Bn], writes=[Bn])
        P.op("dve", lambda E: E.tensor_scalar(out=lsb2[:, 0:1], in0=lsb2[:, 0:1], scalar1=-lam_init, scalar2=None, op0=ALU.add), reads=[Bn], writes=[Bn])
        nlam4 = lsb2[:, 0:1]
        psQ = [(ps[0], Bps[0]), (ps[1], Bps[1])]
        psW, BW = ps[2], Bps[2]
        psOm = [(ps[3], Bps[3]), (ps[6], Bps[6]), (ps[7], Bps[7])]
        psS, BSn = ps[4], Bps[4]
        psT, BT = ps[5], Bps[5]
        KOFF = (FK, DK)
        VOFF = (FV, DV)
        KCUT = int(os.environ.get('KCUT', '99'))
        if KCUT <= 0:
            return
        for b in range(int(os.environ.get('KNB', NBS))):
            c4 = 4 * b
            for s_ in range(16):
                P.dma("pool", lambda E, s_=s_: E.indirect_dma_start(out=Gall[:, s_, :], out_offset=None, in_=pool_all[l], in_offset=bass.IndirectOffsetOnAxis(ap=idx[:, b * 16 + s_:b * 16 + s_ + 1], axis=0)),
                      reads=[Bidx], writes=[BG[s_]])
            if KCUT <= 1:
                continue
            for kind in range(2):
                pq, Bpq = psQ[kind]
                for pr in range(2):
                    for j in range(2):
                        for qi in range(4):
                            P.op("dve", lambda E, pr=pr, qi=qi, j=j: E.tensor_scalar(out=Rq[:, 2 * pr + j, qi, :], in0=consts_sb[:, 322 + 64 * j:386 + 64 * j], scalar1=qsf[kind][:, pr, c4 + qi:c4 + qi + 1], scalar2=None, op0=ALU.mult), reads=[Bqs, B_const], writes=[BRq])
                for pr in range(2):
                    for j in range(2):
                        h = 2 * pr + j
                        mmx(pqv(pq, h), onesf[:, :], Rq[:, h, :, :].rearrange("p q d -> p (q d)"), True, True, [BRq, B_const], [Bpq])
                        if h % 2 == 1:
                            P.op("act", lambda E, h=h: E.activation(out=qb[kind][:, h - 1:h + 1, :, :].rearrange("p h q d -> p (h q d)"), in_=pq[:, 0:512], func=AF.Copy), reads=[Bpq], writes=[Bqb[kind]])
            if KCUT <= 2:
                continue
            for s_ in range(16):
                for kind in range(2):
                    kv = Gall[:, s_, KOFF[kind]:KOFF[kind] + 256].rearrange("p (h d) -> p h d", d=64)
                    i2 = (s_ * 2 + kind) % 2
                    for qi in range(4):
                        P.op("dve", lambda E, qi=qi, kv=kv, i2=i2: E.tensor_tensor(out=prod[i2][:, :, qi, :], in0=qb[kind][:, :, qi, :], in1=kv, op=ALU.mult), reads=[Bqb[kind], BG[s_]], writes=[Bprod[i2]])
                    if kind == 0:
                        P.op("dve", lambda E, i2=i2: E.tensor_reduce(out=sc[0][:, s_, :, :], in_=prod[i2][:, :, :, :], axis=AX.X, op=ALU.add), reads=[Bprod[i2]], writes=[Bsc[0]])
                    else:
                        P.op("dve", lambda E, i2=i2: E.tensor_reduce(out=sc[1][:, s_, :, :, :], in_=prod[i2][:, :, :, :].rearrange("p h q (c d) -> p h q c d", c=2), axis=AX.X, op=ALU.add), reads=[Bprod[i2]], writes=[Bsc[1]])
            if KCUT <= 3:
                continue
            P.op("dve", lambda E: E.tensor_copy(lfc[:, :, :], Gall[:, :, LF:LF + 4]), reads=BG, writes=[Br])
            mmx(psW[:, 0:64], consts_sb[:, 0:128], lfc[:, :, :].rearrange("p s h -> p (s h)"), True, True, [Br, B_const], [BW])
            mmx(psW[:, 64:128], onesf[:, :], lfc[:, :, :].rearrange("p s h -> p (s h)"), True, True, [Br, B_const], [BW])
            P.op("act", lambda E: E.activation(out=tot[:, :, :].rearrange("p s h -> p (s h)"), in_=psW[:, 64:128], func=AF.Copy), reads=[BW], writes=[Br])
            for h in range(4):
                P.op("dve", lambda E, h=h: E.tensor_tensor_scan(out=pg[:, :, h], data0=onesf[:, 0:16], data1=tot[:, :, h], initial=0.0, op0=ALU.mult, op1=ALU.add), reads=[Br, B_const], writes=[Br])
            P.op("dve", lambda E: E.tensor_tensor(out=t1[:, :, :].rearrange("p s h -> p (s h)"), in0=tot[:, :, :].rearrange("p s h -> p (s h)"), in1=psW[:, 0:64], op=ALU.subtract), reads=[Br, BW], writes=[Br])
            P.op("dve", lambda E: E.tensor_tensor(out=t1[:, :, :], in0=pg[:, :, :], in1=t1[:, :, :], op=ALU.subtract), reads=[Br], writes=[Br])
            for h in range(4):
                P.op("dve", lambda E, h=h: E.tensor_scalar(out=t1[:, :, h], in0=t1[:, :, h], scalar1=pg[:, 15, h:h + 1], scalar2=-1.0, op0=ALU.subtract, op1=ALU.mult), reads=[Br], writes=[Br])
            for qi in range(4):
                P.op("dve", lambda E, qi=qi: E.tensor_tensor(out=sc[0][:, :, :, qi], in0=sc[0][:, :, :, qi], in1=t1[:, :, :], op=ALU.add), reads=[Bsc[0], Br], writes=[Bsc[0]])
            if KCUT <= 4:
                continue
            P.op("act", lambda E: E.activation(out=sc[0][:, :, :, :], in_=sc[0][:, :, :, :], func=AF.Exp), reads=[Bsc[0]], writes=[Bsc[0]])
            P.op("act", lambda E: E.activation(out=sc[1][:, :, :, :, :], in_=sc[1][:, :, :, :, :], func=AF.Exp), reads=[Bsc[1]], writes=[Bsc[1]])
            if KCUT <= 5:
                continue
            for m in range(3):
                for h in range(4):
                    pr, j = h // 2, h % 2
                    kk = qs[1] if m == 0 else qs[3]
                    mmx(psS[0:4, (m * 4 + h) * 4:(m * 4 + h) * 4 + 4], kk[:, pr, c4:c4 + 4], qj[m][j][:, pr, c4:c4 + 4], True, True, [Bqs], [BSn])
            P.op("act", lambda E: E.activation(out=snw[:, :, :, :].rearrange("k m h q -> k (m h q)"), in_=psS[0:4, 0:48], func=AF.Copy), reads=[BSn], writes=[Bsn])
            P.op("dve", lambda E: E.tensor_tensor(out=snw[:, 0, :, :], in0=snw[:, 0, :, :], in1=en[:, b, :].unsqueeze(2).to_broadcast([4, 4, 4]), op=ALU.subtract), reads=[Bsn, Bn], writes=[Bsn])
            P.op("act", lambda E: E.activation(out=snw[:, :, :, :], in_=snw[:, :, :, :], func=AF.Exp), reads=[Bsn], writes=[Bsn])
            P.op("dve", lambda E: E.tensor_tensor(out=pnb[:, :, :, :].rearrange("k m h q -> k (m h) q"), in0=snw[:, :, :, :].rearrange("k m h q -> k (m h) q"), in1=consts_sb[0:4, 0:4].unsqueeze(1).to_broadcast([4, 12, 4]), op=ALU.mult), reads=[Bsn, B_const], writes=[Bsn])
            if KCUT <= 6:
                continue
            for kind in range(2):
                P.op("dve", lambda E, kind=kind: E.tensor_copy(vnf[:, kind, :], vn[kind][:, b, :]), reads=[Bn], writes=[Bsn])
            for m in range(3):
                kind = 0 if m == 0 else 1
                psO, BO = psOm[m]
                first = True
                for h in range(4):
                    oc_ = psO[0:4, h * 65:h * 65 + 65]
                    for s_ in range(16):
                        lhs = sc[0][:, s_, h, :] if m == 0 else sc[1][:, s_, h, :, m - 1]
                        mmx(oc_, lhs, Gall[:, s_, VOFF[kind] + h * 65:VOFF[kind] + h * 65 + 65], first, False, [Bsc[kind], BG[s_]], [BO])
                        first = False
                    mmx(oc_, pnb[0:4, m, h, :], vnf[0:4, kind, h * 65:h * 65 + 65], False, True, [Bsn, Bn], [BO])
            if KCUT <= 7:
                continue
            for m in range(3):
                P.op("act", lambda E, m=m: E.activation(out=osb[:, m, :, :].rearrange("k h d -> k (h d)"), in_=psOm[m][0][0:4, 0:260], func=AF.Copy), reads=[psOm[m][1]], writes=[Bo])
            P.op("dve", lambda E: E.reciprocal(rec[:, :, :], osb[:, :, :, 64]), reads=[Bo], writes=[Bo])
            P.op("dve", lambda E: E.tensor_tensor(out=of[:, :, :, :], in0=osb[:, :, :, 0:64], in1=rec[:, :, :].unsqueeze(3).to_broadcast([4, 3, 4, 64]), op=ALU.mult), reads=[Bo], writes=[Bo])
            if KCUT <= 8:
                continue
            P.op("dve", lambda E: E.scalar_tensor_tensor(out=of[:, 1, :, :], in0=of[:, 2, :, :], scalar=nlam4, in1=of[:, 1, :, :], op0=ALU.mult, op1=ALU.add), reads=[Bo, Bn], writes=[Bo])
            P.op("dve", lambda E: E.tensor_tensor(out=o2s[:, :, :], in0=of[:, 1, :, :], in1=of[:, 1, :, :], op=ALU.mult), reads=[Bo], writes=[Bo])
            P.op("dve", lambda E: E.tensor_reduce(out=ssq[:, :], in_=o2s[:, :, :], axis=AX.X, op=ALU.add), reads=[Bo], writes=[Bo])
            P.op("act", lambda E: E.activation(out=ssq[:, :], in_=ssq[:, :], func=AF.Sqrt, scale=1.0 / 64, bias=1e-6), reads=[Bo], writes=[Bo])
            P.op("dve", lambda E: E.reciprocal(ssq[:, :], ssq[:, :]), reads=[Bo], writes=[Bo])
            P.op("dve", lambda E: E.tensor_tensor(out=of[:, 1, :, :], in0=of[:, 1, :, :], in1=ssq[:, :].unsqueeze(2).to_broadcast([4, 4, 64]), op=ALU.mult), reads=[Bo], writes=[Bo])
            P.op("dve", lambda E: E.tensor_tensor(out=of[:, 1, :, :], in0=of[:, 1, :, :], in1=grow[:, :].unsqueeze(1).to_broadcast([4, 4, 64]), op=ALU.mult), reads=[Bo, Bn], writes=[Bo])
            if KCUT <= 9:
                continue
            for kind in range(2):
                for ch in range(2):
                    P.op("pe", lambda E, kind=kind, ch=ch: E.transpose(psT[:, (kind * 2 + ch) * 4:(kind * 2 + ch) * 4 + 4], of[0:4, kind, 2 * ch:2 * ch + 2, :].rearrange("k h d -> k (h d)"), consts_sb[0:4, 128:132]), reads=[Bo, B_const], writes=[BT])
                P.op("act", lambda E, kind=kind: E.activation(out=oc[kind][:, :, c4:c4 + 4], in_=psT[:, kind * 8:kind * 8 + 8].rearrange("p (c q) -> p c q", q=4), func=AF.Copy), reads=[BT], writes=[Boc[kind]])
        for kind, dst in enumerate((ofT, odT)):
            P.dma("sp", lambda E, kind=kind, dst=dst: E.dma_start(out=dst[:, NTP:NT].rearrange("(a p) t -> p a t", p=128), in_=oc[kind][:, :, :]), reads=[Boc[kind]])

    def deltanet(es, l):
        def sb(name, shape, dt):
            _UID[0] += 1
            return es.enter_context(nc.sbuf_tensor(f"{name}_{_UID[0]}", shape, dt))
        S = sb("dS", [128, 2, 64], F32)
        Sa = sb("dSa", [128, 2, 64], F32)
        BS, BSa = Buf(), Buf()
        qk = [sb(f"dqk{i}", [128, 6, 128], F32) for i in range(2)]
        Bqk = [Buf(), Buf()]
        sq = sb("dsq", [128, 4, 128], F32)
        Bsq = Buf()
        bg = [sb(f"dbg{i}", [8, 128], F32) for i in range(2)]
        Bbg = [Buf(), Buf()]
        abc = sb("dabc", [128, 2, 128], F32)
        bbc = sb("dbbc", [128, 2, 128], F32)
        Bab = Buf()
        sel = sb("dsel", [8, 512], F32)
        uu = sb("duu", [128, 2], F32)
        Buu = Buf()
        du = sb("ddu", [128, 2, 64], F32)
        Bdu = Buf()
        oo = sb("doo", [128, 2, 128], F32)
        Boo = Buf()
        o2 = sb("do2", [128, 2, 128], F32)
        Bo2 = Buf()
        zz = sb("dzz", [128, 2, 128], F32)
        Bzz = Buf()
        obf = sb("dobf", [128, 2, 128], BF16)
        Bobf = Buf()
        bdf = consts_sb[:, 192:320]
        P.dma("sp", lambda E: E.dma_start(out=sel[:, :], in_=sel_in), writes=[B_const])
        P.op("dve", lambda E: E.memset(S[:, :, :], 0.0), writes=[BS])
        psK, BK = ps[0], Bps[0]
        psU, BU = ps[1], Bps[1]
        psO, BO = ps[2], Bps[2]
        psN, BN = ps[3], Bps[3]
        gcol = dgain_sb[:, 2 * l + 1:2 * l + 2]

        def block(bi, c0, sample):
            i = bi % 2
            q, Bq = qk[i], Bqk[i]
            P.dma("sp", lambda E: E.dma_start(out=q[:, :, :], in_=cqkvT[:, c0:c0 + 128].rearrange("(c p) t -> p c t", p=128)), writes=[Bq])
            P.dma("sp", lambda E: E.dma_start(out=bg[i][:, :], in_=bgT[:, c0:c0 + 128]), writes=[Bbg[i]])
            P.dma("sp", lambda E: E.dma_start(out=zz[:, :, :], in_=zT[:, c0:c0 + 128].rearrange("(c p) t -> p c t", p=128)), writes=[Bzz])
            P.op("dve", lambda E: E.tensor_tensor(out=sq[:, :, :], in0=q[:, 0:4, :], in1=q[:, 0:4, :], op=ALU.mult), reads=[Bq], writes=[Bsq])
            mmx(psN[:, 0:512], bdf, sq[:, :, :].rearrange("p c t -> p (c t)"), True, True, [Bsq, B_const], [BN])
            P.op("act", lambda E: E.activation(out=sq[:, :, :].rearrange("p c t -> p (c t)"), in_=psN[:, 0:512], func=AF.Sqrt, bias=1e-6), reads=[BN], writes=[Bsq])
            P.op("dve", lambda E: E.reciprocal(sq[:, :, :], sq[:, :, :]), reads=[Bsq], writes=[Bsq])
            P.op("dve", lambda E: E.scalar_tensor_tensor(out=q[:, 0:2, :], in0=q[:, 0:2, :], scalar=0.125, in1=sq[:, 0:2, :], op0=ALU.mult, op1=ALU.mult), reads=[Bq, Bsq], writes=[Bq])
            P.op("dve", lambda E: E.tensor_tensor(out=q[:, 2:4, :], in0=q[:, 2:4, :], in1=sq[:, 2:4, :], op=ALU.mult), reads=[Bq, Bsq], writes=[Bq])
            for which in range(2):
                for pr in range(2):
                    mmx(psN[:, 0:128], sel[0:8, (which * 2 + pr) * 128:(which * 2 + pr + 1) * 128], bg[i][0:8, :], True, True, [Bbg[i], B_const], [BN])
                    dst = bbc if which == 0 else abc
                    P.op("act", lambda E, dst=dst, pr=pr, which=which: E.activation(out=dst[:, pr, :], in_=psN[:, 0:128], func=AF.Copy if which == 0 else AF.Exp), reads=[BN], writes=[Bab])
            for t in range(128):
                if sample and t % 4 == 0:
                    bidx = (c0 - NTP) // 4 + t // 4
                    P.dma("sp", lambda E: E.dma_start(out=S[:, :, :].rearrange("p a v -> p (a v)"), in_=sdel_in[l, bidx]), writes=[BS])
                for pr in range(2):
                    P.op("dve", lambda E, pr=pr: E.tensor_scalar(out=Sa[:, pr, :], in0=S[:, pr, :], scalar1=abc[:, pr, t:t + 1], scalar2=None, op0=ALU.mult), reads=[BS, Bab], writes=[BSa])
                for pr in range(2):
                    for j in range(2):
                        mmx(psK[64 * j:64 * j + 64, pr:pr + 1], Sa[64 * j:64 * j + 64, pr, :], q[64 * j:64 * j + 64, 2 + pr, t:t + 1], True, True, [BSa, Bq], [BK])
                P.op("dve", lambda E: E.tensor_tensor(out=uu[:, 0:2], in0=q[:, 4:6, t], in1=psK[:, 0:2], op=ALU.subtract), reads=[Bq, BK], writes=[Buu])
                P.op("dve", lambda E: E.tensor_tensor(out=uu[:, 0:2], in0=uu[:, 0:2], in1=bbc[:, :, t], op=ALU.mult), reads=[Buu, Bab], writes=[Buu])
                for pr in range(2):
                    P.op("dve", lambda E, pr=pr: E.tensor_scalar(out=du[:, pr, :], in0=ident2, scalar1=uu[:, pr:pr + 1], scalar2=None, op0=ALU.mult), reads=[Buu, B_const], writes=[Bdu])
                for pr in range(2):
                    for j in range(2):
                        mmx(psU[64 * j:64 * j + 64, pr * 64:(pr + 1) * 64], onesf[64 * j:64 * j + 64, 0:64], du[64 * j:64 * j + 64, pr, :], True, True, [Bdu, B_const], [BU])
                for pr in range(2):
                    P.op("dve", lambda E, pr=pr: E.scalar_tensor_tensor(out=S[:, pr, :], in0=psU[:, pr * 64:(pr + 1) * 64], scalar=q[:, 2 + pr, t:t + 1], in1=Sa[:, pr, :], op0=ALU.mult, op1=ALU.add),
                         reads=[BU, Bq, BSa], writes=[BS])
                for pr in range(2):
                    for j in range(2):
                        mmx(psO[64 * j:64 * j + 64, pr * 128 + t:pr * 128 + t + 1], S[64 * j:64 * j + 64, pr, :], q[64 * j:64 * j + 64, pr, t:t + 1], True, True, [BS, Bq], [BO])
                if sample and t % 4 == 3:
                    bidx = (c0 - NTP) // 4 + t // 4
                    P.dma("sp", lambda E: E.dma_start(out=dss_o[l, bidx], in_=S[:, :, :].rearrange("p a v -> p (a v)")), reads=[BS])
            P.op("act", lambda E: E.activation(out=oo[:, :, :].rearrange("p a t -> p (a t)"), in_=psO[:, 0:256], func=AF.Copy), reads=[BO], writes=[Boo])
            P.op("dve", lambda E: E.tensor_tensor(out=o2[:, :, :], in0=oo[:, :, :], in1=oo[:, :, :], op=ALU.mult), reads=[Boo], writes=[Bo2])
            mmx(psN[:, 0:256], bdf, o2[:, :, :].rearrange("p a t -> p (a t)"), True, True, [Bo2, B_const], [BN])
            P.op("act", lambda E: E.activation(out=o2[:, :, :].rearrange("p a t -> p (a t)"), in_=psN[:, 0:256], func=AF.Sqrt, scale=1.0 / 64, bias=1e-6), reads=[BN], writes=[Bo2])
            P.op("dve", lambda E: E.reciprocal(o2[:, :, :], o2[:, :, :]), reads=[Bo2], writes=[Bo2])
            P.op("dve", lambda E: E.scalar_tensor_tensor(out=oo[:, :, :], in0=oo[:, :, :], scalar=gcol, in1=o2[:, :, :], op0=ALU.mult, op1=ALU.mult), reads=[Boo, Bo2, B_const], writes=[Boo])
            P.op("dve", lambda E: E.tensor_tensor(out=obf[:, :, :], in0=oo[:, :, :], in1=zz[:, :, :], op=ALU.mult), reads=[Boo, Bzz], writes=[Bobf])
            P.dma("sp", lambda E: E.dma_start(out=ogT[:, c0:c0 + 128].rearrange("(a p) t -> p a t", p=128), in_=obf[:, :, :]), reads=[Bobf])

        for bi in range(NTP // 128):
            block(bi, bi * 128, False)
        P.dma("sp", lambda E: E.dma_start(out=dsp_o[l], in_=S[:, :, :].rearrange("p a v -> p (a v)")), reads=[BS])
        for bi in range(NS // 128):
            block(bi, NTP + bi * 128, True)

    def attn_prompt(es, l):
        def sb(name, shape, dt):
            _UID[0] += 1
            return es.enter_context(nc.sbuf_tensor(f"{name}_{_UID[0]}", shape, dt))
        lam_init = 0.8 - 0.6 * math.exp(-0.3 * l)
        va = sb("va", [128, 64, 260], BF16)
        Bva = Buf()
        kt = [sb(f"kt{i}", [70, NTP], BF16) for i in range(2)]
        qt = [sb(f"qt{i}", [70, NTP], BF16) for i in range(2)]
        Bkq = [Buf(), Buf()]
        pt = [sb(f"pt{i}", [128, 512], BF16) for i in range(4)]
        Bpt = [Buf() for _ in range(4)]
        osb = [sb(f"osb{i}", [65, 512], F32) for i in range(2)]
        Bosb = [Buf(), Buf()]
        rl = [sb(f"rl{i}", [65, 512], F32) for i in range(2)]
        Brl = [Buf(), Buf()]
        on = sb("on", [64, 512], F32)
        Bon = Buf()
        on2 = sb("on2", [64, 512], F32)
        Bon2 = Buf()
        obf = sb("obf", [64, 512], BF16)
        Bobf = Buf()
        lsb = sb("lsb", [128, 8], F32)
        dl = sb("dl", [1, 128], F32)
        Bl = Buf()
        P.dma("sp", lambda E: E.dma_start(out=dl[0:1, :], in_=dlam[l]), writes=[Bl])
        P.op("dve", lambda E: E.tensor_tensor(out=dl[0:1, 0:32], in0=dl[0:1, 0:32], in1=dl[0:1, 32:64], op=ALU.mult), reads=[Bl], writes=[Bl])
        P.op("dve", lambda E: E.tensor_tensor(out=dl[0:1, 64:96], in0=dl[0:1, 64:96], in1=dl[0:1, 96:128], op=ALU.mult), reads=[Bl], writes=[Bl])
        P.op("dve", lambda E: E.tensor_reduce(out=lsb[0:1, 0:1], in_=dl[0:1, 0:32], axis=AX.X, op=ALU.add), reads=[Bl], writes=[Bl])
        P.op("dve", lambda E: E.tensor_reduce(out=lsb[0:1, 1:2], in_=dl[0:1, 64:96], axis=AX.X, op=ALU.add), reads=[Bl], writes=[Bl])
        P.op("act", lambda E: E.activation(out=lsb[0:1, 2:4], in_=lsb[0:1, 0:2], func=AF.Exp), reads=[Bl], writes=[Bl])
        P.op("dve", lambda E: E.tensor_tensor(out=lsb[0:1, 4:5], in0=lsb[0:1, 3:4], in1=lsb[0:1, 2:3], op=ALU.subtract), reads=[Bl], writes=[Bl])
        P.op("dve", lambda E: E.tensor_scalar(out=lsb[0:1, 4:5], in0=lsb[0:1, 4:5], scalar1=-lam_init, scalar2=None, op0=ALU.add), reads=[Bl], writes=[Bl])
        mmx(ps[6][0:64, 0:1], onesf[0:1, 0:64], lsb[0:1, 4:5], True, True, [Bl, B_const], [Bps[6]])
        P.op("act", lambda E: E.activation(out=lsb[0:64, 5:6], in_=ps[6][0:64, 0:1], func=AF.Copy), reads=[Bps[6]], writes=[Bl])
        nlam = lsb[0:64, 5:6]
        P.op("dve", lambda E: E.tensor_scalar(out=lsb[0:64, 6:7], in0=dgain_sb[0:64, 2 * l:2 * l + 1], scalar1=1.0 - lam_init, scalar2=None, op0=ALU.mult), reads=[B_const], writes=[Bl])
        for q4 in range(4):
            P.dma("sp", lambda E, q4=q4: E.dma_start(out=va[:, 16 * q4:16 * q4 + 16, :], in_=fva[2048 * q4:2048 * (q4 + 1), :].rearrange("(b p) d -> p b d", p=128)), writes=[Bva])
        cnt = {"p": 0, "o": 0}

        def finalize(po, Bpo, which):
            i = cnt["o"] % 2
            cnt["o"] += 1
            P.op("act", lambda E: E.activation(out=osb[i][0:65, :], in_=po[0:65, :], func=AF.Copy), reads=[Bpo], writes=[Bosb[i]])
            P.op("dve", lambda E: E.reciprocal(rl[i][64:65, :], osb[i][64:65, :]), reads=[Bosb[i]], writes=[Brl[i]])
            pb, Bpb = ps[6 + i], Bps[6 + i]
            mmx(pb[0:64, :], onesf[64:65, 0:64], rl[i][64:65, :], True, True, [Brl[i], B_const], [Bpb])
            dst, Bd = (on, Bon) if which == 0 else (on2, Bon2)
            P.op("dve", lambda E: E.tensor_tensor(out=dst[0:64, :], in0=osb[i][0:64, :], in1=pb[0:64, :], op=ALU.mult), reads=[Bosb[i], Bpb], writes=[Bd])

        for kind in range(2):
            if kind == 1:
                for q4 in range(4):
                    P.dma("sp", lambda E, q4=q4: E.dma_start(out=va[:, 16 * q4:16 * q4 + 16, :], in_=dva[2048 * q4:2048 * (q4 + 1), :].rearrange("(b p) d -> p b d", p=128)), writes=[Bva])
            nrow = 70 if kind == 0 else 64
            for h in range(4):
                b = h % 2
                ksrc, qsrc = (fka, fqa) if kind == 0 else (dkT, dqT)
                P.dma("sp", lambda E: E.dma_start(out=kt[b][0:nrow, :], in_=ksrc[h, :, 0:NTP]), writes=[Bkq[b]])
                P.dma("sp", lambda E: E.dma_start(out=qt[b][0:nrow, :], in_=qsrc[h, :, 0:NTP]), writes=[Bkq[b]])
                for qi in range(16):
                    q0 = qi * 512
                    nkb = 4 * qi + 4
                    maps = [(0, nrow)] if kind == 0 else [(0, 32), (32, 64)]
                    pos = [(ps[4], Bps[4]), (ps[5], Bps[5])]
                    if kind == 0:
                        pos = [pos[qi % 2]]
                    for kb in range(nkb):
                        r = kb - 4 * qi
                        c_lo = 128 * r if r > 0 else 0
                        ncol = 512 - c_lo
                        for mi, (r0, r1) in enumerate(maps):
                            j = cnt["p"] % 4
                            cnt["p"] += 1
                            pS, BpS = ps[j], Bps[j]
                            mmx(pS[:, 0:ncol], kt[b][r0:r1, kb * 128:(kb + 1) * 128], qt[b][r0:r1, q0 + c_lo:q0 + 512], True, True, [Bkq[b]], [BpS])
                            P.op("act", lambda E: E.activation(out=pt[j][:, 0:ncol], in_=pS[:, 0:ncol], func=AF.Exp), reads=[BpS], writes=[Bpt[j]])
                            if r >= 0:
                                P.op("pool", lambda E: E.tensor_tensor(out=pt[j][:, 0:128], in0=pt[j][:, 0:128], in1=tri_bf[:, :], op=ALU.mult), reads=[Bpt[j], B_const], writes=[Bpt[j]])
                            po, Bpo = pos[mi]
                            mmx(po[0:65, c_lo:512], va[:, kb, h * 65:(h + 1) * 65], pt[j][:, 0:ncol], kb == 0, kb == nkb - 1, [Bva, Bpt[j]], [Bpo])
                    if kind == 0:
                        finalize(pos[0][0], pos[0][1], 0)
                        P.op("act", lambda E: E.activation(out=obf[:, :], in_=on[0:64, :], func=AF.Copy), reads=[Bon], writes=[Bobf])
                        P.dma("sp", lambda E: E.dma_start(out=ofT[h * 64:(h + 1) * 64, q0:q0 + 512], in_=obf[:, :]), reads=[Bobf])
                    else:
                        finalize(pos[0][0], pos[0][1], 0)
                        finalize(pos[1][0], pos[1][1], 1)
                        P.op("dve", lambda E: E.scalar_tensor_tensor(out=on[0:64, :], in0=on2[0:64, :], scalar=nlam, in1=on[0:64, :], op0=ALU.mult, op1=ALU.add), reads=[Bon, Bon2, Bl], writes=[Bon])
                        P.op("dve", lambda E: E.tensor_tensor(out=obf[:, :], in0=on[0:64, :], in1=on[0:64, :], op=ALU.mult), reads=[Bon], writes=[Bobf])
                        pb, Bpb = ps[6], Bps[6]
                        mmx(pb[0:64, :], ones_bf[0:64, 0:64], obf[:, :], True, True, [Bobf, B_const], [Bpb])
                        P.op("act", lambda E: E.activation(out=on2[0:64, :], in_=pb[0:64, :], func=AF.Sqrt, scale=1.0 / 64, bias=1e-6), reads=[Bpb], writes=[Bon2])
                        P.op("dve", lambda E: E.reciprocal(on2[0:64, :], on2[0:64, :]), reads=[Bon2], writes=[Bon2])
                        P.op("dve", lambda E: E.scalar_tensor_tensor(out=obf[:, :], in0=on[0:64, :], scalar=lsb[0:64, 6:7], in1=on2[0:64, :], op0=ALU.mult, op1=ALU.mult), reads=[Bon, Bon2, Bl], writes=[Bobf])
                        P.dma("sp", lambda E: E.dma_start(out=odT[h * 64:(h + 1) * 64, q0:q0 + 512], in_=obf[:, :]), reads=[Bobf])

    def token_phase(l, do_c, do_a, final=False):
        with ExitStack() as es:
            _token_phase(es, l, do_c, do_a, final)

    def _token_phase(es, l, do_c, do_a, final):
        def sb(name, shape, dt):
            _UID[0] += 1
            return es.enter_context(nc.sbuf_tensor(f"{name}_{_UID[0]}", shape, dt))

        xs = sb(f"xs{l}", [128, 8, 1024], F32)
        hs = sb(f"hs{l}", [128, 8, 1024], BF16)
        at = sb(f"at{l}", [128, NFC, 1024], BF16)
        wst = [sb(f"wst{l}{i}", [128, 2816], F32) for i in range(2)]
        wbf = [sb(f"wbf{l}{i}", [128, 2816], BF16) for i in range(2)]
        Bwst = [Buf() for _ in range(2)]
        Bwbf = [Buf() for _ in range(2)]
        tmp = [sb(f"tmp{l}{i}", [128, 512], F32) for i in range(4)]
        Btmp = [Buf() for _ in range(4)]
        stg = [sb(f"stg{l}{i}", [128, 512], BF16) for i in range(3)]
        Bstg = [Buf() for _ in range(3)]
        cs = sb(f"cs{l}", [128, 2, 1024], F32)
        Bcs = Buf()
        wtm_sb = sb(f"wtm{l}", [128, 8, 768], BF16)
        Bwtm = Buf()
        xp = sb(f"xp{l}", [128, 1024 + 8], F32)
        Bxp = Buf()
        carry = sb(f"carry{l}", [128, 6, 3], F32)
        Bcarry = Buf()
        cw = sb(f"cw{l}", [128, 24], F32)
        ccar = sb(f"ccar{l}", [4, 2], F32)
        csp = sb(f"csp{l}", [4, 8, 512], BF16)
        Bccar = Buf()
        Bcsp = Buf()
        sm = sb(f"sm{l}", [128, 4, 512], F32)
        Bsm = Buf()
        ones3 = sb(f"ones3{l}", [4, 3, 512], BF16)
        tmv = sb(f"tmv{l}", [128, 4, 65], BF16)
        Btmv = Buf()
        rr = {"w": 0, "tmp": 0, "stg": 0, "ps": 0}
        gt = sb(f"gt{l}", [128, 3, 1024], BF16)
        Bgt = Buf()

        P.dma("sp", lambda E: E.dma_start(out=cw[:, :], in_=convw[min(l, DEPTH - 1)]), writes=[B_const])
        P.op("dve", lambda E: E.memset(carry[:, :, :], 0.0), writes=[Bcarry])
        P.op("dve", lambda E: E.memset(ccar[:, :], 0.0), writes=[Bccar])
        P.op("dve", lambda E: E.memset(ones3[:, :, :], 1.0), writes=[B_const])
        P.op("dve", lambda E: E.memset(tmv[:, :, :], 1.0), writes=[Btmv])
        sp2 = sb(f"sp2{l}", [128, 2], F32)
        P.op("dve", lambda E: E.tensor_scalar(out=sp2[0:4, 0:1], in0=smallp_sb[0:4, 4 * min(l, DEPTH - 1):4 * min(l, DEPTH - 1) + 1], scalar1=-1.0, scalar2=None, op0=ALU.mult), reads=[B_const], writes=[B_const])
        P.op("act", lambda E: E.activation(out=sp2[64:68, 1:2], in_=smallp_sb[64:68, 4 * min(l, DEPTH - 1) + 2:4 * min(l, DEPTH - 1) + 3], func=AF.Exp), reads=[B_const], writes=[B_const])
        P.op("dve", lambda E: E.tensor_scalar(out=sp2[64:68, 1:2], in0=sp2[64:68, 1:2], scalar1=-1.0, scalar2=None, op0=ALU.mult), reads=[B_const], writes=[B_const])

        def load_w(src_ap, n):
            i = rr["w"]
            rr["w"] ^= 1
            P.dma("sp", lambda E: E.dma_start(out=wst[i][:, 0:n], in_=src_ap), writes=[Bwst[i]])
            P.op("pool", lambda E: E.tensor_copy(wbf[i][:, 0:n], wst[i][:, 0:n]), reads=[Bwst[i]], writes=[Bwbf[i]])
            return wbf[i], Bwbf[i]

        def next_ps():
            i = rr["ps"]
            rr["ps"] = (i + 1) % 8
            return ps[i], Bps[i]

        def next_tmp():
            i = rr["tmp"]
            rr["tmp"] = (i + 1) % 4
            return tmp[i], Btmp[i]

        def next_stg():
            i = rr["stg"]
            rr["stg"] = (i + 1) % 3
            return stg[i], Bstg[i]

        def mm(out, lhsT, rhs, start, stop, reads, writes):
            P.op("pe", lambda E: E.matmul(out, lhsT=lhsT, rhs=rhs, start=start, stop=stop), reads=reads, writes=writes)

        if stage == 0.2:
            return
        for (t0, tn, is_s) in (TILES[:ntiles] if ntiles > 0 else TILES[ntiles:]):
            subs = subtiles(tn)
            Bx = [[Buf() for _ in subs] for _ in range(8)]
            Bh = [Buf() for _ in subs]
            Bat = [[Buf() for _ in subs] for _ in range(NFC)]
            xsrc = xT0 if (l == 0 and not do_c) else xT
            P.dma("sp", lambda E: E.dma_start(out=xs[:, :, 0:tn], in_=xsrc[:, t0:t0 + tn].rearrange("(k p) t -> p k t", p=128)),
                  writes=[b for r in Bx for b in r])

            def rmsnorm(gcol, mode=0):
                for si, (s0, sn) in enumerate(subs):
                    pn, Bpn = next_ps()
                    for kc in range(8):
                        sq, Bsq = next_stg()
                        P.op("dve", lambda E, kc=kc, sq=sq: E.tensor_tensor(out=sq[:, 0:sn], in0=xs[:, kc, s0:s0 + sn], in1=xs[:, kc, s0:s0 + sn], op=ALU.mult),
                             reads=[Bx[kc][si]], writes=[Bsq])
                        mm(pn[:, 0:sn], ones_bf[:, :], sq[:, 0:sn], kc == 0, kc == 7, [Bsq, B_const], [Bpn])
                    if mode == 1:
                        continue
                    rs, Brs = next_tmp()
                    P.op("act", lambda E: E.activation(out=rs[:, 0:sn], in_=pn[:, 0:sn], func=AF.Sqrt, scale=1.0 / D, bias=1e-6), reads=[Bpn], writes=[Brs])
                    if mode == 2:
                        continue
                    P.op("dve", lambda E: E.reciprocal(rs[:, 0:sn], rs[:, 0:sn]), reads=[Brs], writes=[Brs])
                    for kc in range(8):
                        P.op("dve", lambda E, kc=kc: E.scalar_tensor_tensor(out=hs[:, kc, s0:s0 + sn], in0=xs[:, kc, s0:s0 + sn], scalar=gains_sb[:, gcol + kc:gcol + kc + 1],
                                                                              in1=rs[:, 0:sn], op0=ALU.mult, op1=ALU.mult),
                             reads=[Bx[kc][si], Brs, B_const], writes=[Bh[si]])

            def ffn(which, lw):
                for fc in range(NFC):
                    w, Bw = load_w(wi_r[which][lw, fc], 2048)
                    for si, (s0, sn) in enumerate(subs):
                        pg, Bpg = next_ps()
                        pu, Bpu = next_ps()
                        for kc in range(8):
                            mm(pg[:, 0:sn], w[:, kc * 256:kc * 256 + 128], hs[:, kc, s0:s0 + sn], kc == 0, kc == 7, [Bw, Bh[si]], [Bpg])
                        for kc in range(8):
                            mm(pu[:, 0:sn], w[:, kc * 256 + 128:kc * 256 + 256], hs[:, kc, s0:s0 + sn], kc == 0, kc == 7, [Bw, Bh[si]], [Bpu])
                        sg, Bsg = next_tmp()
                        P.op("act", lambda E: E.activation(out=sg[:, 0:sn], in_=pg[:, 0:sn], func=AF.Silu), reads=[Bpg], writes=[Bsg])
                        P.op("dve", lambda E: E.tensor_tensor(out=at[:, fc, s0:s0 + sn], in0=sg[:, 0:sn], in1=pu[:, 0:sn], op=ALU.mult),
                             reads=[Bsg, Bpu], writes=[Bat[fc][si]])
                for dc in range(8):
                    w, Bw = load_w(wo_r[which][lw, dc], NFC * 128)
                    for si, (s0, sn) in enumerate(subs):
                        po, Bpo = next_ps()
                        for fc in range(NFC):
                            mm(po[:, 0:sn], w[:, fc * 128:(fc + 1) * 128], at[:, fc, s0:s0 + sn], fc == 0, fc == NFC - 1, [Bw, Bat[fc][si]], [Bpo])
                        P.op("dve", lambda E: E.scalar_tensor_tensor(out=xs[:, dc, s0:s0 + sn], in0=po[:, 0:sn], scalar=0.5, in1=xs[:, dc, s0:s0 + sn], op0=ALU.mult, op1=ALU.add),
                             reads=[Bpo, Bx[dc][si]], writes=[Bx[dc][si]])

            def proj_fm(cc, si, s0, sn, w, Bw):
                pp, Bpp = next_ps()
                for kc in range(8):
                    mm(pp[:, 0:sn], w[:, kc * 128:(kc + 1) * 128], hs[:, kc, s0:s0 + sn], kc == 0, kc == 7, [Bw, Bh[si]], [Bpp])
                return pp, Bpp

            def phase_a():
                g0 = (l * 3) * 8
                if stage == 0.3:
                    return
                if stage == 0.5:
                    rmsnorm(g0, mode=1)
                    return
                if stage == 0.6:
                    rmsnorm(g0, mode=2)
                    return
                rmsnorm(g0)
                if stage == 1:
                    return
                ffn(0, l)
                if stage == 2:
                    return
                rmsnorm(g0 + 8)
                P.dma("sp", lambda E: E.dma_start(out=cs[:, :, 0:tn], in_=cossin[:, :, t0:t0 + tn].rearrange("c p t -> p c t")), writes=[Bcs])
                for cc in range(NFM):
                    if fmlist is not None and cc not in fmlist:
                        continue
                    if cc in (6, 7, 10, 11):
                        continue
                    w, Bw = load_w(wfm[l, cc], 1024)
                    if cc in (4, 5, 8, 9):
                        w2, Bw2 = load_w(wfm[l, cc + 2], 1024)
                    for si, (s0, sn) in enumerate(subs):
                        c0 = t0 + s0
                        pp, Bpp = proj_fm(cc, si, s0, sn, w, Bw)
                        if cc in (0, 1):
                            st, Bst = next_stg()
                            P.op("act", lambda E: E.activation(out=st[:, 0:sn], in_=pp[:, 0:sn], func=AF.Copy, scale=0.125), reads=[Bpp], writes=[Bst])
                            for hh in range(2):
                                P.dma("sp", lambda E, hh=hh: E.dma_start(out=fqa[2 * cc + hh, 0:64, c0:c0 + sn], in_=st[64 * hh:64 * hh + 64, 0:sn]), reads=[Bst])
                        elif cc in (2, 3):
                            st, Bst = next_stg()
                            tf, Btf = next_tmp()
                            P.op("act", lambda E: E.activation(out=st[:, 0:sn], in_=pp[:, 0:sn], func=AF.Copy), reads=[Bpp], writes=[Bst])
                            P.op("act", lambda E: E.activation(out=tf[:, 0:sn], in_=pp[:, 0:sn], func=AF.Copy), reads=[Bpp], writes=[Btf])
                            for hh in range(2):
                                P.dma("sp", lambda E, hh=hh: E.dma_start(out=fka[2 * (cc - 2) + hh, 0:64, c0:c0 + sn], in_=st[64 * hh:64 * hh + 64, 0:sn]), reads=[Bst])
                            P.dma("sp", lambda E: E.dma_start(out=fkT_o[l, (cc - 2) * 128:(cc - 1) * 128, c0:c0 + sn], in_=tf[:, 0:sn]), reads=[Btf])
                        elif cc in (4, 5, 8, 9):
                            pp2, Bpp2 = proj_fm(cc + 2, si, s0, sn, w2, Bw2)
                            ta, Bta = next_tmp()
                            tb, Btb = next_tmp()
                            P.op("dve", lambda E: E.tensor_tensor(out=ta[:, 0:sn], in0=pp[:, 0:sn], in1=cs[:, 0, s0:s0 + sn], op=ALU.mult), reads=[Bpp, Bcs], writes=[Bta])
                            P.op("dve", lambda E: E.tensor_tensor(out=tb[:, 0:sn], in0=pp2[:, 0:sn], in1=cs[:, 1, s0:s0 + sn], op=ALU.mult), reads=[Bpp2, Bcs], writes=[Btb])
                            P.op("pool", lambda E: E.tensor_tensor(out=ta[:, 0:sn], in0=ta[:, 0:sn], in1=tb[:, 0:sn], op=ALU.add), reads=[Bta, Btb], writes=[Bta])
                            st, Bst = next_stg()
                            isq = cc in (4, 5)
                            P.op("act", lambda E: E.activation(out=st[:, 0:sn], in_=ta[:, 0:sn], func=AF.Copy, scale=(32 ** -0.5) if isq else 1.0), reads=[Bta], writes=[Bst])
                            dst = dqT if isq else dkT
                            hb = 2 * (cc - (4 if isq else 8))
                            for hh in range(2):
                                P.dma("sp", lambda E, hh=hh: E.dma_start(out=dst[hb + hh, :, c0:c0 + sn], in_=st[64 * hh:64 * hh + 64, 0:sn]), reads=[Bst])
                            if not isq:
                                P.dma("sp", lambda E: E.dma_start(out=dkT_o[l, (cc - 8) * 128:(cc - 7) * 128, c0:c0 + sn], in_=ta[:, 0:sn]), reads=[Bta])
                        elif 12 <= cc <= 17:
                            j = cc - 12
                            if not is_s:
                                if si == 0:
                                    P.op("dve", lambda E: E.tensor_copy(xp[:, 0:3], carry[:, j, :]), reads=[Bcarry], writes=[Bxp])
                                P.op("act", lambda E: E.activation(out=xp[:, 3 + s0:3 + s0 + sn], in_=pp[:, 0:sn], func=AF.Copy), reads=[Bpp], writes=[Bxp])
                                if si == len(subs) - 1:
                                    P.op("dve", lambda E: E.tensor_copy(carry[:, j, :], xp[:, tn:tn + 3]), reads=[Bxp], writes=[Bcarry])
                                    if t0 + tn == NTP:
                                        P.dma("sp", lambda E: E.dma_start(out=cvp_o[l, j * 128:(j + 1) * 128, :], in_=carry[:, j, :]), reads=[Bcarry])
                                ac, Bac = next_tmp()
                                P.op("dve", lambda E: E.tensor_scalar(out=ac[:, 0:sn], in0=xp[:, s0:s0 + sn], scalar1=cw[:, 4 * j:4 * j + 1], scalar2=None, op0=ALU.mult), reads=[Bxp, B_const], writes=[Bac])
                                for i in range(1, 4):
                                    P.op("dve", lambda E, i=i: E.scalar_tensor_tensor(out=ac[:, 0:sn], in0=xp[:, s0 + i:s0 + i + sn], scalar=cw[:, 4 * j + i:4 * j + i + 1], in1=ac[:, 0:sn], op0=ALU.mult, op1=ALU.add),
                                         reads=[Bxp, Bac, B_const], writes=[Bac])
                            else:
                                xv = xp[:, 0:NBS * 7].rearrange("p (b s) -> p b s", s=7)
                                P.dma("sp", lambda E: E.dma_start(out=xv[:, :, 0:3], in_=convs_in[l, j * 128:(j + 1) * 128, :].rearrange("p (b s) -> p b s", s=3)), writes=[Bxp])
                                P.op("act", lambda E: E.activation(out=xv[:, :, 3:7], in_=pp[:, 0:sn].rearrange("p (b s) -> p b s", s=4), func=AF.Copy), reads=[Bpp], writes=[Bxp])
                                P.dma("sp", lambda E: E.dma_start(out=cvs_o[l, j * 128:(j + 1) * 128, :].rearrange("p (b s) -> p b s", s=3), in_=xv[:, :, 4:7]), reads=[Bxp])
                                ac, Bac = next_tmp()
                                acv = ac[:, 0:sn].rearrange("p (b s) -> p b s", s=4)
                                P.op("dve", lambda E: E.tensor_scalar(out=acv, in0=xv[:, :, 0:4], scalar1=cw[:, 4 * j:4 * j + 1], scalar2=None, op0=ALU.mult), reads=[Bxp, B_const], writes=[Bac])
                                for i in range(1, 4):
                                    P.op("dve", lambda E, i=i: E.scalar_tensor_tensor(out=acv, in0=xv[:, :, i:i + 4], scalar=cw[:, 4 * j + i:4 * j + i + 1], in1=acv, op0=ALU.mult, op1=ALU.add),
                                         reads=[Bxp, Bac, B_const], writes=[Bac])
                            P.op("act", lambda E: E.activation(out=ac[:, 0:sn], in_=ac[:, 0:sn], func=AF.Silu), reads=[Bac], writes=[Bac])
                            P.dma("sp", lambda E: E.dma_start(out=cqkvT[j * 128:(j + 1) * 128, c0:c0 + sn], in_=ac[:, 0:sn]), reads=[Bac])
                        elif cc == 18:
                            pc = l * 4
                            P.op("act", lambda E: E.activation(out=sm[0:4, 0, 0:sn], in_=pp[0:4, 0:sn], func=AF.Exp, scale=-1.0, bias=sp2[0:4, 0:1]), reads=[Bpp, B_const], writes=[Bsm])
                            P.op("act", lambda E: E.activation(out=sm[64:68, 0, 0:sn], in_=pp[64:68, 0:sn], func=AF.Exp, bias=smallp_sb[64:68, pc + 1:pc + 2]), reads=[Bpp, B_const], writes=[Bsm])
                            P.op("act", lambda E: E.activation(out=sm[0:4, 1, 0:sn], in_=sm[0:4, 0, 0:sn], func=AF.Ln, bias=1.0), reads=[Bsm], writes=[Bsm])
                            P.op("act", lambda E: E.activation(out=sm[64:68, 1, 0:sn], in_=sm[64:68, 0, 0:sn], func=AF.Ln, bias=1.0), reads=[Bsm], writes=[Bsm])
                            P.op("act", lambda E: E.activation(out=sm[32:36, 1, 0:sn], in_=pp[32:36, 0:sn], func=AF.Sigmoid), reads=[Bpp], writes=[Bsm])
                            P.op("dve", lambda E: E.tensor_scalar(out=sm[0:4, 2, 0:sn], in0=sm[0:4, 1, 0:sn], scalar1=-1.0, scalar2=None, op0=ALU.mult), reads=[Bsm], writes=[Bsm])
                            P.op("dve", lambda E: E.tensor_scalar(out=sm[64:68, 2, 0:sn], in0=sm[64:68, 1, 0:sn], scalar1=sp2[64:68, 1:2], scalar2=None, op0=ALU.mult), reads=[Bsm, B_const], writes=[Bsm])
                            P.dma("sp", lambda E: E.dma_start(out=lfT_o[l, :, c0:c0 + sn], in_=sm[0:4, 2, 0:sn]), reads=[Bsm])
                            P.dma("sp", lambda E: E.dma_start(out=bgT[0:4, c0:c0 + sn], in_=sm[32:36, 1, 0:sn]), reads=[Bsm])
                            P.dma("sp", lambda E: E.dma_start(out=bgT[4:8, c0:c0 + sn], in_=sm[64:68, 2, 0:sn]), reads=[Bsm])
                            if is_s:
                                P.dma("sp", lambda E: E.dma_start(out=lfs[:, 0:sn], in_=sm[0:4, 2, 0:sn]), reads=[Bsm])
                            else:
                                P.op("dve", lambda E: E.tensor_tensor_scan(out=sm[0:4, 3, 0:sn], data0=ones3[0:4, 0, 0:sn], data1=sm[0:4, 2, 0:sn], initial=ccar[0:4, 0:1], op0=ALU.mult, op1=ALU.add),
                                     reads=[Bsm, Bccar, B_const], writes=[Bsm])
                                P.op("dve", lambda E: E.tensor_copy(ccar[0:4, 0:1], sm[0:4, 3, sn - 1:sn]), reads=[Bsm], writes=[Bccar])
                                P.op("dve", lambda E: E.tensor_copy(csp[0:4, 0, 0:sn], sm[0:4, 3, 0:sn]), reads=[Bsm], writes=[Bcsp])
                                P.op("dve", lambda E: E.tensor_tensor(out=sm[0:4, 0, 0:sn], in0=sm[0:4, 3, 0:sn], in1=csp[0:4, 0, 0:sn], op=ALU.subtract), reads=[Bsm, Bcsp], writes=[Bsm])
                                P.op("dve", lambda E: E.tensor_copy(csp[0:4, 1, 0:sn], sm[0:4, 0, 0:sn]), reads=[Bsm], writes=[Bcsp])
                                P.op("dve", lambda E: E.tensor_tensor(out=sm[0:4, 1, 0:sn], in0=sm[0:4, 0, 0:sn], in1=csp[0:4, 1, 0:sn], op=ALU.subtract), reads=[Bsm, Bcsp], writes=[Bsm])
                                P.op("dve", lambda E: E.tensor_copy(csp[0:4, 2, 0:sn], sm[0:4, 1, 0:sn]), reads=[Bsm], writes=[Bcsp])
                                P.op("dve", lambda E: E.tensor_scalar(out=csp[0:4, 3:6, 0:sn], in0=csp[0:4, 0:3, 0:sn], scalar1=-1.0, scalar2=None, op0=ALU.mult), reads=[Bcsp], writes=[Bcsp])
                                P.dma("sp", lambda E: E.dma_start(out=fqa[:, 64:67, c0:c0 + sn], in_=csp[0:4, 0:3, 0:sn]), reads=[Bcsp])
                                P.dma("sp", lambda E: E.dma_start(out=fka[:, 67:70, c0:c0 + sn], in_=csp[0:4, 3:6, 0:sn]), reads=[Bcsp])
                                P.dma("sp", lambda E: E.dma_start(out=fqa[:, 67:70, c0:c0 + sn], in_=ones3[0:4, :, 0:sn]), reads=[B_const])
                                P.dma("sp", lambda E: E.dma_start(out=fka[:, 64:67, c0:c0 + sn], in_=ones3[0:4, :, 0:sn]), reads=[B_const])
                        elif cc >= 43:
                            tz, Btz = next_tmp()
                            P.op("act", lambda E: E.activation(out=tz[:, 0:sn], in_=pp[:, 0:sn], func=AF.Silu), reads=[Bpp], writes=[Btz])
                            P.dma("sp", lambda E: E.dma_start(out=zT[(cc - 43) * 128:(cc - 42) * 128, c0:c0 + sn], in_=tz[:, 0:sn]), reads=[Btz])
                        else:
                            st, Bst = next_stg()
                            P.op("act", lambda E: E.activation(out=st[:, 0:sn], in_=pp[:, 0:sn], func=AF.Sigmoid), reads=[Bpp], writes=[Bst])
                            P.dma("sp", lambda E: E.dma_start(out=gatesT[(cc - 19) * 128:(cc - 18) * 128, c0:c0 + sn], in_=st[:, 0:sn]), reads=[Bst])
                if stage == 3:
                    return
                for half in range(2):
                    P.dma("sp", lambda E, half=half: E.dma_start(out=wst[half][:, 0:3072], in_=wtm[l, :, :].rearrange("p (k c) -> p k c", c=768)[:, 4 * half:4 * half + 4, :]) if False else
                          E.dma_start(out=wst[half][:, 0:2816], in_=wtm[l, :, half * 2816:(half + 1) * 2816]), writes=[Bwst[half]])
                    P.op("pool", lambda E, half=half: E.tensor_copy(wtm_sb[:, :, :].rearrange("p k c -> p (k c)")[:, half * 2816:(half + 1) * 2816], wst[half][:, 0:2816]), reads=[Bwst[half]], writes=[Bwtm])
                P.dma("sp", lambda E: E.dma_start(out=wst[0][:, 0:512], in_=wtm[l, :, 5632:6144]), writes=[Bwst[0]])
                P.op("pool", lambda E: E.tensor_copy(wtm_sb[:, :, :].rearrange("p k c -> p (k c)")[:, 5632:6144], wst[0][:, 0:512]), reads=[Bwst[0]], writes=[Bwtm])
                for b0 in range(0, tn, 128):
                    si = b0 // 512
                    r0 = t0 + b0
                    pa, Bpa = next_ps()
                    pz, Bpz = next_ps()
                    for kc in range(8):
                        mm(pa[:, 0:512], hs[:, kc, b0:b0 + 128], wtm_sb[:, kc, 0:512], kc == 0, kc == 7, [Bh[si], Bwtm], [Bpa])
                    for kc in range(8):
                        mm(pz[:, 0:256], hs[:, kc, b0:b0 + 128], wtm_sb[:, kc, 512:768], kc == 0, kc == 7, [Bh[si], Bwtm], [Bpz])
                    for vi, (dst_o, dst_a) in enumerate(((fv_o, fva), (dv_o, dva))):
                        tf, Btf = next_tmp()
                        P.op("act", lambda E, vi=vi, tf=tf: E.activation(out=tf[:, 0:256], in_=pa[:, vi * 256:vi * 256 + 256], func=AF.Copy), reads=[Bpa], writes=[Btf])
                        P.dma("sp", lambda E, tf=tf, dst_o=dst_o: E.dma_start(out=dst_o[l, r0:r0 + 128, :], in_=tf[:, 0:256]), reads=[Btf])
                        P.op("dve", lambda E, tf=tf: E.tensor_copy(tmv[:, :, 0:64], tf[:, 0:256].rearrange("p (h d) -> p h d", d=64)), reads=[Btf], writes=[Btmv])
                        P.dma("sp", lambda E, dst_a=dst_a: E.dma_start(out=dst_a[r0:r0 + 128, :], in_=tmv[:, :, :].rearrange("p h d -> p (h d)")), reads=[Btmv])
                    tf, Btf = next_tmp()
                    P.op("act", lambda E, tf=tf: E.activation(out=tf[:, 0:256], in_=pz[:, 0:256], func=AF.Silu), reads=[Bpz], writes=[Btf])
                    P.dma("sp", lambda E, tf=tf: E.dma_start(out=zs[r0:r0 + 128, :], in_=tf[:, 0:256]), reads=[Btf])

            def phase_c(lc):
                ob = at
                Bob = Buf()
                for bi, src in enumerate((ofT, odT, ogT)):
                    P.dma("sp", lambda E, bi=bi, src=src: E.dma_start(out=ob[:, 2 * bi:2 * bi + 2, 0:tn], in_=src[:, t0:t0 + tn].rearrange("(k p) t -> p k t", p=128)), writes=[Bob])
                Bm = [Buf() for _ in subs]
                for dc in range(8):
                    w, Bw = load_w(wbr[lc, dc], 768)
                    P.dma("sp", lambda E: E.dma_start(out=gt[:, :, 0:tn], in_=gatesT[:, t0:t0 + tn].rearrange("(b k p) t -> k p b t", p=128, k=8)[dc]), writes=[Bgt])
                    for si, (s0, sn) in enumerate(subs):
                        ma, Bma = next_tmp()
                        for bi in range(3):
                            pp, Bpp = next_ps()
                            for k2 in range(2):
                                mm(pp[:, 0:sn], w[:, (bi * 2 + k2) * 128:(bi * 2 + k2 + 1) * 128], ob[:, 2 * bi + k2, s0:s0 + sn], k2 == 0, k2 == 1, [Bw, Bob], [Bpp])
                            if bi == 0:
                                P.op("dve", lambda E: E.tensor_tensor(out=ma[:, 0:sn], in0=pp[:, 0:sn], in1=gt[:, 0, s0:s0 + sn], op=ALU.mult), reads=[Bpp, Bgt], writes=[Bma])
                            else:
                                t2, Bt2 = next_tmp()
                                P.op("dve", lambda E, bi=bi: E.tensor_tensor(out=t2[:, 0:sn], in0=pp[:, 0:sn], in1=gt[:, bi, s0:s0 + sn], op=ALU.mult), reads=[Bpp, Bgt], writes=[Bt2])
                                if bi == 1:
                                    P.op("pool", lambda E: E.tensor_tensor(out=ma[:, 0:sn], in0=ma[:, 0:sn], in1=t2[:, 0:sn], op=ALU.add), reads=[Bma, Bt2], writes=[Bma])
                                else:
                                    P.op("pool", lambda E: E.tensor_tensor(out=hs[:, dc, s0:s0 + sn], in0=ma[:, 0:sn], in1=t2[:, 0:sn], op=ALU.add), reads=[Bma, Bt2], writes=[Bm[si]])
                for dc in range(8):
                    w, Bw = load_w(wout_r[lc, dc], 1024)
                    for si, (s0, sn) in enumerate(subs):
                        pp, Bpp = next_ps()
                        for kc in range(8):
                            mm(pp[:, 0:sn], w[:, kc * 128:(kc + 1) * 128], hs[:, kc, s0:s0 + sn], kc == 0, kc == 7, [Bw, Bm[si]], [Bpp])
                        P.op("dve", lambda E: E.tensor_tensor(out=xs[:, dc, s0:s0 + sn], in0=pp[:, 0:sn], in1=xs[:, dc, s0:s0 + sn], op=ALU.add), reads=[Bpp, Bx[dc][si]], writes=[Bx[dc][si]])
                for si in range(len(subs)):
                    Bh[si].r.extend(Bm[si].r)
                    Bh[si].w = list(Bm[si].w)
                rmsnorm((lc * 3 + 2) * 8)
                ffn(1, lc)

            def final_norm():
                gcol = DEPTH * 24
                for si, (s0, sn) in enumerate(subs):
                    pn, Bpn = next_ps()
                    for kc in range(8):
                        sq, Bsq = next_stg()
                        P.op("dve", lambda E, kc=kc, sq=sq: E.tensor_tensor(out=sq[:, 0:sn], in0=xs[:, kc, s0:s0 + sn], in1=xs[:, kc, s0:s0 + sn], op=ALU.mult), reads=[Bx[kc][si]], writes=[Bsq])
                        mm(pn[:, 0:sn], ones_bf[:, :], sq[:, 0:sn], kc == 0, kc == 7, [Bsq, B_const], [Bpn])
                    rs, Brs = xp, Bxp
                    P.op("act", lambda E: E.activation(out=rs[:, 0:sn], in_=pn[:, 0:sn], func=AF.Sqrt, scale=1.0 / D, bias=1e-6), reads=[Bpn], writes=[Brs])
                    P.op("dve", lambda E: E.reciprocal(rs[:, 0:sn], rs[:, 0:sn]), reads=[Brs], writes=[Brs])
                    for kc in range(8):
                        yo, Byo = next_tmp()
                        P.op("dve", lambda E, kc=kc, yo=yo: E.scalar_tensor_tensor(out=yo[:, 0:sn], in0=xs[:, kc, s0:s0 + sn], scalar=gains_sb[:, gcol + kc:gcol + kc + 1], in1=rs[:, 0:sn], op0=ALU.mult, op1=ALU.mult),
                             reads=[Bx[kc][si], Brs, B_const], writes=[Byo])
                        P.dma("sp", lambda E, kc=kc, yo=yo: E.dma_start(out=yT[kc * 128:(kc + 1) * 128, t0 + s0:t0 + s0 + sn], in_=yo[:, 0:sn]), reads=[Byo])

            if do_c:
                phase_c(l - 1)
            if final:
                final_norm()
                continue
            if do_a:
                phase_a()
            P.dma("sp", lambda E: E.dma_start(out=xT[:, t0:t0 + tn].rearrange("(k p) t -> p k t", p=128), in_=xs[:, :, 0:tn]), reads=[b for r in Bx for b in r])

    if stage == 6:
        with ExitStack() as es:
            attn_sample(es, 0)
    elif stage != 0.1:
        token_phase(0, False, True)
        if stage >= 5:
            for l in range(1, DEPTH):
                P.barrier()
                mixers(l - 1)
                P.barrier()
                token_phase(l, True, True)
            P.barrier()
            mixers(DEPTH - 1)
            P.barrier()
            token_phase(DEPTH, True, False, final=True)
    P.emit()
    return nc


OFF = {}
_o = 0
for _n, _w in (("fq", 256), ("fk", 256), ("fv", 256), ("ff", 4), ("dq", 256), ("dk", 256), ("dv", 256), ("cqkv", 768), ("cb", 4), ("ca", 4), ("cz", 256), ("gates", 3072)):
    OFF[_n] = _o
    _o += _w


def _fm_cols():
    cols = np.full((NFM, 128), -1, np.int64)
    ar = np.arange(128)
    swp = (ar // 32) * 32 + (ar % 32 + 16) % 32
    for i in range(2):
        cols[i] = OFF["fq"] + 128 * i + ar
        cols[2 + i] = OFF["fk"] + 128 * i + ar
        cols[4 + i] = OFF["dq"] + 128 * i + ar
        cols[6 + i] = OFF["dq"] + 128 * i + swp
        cols[8 + i] = OFF["dk"] + 128 * i + ar
        cols[10 + i] = OFF["dk"] + 128 * i + swp
    for i in range(6):
        cols[12 + i] = OFF["cqkv"] + 128 * i + ar
    cols[18, 0:4] = OFF["ff"] + np.arange(4)
    cols[18, 32:36] = OFF["cb"] + np.arange(4)
    cols[18, 64:68] = OFF["ca"] + np.arange(4)
    for i in range(24):
        cols[19 + i] = OFF["gates"] + 128 * i + ar
    for i in range(2):
        cols[43 + i] = OFF["cz"] + 128 * i + ar
    return cols


def _prep_shared(inp):
    f = np.float32
    sh = {}
    for i, nm in enumerate(("ffn1", "ffn2")):
        wi = np.asarray(inp[nm + "_wi"], f)
        sh[f"wi{i}"] = np.ascontiguousarray(wi.reshape(DEPTH, 8, 128, 2, NFC, 128).transpose(0, 4, 2, 1, 3, 5)).reshape(DEPTH, NFC, 128, 2048)
        wo = np.asarray(inp[nm + "_wo"], f)
        sh[f"wo{i}"] = np.ascontiguousarray(wo.reshape(DEPTH, NFC, 128, 8, 128).transpose(0, 3, 2, 1, 4)).reshape(DEPTH, 8, 128, NFC * 128)
    g = np.zeros((128, DEPTH * 24 + 8), f)
    for l in range(DEPTH):
        for n, nm in enumerate(("norm_ffn1", "norm_mix", "norm_ffn2")):
            g[:, (l * 3 + n) * 8:(l * 3 + n) * 8 + 8] = np.asarray(inp[nm], f)[l].reshape(8, 128).T
    g[:, DEPTH * 24:] = np.asarray(inp["norm_final"], f).reshape(8, 128).T
    sh["gains"] = g
    w_in = np.asarray(inp["w_in"], f)
    cols = _fm_cols()
    wpad = np.concatenate([w_in, np.zeros((DEPTH, D, 1), f)], axis=2)
    wf = wpad[:, :, cols.reshape(-1)].reshape(DEPTH, 8, 128, NFM, 128)
    sh["wfm"] = np.ascontiguousarray(wf.transpose(0, 3, 2, 1, 4)).reshape(DEPTH, NFM, 128, 1024)
    tmc = np.concatenate([OFF["fv"] + np.arange(256), OFF["dv"] + np.arange(256), OFF["cz"] + np.arange(256)])
    wt = w_in[:, :, tmc].reshape(DEPTH, 8, 128, 768)
    sh["wtm"] = np.ascontiguousarray(wt.transpose(0, 2, 1, 3)).reshape(DEPTH, 128, 8 * 768)
    pos = np.concatenate([np.arange(NTP), np.tile(2048 + np.arange(4), NBS)]).astype(f)
    r = np.arange(128)
    inv = (np.float32(10000.0) ** (-(np.arange(0, 32, 2).astype(f)) / np.float32(32))).astype(f)
    ang = (pos[None, :] * inv[(r % 32) % 16][:, None]).astype(f)
    sgn = np.where((r % 32) < 16, -1.0, 1.0).astype(f)[:, None]
    sh["cossin"] = np.stack([np.cos(ang).astype(f), (np.sin(ang).astype(f) * sgn).astype(f)]).astype(f)
    sp = np.zeros((128, DEPTH * 4), f)
    for l in range(DEPTH):
        sp[0:4, 4 * l] = np.asarray(inp["fox_f_bias"], f)[l]
        sp[64:68, 4 * l + 1] = np.asarray(inp["delta_dt_bias"], f)[l]
        sp[64:68, 4 * l + 2] = np.asarray(inp["delta_A_log"], f)[l]
    sh["smallp"] = sp
    wb = np.asarray(inp["w_branch"], f)
    sh["wbr"] = np.ascontiguousarray(wb.reshape(DEPTH, 3, 2, 128, 8, 128).transpose(0, 4, 3, 1, 2, 5)).reshape(DEPTH, 8, 128, 768)
    wo_ = np.asarray(inp["w_out"], f)
    sh["wout_r"] = np.ascontiguousarray(wo_.reshape(DEPTH, 8, 128, 8, 128).transpose(0, 3, 2, 1, 4)).reshape(DEPTH, 8, 128, 1024)
    cst = np.zeros((128, 452), f)
    cst[0:64, 450] = 1.0
    cst[64:128, 451] = 1.0
    cst[0:64, 322:386] = np.eye(64, dtype=f)
    cst[64:128, 386:450] = np.eye(64, dtype=f)
    cst[:, 320] = ((np.arange(128) % 64) < 32).astype(f)
    cst[:, 321] = ((np.arange(128) % 64) >= 32).astype(f)
    ar = np.arange(128)
    cst[:, 0:128] = (ar[:, None] <= ar[None, :]).astype(f)
    cst[:, 128:192] = np.concatenate([np.eye(64, dtype=f), np.eye(64, dtype=f)])
    cst[:, 192:320] = ((ar[:, None] // 64) == (ar[None, :] // 64)).astype(f)
    sh["consts"] = cst
    sel = np.zeros((8, 4, 128), f)
    for which in range(2):
        for pr in range(2):
            for p in range(128):
                sel[which * 4 + 2 * pr + p // 64, which * 2 + pr, p] = 1.0
    sh["sel_in"] = sel.reshape(8, 512)
    sh["dlam"] = np.asarray(inp["diff_lambda"], f).reshape(DEPTH, 1, 128)
    sh["dnorm_row"] = np.asarray(inp["diff_norm"], f).reshape(DEPTH, 1, 64)
    pa = np.ones((DEPTH, NPOOL * 128, 1036), f)
    pa[:, :, 0:256] = np.asarray(inp["cache_fox_k"], f).reshape(DEPTH, NPOOL * 128, 256)
    pa[:, :, 256:512] = np.asarray(inp["cache_diff_k"], f).reshape(DEPTH, NPOOL * 128, 256)
    pa[:, :, 512:772].reshape(DEPTH, NPOOL * 128, 4, 65)[..., 0:64] = np.asarray(inp["cache_fox_v"], f).reshape(DEPTH, NPOOL * 128, 4, 64)
    pa[:, :, 772:1032].reshape(DEPTH, NPOOL * 128, 4, 65)[..., 0:64] = np.asarray(inp["cache_diff_v"], f).reshape(DEPTH, NPOOL * 128, 4, 64)
    pa[:, :, 1032:1036] = np.asarray(inp["cache_fox_logf"], f).reshape(DEPTH, NPOOL * 128, 4)
    for i in range(DEPTH):
        sh[f"pool_all{i}"] = pa[i]
    dg = np.zeros((128, DEPTH * 2), f)
    for l in range(DEPTH):
        dg[:, 2 * l] = np.tile(np.asarray(inp["diff_norm"], f)[l], 2)
        dg[:, 2 * l + 1] = np.tile(np.asarray(inp["delta_norm"], f)[l], 2)
    sh["dgain"] = dg
    cw = np.asarray(inp["delta_conv_w"], f)
    sh["convw"] = np.ascontiguousarray(cw.reshape(DEPTH, 4, 6, 128).transpose(0, 3, 2, 1)).reshape(DEPTH, 128, 24)
    return sh


def _prep_core(inp, c):
    f = np.float32
    m = {}
    xs = np.asarray(inp["x_sample"], f)[NBS * c:NBS * (c + 1)].reshape(NS, D)
    m["xT0"] = np.ascontiguousarray(np.concatenate([np.asarray(inp["x_prompt"], f)[c], xs], axis=0).T)
    sc = np.asarray(inp["state_conv"], f)[:, NBS * c:NBS * (c + 1)]
    m["convs_in"] = np.ascontiguousarray(sc.transpose(0, 3, 1, 2)).reshape(DEPTH, 768, NBS * 3)
    m["ptab"] = np.ascontiguousarray(np.asarray(inp["page_table"], np.int32)[NBS * c:NBS * (c + 1)]).reshape(1, NBS * 16)
    sd = np.asarray(inp["state_delta"], f)[:, NBS * c:NBS * (c + 1)]
    m["sdel_in"] = np.ascontiguousarray(sd.reshape(DEPTH, NBS, 2, 2, 64, 64).transpose(0, 1, 3, 4, 2, 5)).reshape(DEPTH, NBS, 128, 128)
    return m


_CACHE = {}
_UID = [0]


def kernel(**inp):
    if "nc" not in _CACHE:
        _CACHE["nc"] = build(*_CACHE.get("args", ()))
    nc = _CACHE["nc"]
    sh = _prep_shared(inp)
    in_maps = []
    for c in range(NCORES):
        m = dict(sh)
        m.update(_prep_core(inp, c))
        in_maps.append(m)
    res = run_bass_kernel_spmd(nc, in_maps, core_ids=list(range(NCORES)))
    R = res.results
    _CACHE["R"] = R
    f = np.float32

    def tm(name, c):
        return R[c][name]

    outs = {}
    yT = [R[c]["yT"] for c in range(NCORES)]
    y_prompt = np.stack([yT[c][:, :NTP].T for c in range(NCORES)])
    y_sample = np.concatenate([yT[c][:, NTP:].T.reshape(NBS, 4, D) for c in range(NCORES)])

    def fm_out(name, shp_tail):
        p = np.stack([R[c][name][:, :, :NTP].transpose(0, 2, 1) for c in range(NCORES)], axis=1)
        s = np.concatenate([R[c][name][:, :, NTP:].transpose(0, 2, 1).reshape(DEPTH, NBS, 4, -1) for c in range(NCORES)], axis=1)
        return p.reshape((DEPTH, NCORES, NTP) + shp_tail), s.reshape((DEPTH, NCORES * NBS, 4) + shp_tail)

    def tm_out(name, shp_tail):
        p = np.stack([R[c][name][:, :NTP] for c in range(NCORES)], axis=1)
        s = np.concatenate([R[c][name][:, NTP:].reshape(DEPTH, NBS, 4, -1) for c in range(NCORES)], axis=1)
        return p.reshape((DEPTH, NCORES, NTP) + shp_tail), s.reshape((DEPTH, NCORES * NBS, 4) + shp_tail)

    fk_p, fk_s = fm_out("fkT_o", (4, 64))
    fv_p, fv_s = tm_out("fv_o", (4, 64))
    lf_p, lf_s = fm_out("lfT_o", (4,))
    dk_p, dk_s = fm_out("dkT_o", (4, 2, 32))
    dv_p, dv_s = tm_out("dv_o", (4, 64))
    cv_p = np.stack([R[c]["cvp_o"].transpose(0, 2, 1) for c in range(NCORES)], axis=1)
    cv_s = np.concatenate([R[c]["cvs_o"].reshape(DEPTH, 768, NBS, 3).transpose(0, 2, 3, 1) for c in range(NCORES)], axis=1)
    ds_p = np.stack([R[c]["dsp_o"].reshape(DEPTH, 2, 64, 2, 64).transpose(0, 3, 1, 2, 4).reshape(DEPTH, 4, 64, 64) for c in range(NCORES)], axis=1)
    ds_s = np.concatenate([R[c]["dss_o"].reshape(DEPTH, NBS, 2, 64, 2, 64).transpose(0, 1, 4, 2, 3, 5).reshape(DEPTH, NBS, 4, 64, 64) for c in range(NCORES)], axis=1)
    return tuple(np.ascontiguousarray(a, dtype=f) for a in (y_prompt, y_sample, fk_p, fk_s, fv_p, fv_s, lf_p, lf_s, dk_p, dk_s, dv_p, dv_s, ds_p, ds_s, cv_p, cv_s))
```

```python
import math
import os
from contextlib import ExitStack
import numpy as np
import concourse.bass as bass
import concourse.mybir as mybir
from concourse.bass_utils import run_bass_kernel_spmd

AF = mybir.ActivationFunctionType
ALU = mybir.AluOpType
AX = mybir.AxisListType
F32, BF16, I32 = mybir.dt.float32, mybir.dt.bfloat16, mybir.dt.int32
ENGS = ["pe", "act", "dve", "pool", "sp"]
ND = 10

class _RecInst:
    def __init__(self, name, a, k):
        self.name, self.a, self.k = name, a, k

    def then_inc(self, *a, **k):
        return self


class _Rec:
    def __getattr__(self, name):
        return lambda *a, **k: _RecInst(name, a, k)


_REC = _Rec()


class Buf:
    __slots__ = ("name", "w", "r")

    def __init__(self, name=""):
        self.name = name
        self.w = []
        self.r = []


class Prog:
    def __init__(self, nc):
        self.nc = nc
        self.q = {e: [] for e in ENGS}
        self.cnt = {e: 0 for e in ENGS}
        self.sem = {e: nc.alloc_semaphore(name=f"c_{e}") for e in ENGS}
        self.dsem = {e: [nc.alloc_semaphore(name=f"d_{e}{i}") for i in range(ND)] for e in ENGS if e != "pe"}
        self.dval = {e: [0] * ND for e in self.dsem}
        self.dnext = {e: 0 for e in self.dsem}
        self.seen = {e: {} for e in ENGS}
        self.nwait = 0

    def eng(self, e):
        nc = self.nc
        return {"pe": nc.tensor, "act": nc.scalar, "dve": nc.vector, "pool": nc.gpsimd, "sp": nc.sync}[e]

    def _wait(self, e, tok):
        kind, key, val = tok
        if kind == "eng" and key == e and e == "pe":
            return
        k = (kind, key if kind == "eng" else id(key))
        if self.seen[e].get(k, 0) >= val:
            return
        self.seen[e][k] = val
        sem = self.sem[key] if kind == "eng" else key
        self.q[e].append(lambda E, sem=sem, val=val: E.wait_ge(sem, val))
        self.nwait += 1

    def _deps(self, e, reads, writes):
        for b in reads:
            for t in b.w:
                self._wait(e, t)
        for b in writes:
            for t in b.w:
                self._wait(e, t)
            for t in b.r:
                self._wait(e, t)

    def _commit(self, tok, reads, writes):
        for b in writes:
            b.w = [tok]
            b.r = []
        for b in reads:
            if b not in writes:
                b.r.append(tok)

    def op(self, e, fn, reads=(), writes=()):
        self._deps(e, reads, writes)
        self.cnt[e] += 1
        k = self.cnt[e]
        sem = self.sem[e]
        r = fn(_REC)
        self.q[e].append(lambda E, r=r, sem=sem: getattr(E, r.name)(*r.a, **r.k).then_inc(sem, 1))
        self._commit(("eng", e, k), reads, writes)

    def dma(self, e, fn, reads=(), writes=()):
        i = self.dnext[e]
        self.dnext[e] = (i + 1) % ND
        sem = self.dsem[e][i]
        if self.dval[e][i] > 0:
            self._wait(e, ("dma", sem, self.dval[e][i]))
        self._deps(e, reads, writes)
        self.dval[e][i] += 16
        v = self.dval[e][i]
        r = fn(_REC)
        self.q[e].append(lambda E, r=r, sem=sem: getattr(E, r.name)(*r.a, **r.k).then_inc(sem, 16))
        self._commit(("dma", sem, v), reads, writes)

    def finish(self):
        for e in self.dsem:
            for i in range(ND):
                if self.dval[e][i] > 0:
                    self._wait("sp", ("dma", self.dsem[e][i], self.dval[e][i]))
        for e in ENGS:
            if e != "sp" and self.cnt[e] > 0:
                self._wait("sp", ("eng", e, self.cnt[e]))

    def emit(self):
        nc = self.nc
        self.finish()
        q = self.q
        with nc.Block() as block:
            @block.sync
            def _(E):
                for f in q["sp"]:
                    f(E)

            @block.tensor
            def _(E):
                for f in q["pe"]:
                    f(E)

            @block.scalar
            def _(E):
                for f in q["act"]:
                    f(E)

            @block.vector
            def _(E):
                for f in q["dve"]:
                    f(E)

            @block.gpsimd
            def _(E):
                for f in q["pool"]:
                    f(E)

    def barrier(self):
        toks = []
        for e in self.dsem:
            for i in range(ND):
                if self.dval[e][i] > 0:
                    toks.append(("dma", self.dsem[e][i], self.dval[e][i]))
        for e in ENGS:
            if self.cnt[e] > 0:
                toks.append(("eng", e, self.cnt[e]))
        for e in ENGS:
            for t in toks:
                if t[0] == "eng" and t[1] == e:
                    continue
                self._wait(e, t)


D = 1024
DFF = 2816
NFC = 22
NTP = 8192
NBS = 64
NS = NBS * 4
NT = NTP + NS
NCORES = 2
DEPTH = 2
NPOOL = 2560
TILES = [(i * 1024, 1024, False) for i in range(8)] + [(NTP, NS, True)]
_UID = [0]
NFM = 45
STAGE = 5


def subtiles(tn):
    return [(s, min(512, tn - s)) for s in range(0, tn, 512)]


def build(stage=STAGE, ntiles=9, fmlist=None):
    nc = bass.Bass("TRN2", target_bir_lowering=False)
    P = Prog(nc)

    def din(name, shape, dt=F32):
        return nc.dram_tensor(name, list(shape), dt, kind="ExternalInput").ap()

    def dout(name, shape, dt=F32):
        return nc.dram_tensor(name, list(shape), dt, kind="ExternalOutput").ap()

    def dint(name, shape, dt=F32):
        return nc.dram_tensor(name, list(shape), dt, kind="Internal").ap()

    xT0 = din("xT0", [D, NT])
    wi_r = [din(f"wi{i}", [DEPTH, NFC, 128, 8 * 256]) for i in range(2)]
    wo_r = [din(f"wo{i}", [DEPTH, 8, 128, NFC * 128]) for i in range(2)]
    gains = din("gains", [128, DEPTH * 3 * 8 + 8])
    wfm = din("wfm", [DEPTH, NFM, 128, 8 * 128])
    wtm = din("wtm", [DEPTH, 128, 8 * 768])
    cossin = din("cossin", [2, 128, NT])
    smallp = din("smallp", [128, DEPTH * 4])
    convw = din("convw", [DEPTH, 128, 6 * 4])
    convs_in = din("convs_in", [DEPTH, 768, NBS * 3])
    consts = din("consts", [128, 452])
    ptab = din("ptab", [1, NBS * 16], I32)
    pool_all = [din(f"pool_all{i}", [NPOOL * 128, 1036]) for i in range(DEPTH)]
    dnorm_row = din("dnorm_row", [DEPTH, 1, 64])
    sel_in = din("sel_in", [8, 512])
    dlam = din("dlam", [DEPTH, 1, 128])
    dgain = din("dgain", [128, DEPTH * 2])
    sdel_in = din("sdel_in", [DEPTH, NBS, 128, 128])
    wbr = din("wbr", [DEPTH, 8, 128, 768])
    wout_r = din("wout_r", [DEPTH, 8, 128, 1024])
    yT = dout("yT", [D, NT])
    fkT_o = dout("fkT_o", [DEPTH, 256, NT])
    fv_o = dout("fv_o", [DEPTH, NT, 256])
    lfT_o = dout("lfT_o", [DEPTH, 4, NT])
    dkT_o = dout("dkT_o", [DEPTH, 256, NT])
    dv_o = dout("dv_o", [DEPTH, NT, 256])
    cvp_o = dout("cvp_o", [DEPTH, 768, 3])
    cvs_o = dout("cvs_o", [DEPTH, 768, NBS * 3])
    dsp_o = dout("dsp_o", [DEPTH, 128, 128])
    dss_o = dout("dss_o", [DEPTH, NBS, 128, 128])
    xT = dint("xT", [D, NT])
    fqa = dint("fqa", [4, 70, NT], BF16)
    fka = dint("fka", [4, 70, NT], BF16)
    fva = dint("fva", [NT, 4 * 65], BF16)
    dqT = dint("dqT", [4, 64, NT], BF16)
    dkT = dint("dkT", [4, 64, NT], BF16)
    dva = dint("dva", [NT, 4 * 65], BF16)
    cqkvT = dint("cqkvT", [768, NT])
    bgT = dint("bgT", [8, NT])
    zs = dint("zs", [NT, 256])
    gatesT = dint("gatesT", [3 * D, NT], BF16)
    lfs = dint("lfs", [4, NS])
    zT = dint("zT", [256, NT])
    ofT = dint("ofT", [256, NT], BF16)
    odT = dint("odT", [256, NT], BF16)
    ogT = dint("ogT", [256, NT], BF16)

    sb = nc.alloc_sbuf_tensor
    gains_sb = sb("gains_sb", [128, DEPTH * 3 * 8 + 8], F32)
    ones_bf = sb("ones_bf", [128, 128], BF16)
    smallp_sb = sb("smallp_sb", [128, DEPTH * 4], F32)
    B_const = Buf("const")
    P.dma("sp", lambda E: E.dma_start(out=gains_sb[:, :], in_=gains), writes=[B_const])
    P.dma("sp", lambda E: E.dma_start(out=smallp_sb[:, :], in_=smallp), writes=[B_const])
    P.op("dve", lambda E: E.memset(ones_bf[:, :], 1.0), writes=[B_const])
    dgain_sb = sb("dgain_sb", [128, DEPTH * 2], F32)
    P.dma("sp", lambda E: E.dma_start(out=dgain_sb[:, :], in_=dgain), writes=[B_const])

    ps = [nc.alloc_psum_tensor(f"ps{i}", [128, 512], F32) for i in range(8)]
    Bps = [Buf(f"ps{i}") for i in range(8)]
    consts_sb = sb("consts_sb", [128, 452], F32)
    tri_bf = sb("tri_bf", [128, 128], BF16)
    bd_bf = sb("bd_bf", [128, 128], BF16)
    onesf = sb("onesf", [128, 128], F32)
    P.dma("sp", lambda E: E.dma_start(out=consts_sb[:, :], in_=consts), writes=[B_const])
    P.op("dve", lambda E: E.tensor_copy(tri_bf[:, :], consts_sb[:, 0:128]), reads=[B_const], writes=[B_const])
    P.op("dve", lambda E: E.tensor_copy(bd_bf[:, :], consts_sb[:, 192:320]), reads=[B_const], writes=[B_const])
    P.op("dve", lambda E: E.memset(onesf[:, :], 1.0), writes=[B_const])
    ident2 = consts_sb[:, 128:192]

    def pqv(pq, h):
        return pq[:, (h % 2) * 256:(h % 2) * 256 + 256]

    def psOv(psO):
        return psO[0:4, 0:780] if False else psO[0:4, 0:512]

    def mmx(out, lhsT, rhs, start, stop, reads, writes):
        P.op("pe", lambda E: E.matmul(out, lhsT=lhsT, rhs=rhs, start=start, stop=stop), reads=reads, writes=writes)

    def mixers(l):
        with ExitStack() as es:
            attn_prompt(es, l)
        P.barrier()
        with ExitStack() as es:
            deltanet(es, l)
        P.barrier()
        with ExitStack() as es:
            attn_sample(es, l)

    def attn_sample(es, l):
        def sb(name, shape, dt):
            _UID[0] += 1
            return es.enter_context(nc.sbuf_tensor(f"{name}_{_UID[0]}", shape, dt))
        lam_init = 0.8 - 0.6 * math.exp(-0.3 * l)
        FK, DK, FV, DV, LF, RW = 0, 256, 512, 772, 1032, 1036
        Gall = sb("Gall", [128, 16, RW], F32)
        BG = [Buf() for _ in range(16)]
        idx = sb("idx", [128, NBS * 16], I32)
        ptf = sb("ptf", [128, NBS * 16], F32)
        iot = sb("iot", [128, 1], F32)
        Bidx = Buf()
        qs = [sb(f"qs{k}", [128, 2, NS], BF16) for k in range(4)]
        qsm = [sb(f"qsm{c}", [128, 2, NS], BF16) for c in range(2)]
        qsf = [sb(f"qsf{k}", [128, 2, NS], F32) for k in range(2)]
        Bqs = Buf()
        vn = [sb(f"vn{k}", [4, NBS, 260], BF16) for k in range(2)]
        lfn = sb("lfn", [4, NBS * 4], F32)
        en = sb("en", [4, NBS, 4], F32)
        Bn = Buf()
        Rq = sb("Rq", [128, 4, 4, 64], F32)
        BRq = Buf()
        qb = [sb(f"qb{k}", [128, 4, 4, 64], F32) for k in range(2)]
        Bqb = [Buf(), Buf()]
        prod = [sb(f"prod{k}", [128, 4, 4, 64], F32) for k in range(2)]
        Bprod = [Buf(), Buf()]
        sc = [sb("scf", [128, 16, 4, 4], F32), sb("scd", [128, 16, 4, 4, 2], F32)]
        Bsc = [Buf(), Buf()]
        lfc = sb("lfc", [128, 16, 4], F32)
        tot = sb("tot", [128, 16, 4], F32)
        pg = sb("pg", [128, 16, 4], F32)
        t1 = sb("t1", [128, 16, 4], F32)
        Br = Buf()
        snw = sb("snw", [4, 3, 4, 4], F32)
        pnb = sb("pnb", [4, 3, 4, 4], F32)
        vnf = sb("vnf", [4, 2, 260], F32)
        Bsn = Buf()
        osb = sb("osb_s", [4, 3, 4, 65], F32)
        rec = sb("rec_s", [4, 3, 4], F32)
        of = sb("of_s", [4, 3, 4, 64], F32)
        o2s = sb("o2_s", [4, 4, 64], F32)
        ssq = sb("ssq_s", [4, 4], F32)
        Bo = Buf()
        grow = sb("grow", [4, 64], F32)
        oc = [sb(f"oc{k}", [128, 2, NS], BF16) for k in range(2)]
        Boc = [Buf(), Buf()]
        lsb2 = sb("lsb2", [4, 2], F32)
        P.dma("pool", lambda E: E.dma_start(out=idx[:, :], in_=ptab.partition_broadcast(128)), writes=[Bidx])
        P.op("pool", lambda E: E.iota(iot[:, :], [[0, 1]], base=0, channel_multiplier=1, allow_small_or_imprecise_dtypes=True), writes=[Bidx])
        P.op("pool", lambda E: E.tensor_copy(ptf[:, :], idx[:, :]), reads=[Bidx], writes=[Bidx])
        P.op("pool", lambda E: E.tensor_scalar(idx[:, :], ptf[:, :], 128.0, iot[:, 0:1], ALU.mult, ALU.add), reads=[Bidx], writes=[Bidx])
        for k, src in enumerate((fqa, fka, dqT, dkT)):
            for h in range(4):
                P.dma("sp", lambda E, k=k, src=src, h=h: E.dma_start(out=qs[k][64 * (h % 2):64 * (h % 2) + 64, h // 2, :], in_=src[h, 0:64, NTP:NT]), writes=[Bqs])
        for c in range(2):
            P.op("dve", lambda E, c=c: E.tensor_scalar(out=qsm[c][:, :, :], in0=qs[2][:, :, :], scalar1=consts_sb[:, 320 + c:321 + c], scalar2=None, op0=ALU.mult), reads=[Bqs, B_const], writes=[Bqs])
        qj = [[sb(f"qj{m}{j}", [128, 2, NS], BF16) for j in range(2)] for m in range(3)]
        for m in range(3):
            srcq = qs[0] if m == 0 else qsm[m - 1]
            for j in range(2):
                P.op("dve", lambda E, m=m, j=j, srcq=srcq: E.tensor_scalar(out=qj[m][j][:, :, :], in0=srcq[:, :, :], scalar1=consts_sb[:, 450 + j:451 + j], scalar2=None, op0=ALU.mult), reads=[Bqs, B_const], writes=[Bqs])
        P.op("dve", lambda E: E.tensor_copy(qsf[0][:, :, :], qs[0][:, :, :]), reads=[Bqs], writes=[Bqs])
        P.op("dve", lambda E: E.tensor_copy(qsf[1][:, :, :], qs[2][:, :, :]), reads=[Bqs], writes=[Bqs])
        for k, src in enumerate((fva, dva)):
            P.dma("sp", lambda E, k=k, src=src: E.dma_start(out=vn[k][:, :, :], in_=src[NTP:NT, :].rearrange("(b k) d -> k b d", k=4)), writes=[Bn])
        for h in range(4):
            P.dma("sp", lambda E, h=h: E.dma_start(out=lfn[:, :].rearrange("k (b h) -> k b h", h=4)[:, :, h], in_=lfs[h].rearrange("(b k) -> k b", k=4), allow_slow_non_contiguous=True), writes=[Bn])
        mmx(ps[7][0:4, 0:NBS * 4], consts_sb[0:4, 0:4], lfn[0:4, :], True, True, [Bn, B_const], [Bps[7]])
        P.op("act", lambda E: E.activation(out=en[:, :, :].rearrange("k b h -> k (b h)"), in_=ps[7][0:4, 0:NBS * 4], func=AF.Copy), reads=[Bps[7]], writes=[Bn])
        P.dma("sp", lambda E: E.dma_start(out=grow[:, :], in_=dnorm_row[l].partition_broadcast(4)), writes=[Bn])
        P.op("dve", lambda E: E.tensor_scalar(out=grow[:, :], in0=grow[:, :], scalar1=1.0 - lam_init, scalar2=None, op0=ALU.mult), reads=[Bn], writes=[Bn])
        dl = sb("dl_s", [4, 128], F32)
        P.dma("sp", lambda E: E.dma_start(out=dl[:, :], in_=dlam[l].partition_broadcast(4)), writes=[Bn])
        P.op("dve", lambda E: E.tensor_tensor(out=dl[:, 0:32], in0=dl[:, 0:32], in1=dl[:, 32:64], op=ALU.mult), reads=[Bn], writes=[Bn])
        P.op("dve", lambda E: E.tensor_tensor(out=dl[:, 64:96], in0=dl[:, 64:96], in1=dl[:, 96:128], op=ALU.mult), reads=[Bn], writes=[Bn])
        P.op("dve", lambda E: E.tensor_reduce(out=lsb2[:, 0:1], in_=dl[:, 0:32], axis=AX.X, op=ALU.add), reads=[Bn], writes=[Bn])
        P.op("dve", lambda E: E.tensor_reduce(out=lsb2[:, 1:2], in_=dl[:, 64:96], axis=AX.X, op=ALU.add), reads=[Bn], writes=[Bn])
        P.op("act", lambda E: E.activation(out=lsb2[:, 0:2], in_=lsb2[:, 0:2], func=AF.Exp), reads=[Bn], writes=[Bn])
        P.op("dve", lambda E: E.tensor_tensor(out=lsb2[:, 0:1], in0=lsb2[:, 1:2], in1=lsb2[:, 0:1], op=ALU.subtract), reads=[Bn], writes=[Bn])
        P.op("dve", lambda E: E.tensor_scalar(out=lsb2[:, 0:1], in0=lsb2[:, 0:1], scalar1=-lam_init, scalar2=None, op0=ALU.add), reads=[Bn], writes=[Bn])
        nlam4 = lsb2[:, 0:1]
        psQ = [(ps[0], Bps[0]), (ps[1], Bps[1])]
        psW, BW = ps[2], Bps[2]
        psOm = [(ps[3], Bps[3]), (ps[6], Bps[6]), (ps[7], Bps[7])]
        psS, BSn = ps[4], Bps[4]
        psT, BT = ps[5], Bps[5]
        KOFF = (FK, DK)
        VOFF = (FV, DV)
        KCUT = int(os.environ.get('KCUT', '99'))
        if KCUT <= 0:
            return
        for b in range(int(os.environ.get('KNB', NBS))):
            c4 = 4 * b
            for s_ in range(16):
                P.dma("pool", lambda E, s_=s_: E.indirect_dma_start(out=Gall[:, s_, :], out_offset=None, in_=pool_all[l], in_offset=bass.IndirectOffsetOnAxis(ap=idx[:, b * 16 + s_:b * 16 + s_ + 1], axis=0)),
                      reads=[Bidx], writes=[BG[s_]])
            if KCUT <= 1:
                continue
            for kind in range(2):
                pq, Bpq = psQ[kind]
                for pr in range(2):
                    for j in range(2):
                        for qi in range(4):
                            P.op("dve", lambda E, pr=pr, qi=qi, j=j: E.tensor_scalar(out=Rq[:, 2 * pr + j, qi, :], in0=consts_sb[:, 322 + 64 * j:386 + 64 * j], scalar1=qsf[kind][:, pr, c4 + qi:c4 + qi + 1], scalar2=None, op0=ALU.mult), reads=[Bqs, B_const], writes=[BRq])
                for pr in range(2):
                    for j in range(2):
                        h = 2 * pr + j
                        mmx(pqv(pq, h), onesf[:, :], Rq[:, h, :, :].rearrange("p q d -> p (q d)"), True, True, [BRq, B_const], [Bpq])
                        if h % 2 == 1:
                            P.op("act", lambda E, h=h: E.activation(out=qb[kind][:, h - 1:h + 1, :, :].rearrange("p h q d -> p (h q d)"), in_=pq[:, 0:512], func=AF.Copy), reads=[Bpq], writes=[Bqb[kind]])
            if KCUT <= 2:
                continue
            for s_ in range(16):
                for kind in range(2):
                    kv = Gall[:, s_, KOFF[kind]:KOFF[kind] + 256].rearrange("p (h d) -> p h d", d=64)
                    i2 = (s_ * 2 + kind) % 2
                    for qi in range(4):
                        P.op("dve", lambda E, qi=qi, kv=kv, i2=i2: E.tensor_tensor(out=prod[i2][:, :, qi, :], in0=qb[kind][:, :, qi, :], in1=kv, op=ALU.mult), reads=[Bqb[kind], BG[s_]], writes=[Bprod[i2]])
                    if kind == 0:
                        P.op("dve", lambda E, i2=i2: E.tensor_reduce(out=sc[0][:, s_, :, :], in_=prod[i2][:, :, :, :], axis=AX.X, op=ALU.add), reads=[Bprod[i2]], writes=[Bsc[0]])
                    else:
                        P.op("dve", lambda E, i2=i2: E.tensor_reduce(out=sc[1][:, s_, :, :, :], in_=prod[i2][:, :, :, :].rearrange("p h q (c d) -> p h q c d", c=2), axis=AX.X, op=ALU.add), reads=[Bprod[i2]], writes=[Bsc[1]])
            if KCUT <= 3:
                continue
            P.op("dve", lambda E: E.tensor_copy(lfc[:, :, :], Gall[:, :, LF:LF + 4]), reads=BG, writes=[Br])
            mmx(psW[:, 0:64], consts_sb[:, 0:128], lfc[:, :, :].rearrange("p s h -> p (s h)"), True, True, [Br, B_const], [BW])
            mmx(psW[:, 64:128], onesf[:, :], lfc[:, :, :].rearrange("p s h -> p (s h)"), True, True, [Br, B_const], [BW])
            P.op("act", lambda E: E.activation(out=tot[:, :, :].rearrange("p s h -> p (s h)"), in_=psW[:, 64:128], func=AF.Copy), reads=[BW], writes=[Br])
            for h in range(4):
                P.op("dve", lambda E, h=h: E.tensor_tensor_scan(out=pg[:, :, h], data0=onesf[:, 0:16], data1=tot[:, :, h], initial=0.0, op0=ALU.mult, op1=ALU.add), reads=[Br, B_const], writes=[Br])
            P.op("dve", lambda E: E.tensor_tensor(out=t1[:, :, :].rearrange("p s h -> p (s h)"), in0=tot[:, :, :].rearrange("p s h -> p (s h)"), in1=psW[:, 0:64], op=ALU.subtract), reads=[Br, BW], writes=[Br])
            P.op("dve", lambda E: E.tensor_tensor(out=t1[:, :, :], in0=pg[:, :, :], in1=t1[:, :, :], op=ALU.subtract), reads=[Br], writes=[Br])
            for h in range(4):
                P.op("dve", lambda E, h=h: E.tensor_scalar(out=t1[:, :, h], in0=t1[:, :, h], scalar1=pg[:, 15, h:h + 1], scalar2=-1.0, op0=ALU.subtract, op1=ALU.mult), reads=[Br], writes=[Br])
            for qi in range(4):
                P.op("dve", lambda E, qi=qi: E.tensor_tensor(out=sc[0][:, :, :, qi], in0=sc[0][:, :, :, qi], in1=t1[:, :, :], op=ALU.add), reads=[Bsc[0], Br], writes=[Bsc[0]])
            if KCUT <= 4:
                continue
            P.op("act", lambda E: E.activation(out=sc[0][:, :, :, :], in_=sc[0][:, :, :, :], func=AF.Exp), reads=[Bsc[0]], writes=[Bsc[0]])
            P.op("act", lambda E: E.activation(out=sc[1][:, :, :, :, :], in_=sc[1][:, :, :, :, :], func=AF.Exp), reads=[Bsc[1]], writes=[Bsc[1]])
            if KCUT <= 5:
                continue
            for m in range(3):
                for h in range(4):
                    pr, j = h // 2, h % 2
                    kk = qs[1] if m == 0 else qs[3]
                    mmx(psS[0:4, (m * 4 + h) * 4:(m * 4 + h) * 4 + 4], kk[:, pr, c4:c4 + 4], qj[m][j][:, pr, c4:c4 + 4], True, True, [Bqs], [BSn])
            P.op("act", lambda E: E.activation(out=snw[:, :, :, :].rearrange("k m h q -> k (m h q)"), in_=psS[0:4, 0:48], func=AF.Copy), reads=[BSn], writes=[Bsn])
            P.op("dve", lambda E: E.tensor_tensor(out=snw[:, 0, :, :], in0=snw[:, 0, :, :], in1=en[:, b, :].unsqueeze(2).to_broadcast([4, 4, 4]), op=ALU.subtract), reads=[Bsn, Bn], writes=[Bsn])
            P.op("act", lambda E: E.activation(out=snw[:, :, :, :], in_=snw[:, :, :, :], func=AF.Exp), reads=[Bsn], writes=[Bsn])
            P.op("dve", lambda E: E.tensor_tensor(out=pnb[:, :, :, :].rearrange("k m h q -> k (m h) q"), in0=snw[:, :, :, :].rearrange("k m h q -> k (m h) q"), in1=consts_sb[0:4, 0:4].unsqueeze(1).to_broadcast([4, 12, 4]), op=ALU.mult), reads=[Bsn, B_const], writes=[Bsn])
            if KCUT <= 6:
                continue
            for kind in range(2):
                P.op("dve", lambda E, kind=kind: E.tensor_copy(vnf[:, kind, :], vn[kind][:, b, :]), reads=[Bn], writes=[Bsn])
            for m in range(3):
                kind = 0 if m == 0 else 1
                psO, BO = psOm[m]
                first = True
                for h in range(4):
                    oc_ = psO[0:4, h * 65:h * 65 + 65]
                    for s_ in range(16):
                        lhs = sc[0][:, s_, h, :] if m == 0 else sc[1][:, s_, h, :, m - 1]
                        mmx(oc_, lhs, Gall[:, s_, VOFF[kind] + h * 65:VOFF[kind] + h * 65 + 65], first, False, [Bsc[kind], BG[s_]], [BO])
                        first = False
                    mmx(oc_, pnb[0:4, m, h, :], vnf[0:4, kind, h * 65:h * 65 + 65], False, True, [Bsn, Bn], [BO])
            if KCUT <= 7:
                continue
            for m in range(3):
                P.op("act", lambda E, m=m: E.activation(out=osb[:, m, :, :].rearrange("k h d -> k (h d)"), in_=psOm[m][0][0:4, 0:260], func=AF.Copy), reads=[psOm[m][1]], writes=[Bo])
            P.op("dve", lambda E: E.reciprocal(rec[:, :, :], osb[:, :, :, 64]), reads=[Bo], writes=[Bo])
            P.op("dve", lambda E: E.tensor_tensor(out=of[:, :, :, :], in0=osb[:, :, :, 0:64], in1=rec[:, :, :].unsqueeze(3).to_broadcast([4, 3, 4, 64]), op=ALU.mult), reads=[Bo], writes=[Bo])
            if KCUT <= 8:
                continue
            P.op("dve", lambda E: E.scalar_tensor_tensor(out=of[:, 1, :, :], in0=of[:, 2, :, :], scalar=nlam4, in1=of[:, 1, :, :], op0=ALU.mult, op1=ALU.add), reads=[Bo, Bn], writes=[Bo])
            P.op("dve", lambda E: E.tensor_tensor(out=o2s[:, :, :], in0=of[:, 1, :, :], in1=of[:, 1, :, :], op=ALU.mult), reads=[Bo], writes=[Bo])
            P.op("dve", lambda E: E.tensor_reduce(out=ssq[:, :], in_=o2s[:, :, :], axis=AX.X, op=ALU.add), reads=[Bo], writes=[Bo])
            P.op("act", lambda E: E.activation(out=ssq[:, :], in_=ssq[:, :], func=AF.Sqrt, scale=1.0 / 64, bias=1e-6), reads=[Bo], writes=[Bo])
            P.op("dve", lambda E: E.reciprocal(ssq[:, :], ssq[:, :]), reads=[Bo], writes=[Bo])
            P.op("dve", lambda E: E.tensor_tensor(out=of[:, 1, :, :], in0=of[:, 1, :, :], in1=ssq[:, :].unsqueeze(2).to_broadcast([4, 4, 64]), op=ALU.mult), reads=[Bo], writes=[Bo])
            P.op("dve", lambda E: E.tensor_tensor(out=of[:, 1, :, :], in0=of[:, 1, :, :], in1=grow[:, :].unsqueeze(1).to_broadcast([4, 4, 64]), op=ALU.mult), reads=[Bo, Bn], writes=[Bo])
            if KCUT <= 9:
                continue
            for kind in range(2):
                for ch in range(2):
                    P.op("pe", lambda E, kind=kind, ch=ch: E.transpose(psT[:, (kind * 2 + ch) * 4:(kind * 2 + ch) * 4 + 4], of[0:4, kind, 2 * ch:2 * ch + 2, :].rearrange("k h d -> k (h d)"), consts_sb[0:4, 128:132]), reads=[Bo, B_const], writes=[BT])
                P.op("act", lambda E, kind=kind: E.activation(out=oc[kind][:, :, c4:c4 + 4], in_=psT[:, kind * 8:kind * 8 + 8].rearrange("p (c q) -> p c q", q=4), func=AF.Copy), reads=[BT], writes=[Boc[kind]])
        for kind, dst in enumerate((ofT, odT)):
            P.dma("sp", lambda E, kind=kind, dst=dst: E.dma_start(out=dst[:, NTP:NT].rearrange("(a p) t -> p a t", p=128), in_=oc[kind][:, :, :]), reads=[Boc[kind]])

    def deltanet(es, l):
        def sb(name, shape, dt):
            _UID[0] += 1
            return es.enter_context(nc.sbuf_tensor(f"{name}_{_UID[0]}", shape, dt))
        S = sb("dS", [128, 2, 64], F32)
        Sa = sb("dSa", [128, 2, 64], F32)
        BS, BSa = Buf(), Buf()
        qk = [sb(f"dqk{i}", [128, 6, 128], F32) for i in range(2)]
        Bqk = [Buf(), Buf()]
        sq = sb("dsq", [128, 4, 128], F32)
        Bsq = Buf()
        bg = [sb(f"dbg{i}", [8, 128], F32) for i in range(2)]
        Bbg = [Buf(), Buf()]
        abc = sb("dabc", [128, 2, 128], F32)
        bbc = sb("dbbc", [128, 2, 128], F32)
        Bab = Buf()
        sel = sb("dsel", [8, 512], F32)
        uu = sb("duu", [128, 2], F32)
        Buu = Buf()
        du = sb("ddu", [128, 2, 64], F32)
        Bdu = Buf()
        oo = sb("doo", [128, 2, 128], F32)
        Boo = Buf()
        o2 = sb("do2", [128, 2, 128], F32)
        Bo2 = Buf()
        zz = sb("dzz", [128, 2, 128], F32)
        Bzz = Buf()
        obf = sb("dobf", [128, 2, 128], BF16)
        Bobf = Buf()
        bdf = consts_sb[:, 192:320]
        P.dma("sp", lambda E: E.dma_start(out=sel[:, :], in_=sel_in), writes=[B_const])
        P.op("dve", lambda E: E.memset(S[:, :, :], 0.0), writes=[BS])
        psK, BK = ps[0], Bps[0]
        psU, BU = ps[1], Bps[1]
        psO, BO = ps[2], Bps[2]
        psN, BN = ps[3], Bps[3]
        gcol = dgain_sb[:, 2 * l + 1:2 * l + 2]

        def block(bi, c0, sample):
            i = bi % 2
            q, Bq = qk[i], Bqk[i]
            P.dma("sp", lambda E: E.dma_start(out=q[:, :, :], in_=cqkvT[:, c0:c0 + 128].rearrange("(c p) t -> p c t", p=128)), writes=[Bq])
            P.dma("sp", lambda E: E.dma_start(out=bg[i][:, :], in_=bgT[:, c0:c0 + 128]), writes=[Bbg[i]])
            P.dma("sp", lambda E: E.dma_start(out=zz[:, :, :], in_=zT[:, c0:c0 + 128].rearrange("(c p) t -> p c t", p=128)), writes=[Bzz])
            P.op("dve", lambda E: E.tensor_tensor(out=sq[:, :, :], in0=q[:, 0:4, :], in1=q[:, 0:4, :], op=ALU.mult), reads=[Bq], writes=[Bsq])
            mmx(psN[:, 0:512], bdf, sq[:, :, :].rearrange("p c t -> p (c t)"), True, True, [Bsq, B_const], [BN])
            P.op("act", lambda E: E.activation(out=sq[:, :, :].rearrange("p c t -> p (c t)"), in_=psN[:, 0:512], func=AF.Sqrt, bias=1e-6), reads=[BN], writes=[Bsq])
            P.op("dve", lambda E: E.reciprocal(sq[:, :, :], sq[:, :, :]), reads=[Bsq], writes=[Bsq])
            P.op("dve", lambda E: E.scalar_tensor_tensor(out=q[:, 0:2, :], in0=q[:, 0:2, :], scalar=0.125, in1=sq[:, 0:2, :], op0=ALU.mult, op1=ALU.mult), reads=[Bq, Bsq], writes=[Bq])
            P.op("dve", lambda E: E.tensor_tensor(out=q[:, 2:4, :], in0=q[:, 2:4, :], in1=sq[:, 2:4, :], op=ALU.mult), reads=[Bq, Bsq], writes=[Bq])
            for which in range(2):
                for pr in range(2):
                    mmx(psN[:, 0:128], sel[0:8, (which * 2 + pr) * 128:(which * 2 + pr + 1) * 128], bg[i][0:8, :], True, True, [Bbg[i], B_const], [BN])
                    dst = bbc if which == 0 else abc
                    P.op("act", lambda E, dst=dst, pr=pr, which=which: E.activation(out=dst[:, pr, :], in_=psN[:, 0:128], func=AF.Copy if which == 0 else AF.Exp), reads=[BN], writes=[Bab])
            for t in range(128):
                if sample and t % 4 == 0:
                    bidx = (c0 - NTP) // 4 + t // 4
                    P.dma("sp", lambda E: E.dma_start(out=S[:, :, :].rearrange("p a v -> p (a v)"), in_=sdel_in[l, bidx]), writes=[BS])
                for pr in range(2):
                    for j in range(2):
                        mmx(psK[64 * j:64 * j + 64, pr:pr + 1], S[64 * j:64 * j + 64, pr, :], q[64 * j:64 * j + 64, 2 + pr, t:t + 1], True, True, [BS, Bq], [BK])
                for pr in range(2):
                    P.op("dve", lambda E, pr=pr: E.tensor_scalar(out=Sa[:, pr, :], in0=S[:, pr, :], scalar1=abc[:, pr, t:t + 1], scalar2=None, op0=ALU.mult), reads=[BS, Bab], writes=[BSa])
                P.op("dve", lambda E: E.tensor_tensor(out=uu[:, 0:2], in0=psK[:, 0:2], in1=abc[:, :, t], op=ALU.mult), reads=[BK, Bab], writes=[Buu])
                P.op("dve", lambda E: E.tensor_tensor(out=uu[:, 0:2], in0=q[:, 4:6, t], in1=uu[:, 0:2], op=ALU.subtract), reads=[Bq, Buu], writes=[Buu])
                for pr in range(2):
                    P.op("dve", lambda E, pr=pr: E.tensor_scalar(out=du[:, pr, :], in0=ident2, scalar1=uu[:, pr:pr + 1], scalar2=bbc[:, pr, t:t + 1], op0=ALU.mult, op1=ALU.mult), reads=[Buu, Bab, B_const], writes=[Bdu])
                for pr in range(2):
                    for j in range(2):
                        mmx(psU[64 * j:64 * j + 64, pr * 64:(pr + 1) * 64], onesf[64 * j:64 * j + 64, 0:64], du[64 * j:64 * j + 64, pr, :], True, True, [Bdu, B_const], [BU])
                for pr in range(2):
                    P.op("dve", lambda E, pr=pr: E.scalar_tensor_tensor(out=S[:, pr, :], in0=psU[:, pr * 64:(pr + 1) * 64], scalar=q[:, 2 + pr, t:t + 1], in1=Sa[:, pr, :], op0=ALU.mult, op1=ALU.add),
                         reads=[BU, Bq, BSa], writes=[BS])
                for pr in range(2):
                    for j in range(2):
                        mmx(psO[64 * j:64 * j + 64, pr * 128 + t:pr * 128 + t + 1], S[64 * j:64 * j + 64, pr, :], q[64 * j:64 * j + 64, pr, t:t + 1], True, True, [BS, Bq], [BO])
                if sample and t % 4 == 3:
                    bidx = (c0 - NTP) // 4 + t // 4
                    P.dma("sp", lambda E: E.dma_start(out=dss_o[l, bidx], in_=S[:, :, :].rearrange("p a v -> p (a v)")), reads=[BS])
            P.op("act", lambda E: E.activation(out=oo[:, :, :].rearrange("p a t -> p (a t)"), in_=psO[:, 0:256], func=AF.Copy), reads=[BO], writes=[Boo])
            P.op("dve", lambda E: E.tensor_tensor(out=o2[:, :, :], in0=oo[:, :, :], in1=oo[:, :, :], op=ALU.mult), reads=[Boo], writes=[Bo2])
            mmx(psN[:, 0:256], bdf, o2[:, :, :].rearrange("p a t -> p (a t)"), True, True, [Bo2, B_const], [BN])
            P.op("act", lambda E: E.activation(out=o2[:, :, :].rearrange("p a t -> p (a t)"), in_=psN[:, 0:256], func=AF.Sqrt, scale=1.0 / 64, bias=1e-6), reads=[BN], writes=[Bo2])
            P.op("dve", lambda E: E.reciprocal(o2[:, :, :], o2[:, :, :]), reads=[Bo2], writes=[Bo2])
            P.op("dve", lambda E: E.scalar_tensor_tensor(out=oo[:, :, :], in0=oo[:, :, :], scalar=gcol, in1=o2[:, :, :], op0=ALU.mult, op1=ALU.mult), reads=[Boo, Bo2, B_const], writes=[Boo])
            P.op("dve", lambda E: E.tensor_tensor(out=obf[:, :, :], in0=oo[:, :, :], in1=zz[:, :, :], op=ALU.mult), reads=[Boo, Bzz], writes=[Bobf])
            P.dma("sp", lambda E: E.dma_start(out=ogT[:, c0:c0 + 128].rearrange("(a p) t -> p a t", p=128), in_=obf[:, :, :]), reads=[Bobf])

        for bi in range(NTP // 128):
            block(bi, bi * 128, False)
        P.dma("sp", lambda E: E.dma_start(out=dsp_o[l], in_=S[:, :, :].rearrange("p a v -> p (a v)")), reads=[BS])
        for bi in range(NS // 128):
            block(bi, NTP + bi * 128, True)

    def attn_prompt(es, l):
        def sb(name, shape, dt):
            _UID[0] += 1
            return es.enter_context(nc.sbuf_tensor(f"{name}_{_UID[0]}", shape, dt))
        lam_init = 0.8 - 0.6 * math.exp(-0.3 * l)
        va = sb("va", [128, 64, 260], BF16)
        Bva = Buf()
        kt = [sb(f"kt{i}", [70, NTP], BF16) for i in range(2)]
        qt = [sb(f"qt{i}", [70, NTP], BF16) for i in range(2)]
        Bkq = [Buf(), Buf()]
        pt = [sb(f"pt{i}", [128, 512], BF16) for i in range(4)]
        Bpt = [Buf() for _ in range(4)]
        osb = [sb(f"osb{i}", [65, 512], F32) for i in range(2)]
        Bosb = [Buf(), Buf()]
        rl = [sb(f"rl{i}", [65, 512], F32) for i in range(2)]
        Brl = [Buf(), Buf()]
        on = sb("on", [64, 512], F32)
        Bon = Buf()
        on2 = sb("on2", [64, 512], F32)
        Bon2 = Buf()
        obf = sb("obf", [64, 512], BF16)
        Bobf = Buf()
        lsb = sb("lsb", [128, 8], F32)
        dl = sb("dl", [1, 128], F32)
        Bl = Buf()
        P.dma("sp", lambda E: E.dma_start(out=dl[0:1, :], in_=dlam[l]), writes=[Bl])
        P.op("dve", lambda E: E.tensor_tensor(out=dl[0:1, 0:32], in0=dl[0:1, 0:32], in1=dl[0:1, 32:64], op=ALU.mult), reads=[Bl], writes=[Bl])
        P.op("dve", lambda E: E.tensor_tensor(out=dl[0:1, 64:96], in0=dl[0:1, 64:96], in1=dl[0:1, 96:128], op=ALU.mult), reads=[Bl], writes=[Bl])
        P.op("dve", lambda E: E.tensor_reduce(out=lsb[0:1, 0:1], in_=dl[0:1, 0:32], axis=AX.X, op=ALU.add), reads=[Bl], writes=[Bl])
        P.op("dve", lambda E: E.tensor_reduce(out=lsb[0:1, 1:2], in_=dl[0:1, 64:96], axis=AX.X, op=ALU.add), reads=[Bl], writes=[Bl])
        P.op("act", lambda E: E.activation(out=lsb[0:1, 2:4], in_=lsb[0:1, 0:2], func=AF.Exp), reads=[Bl], writes=[Bl])
        P.op("dve", lambda E: E.tensor_tensor(out=lsb[0:1, 4:5], in0=lsb[0:1, 3:4], in1=lsb[0:1, 2:3], op=ALU.subtract), reads=[Bl], writes=[Bl])
        P.op("dve", lambda E: E.tensor_scalar(out=lsb[0:1, 4:5], in0=lsb[0:1, 4:5], scalar1=-lam_init, scalar2=None, op0=ALU.add), reads=[Bl], writes=[Bl])
        mmx(ps[6][0:64, 0:1], onesf[0:1, 0:64], lsb[0:1, 4:5], True, True, [Bl, B_const], [Bps[6]])
        P.op("act", lambda E: E.activation(out=lsb[0:64, 5:6], in_=ps[6][0:64, 0:1], func=AF.Copy), reads=[Bps[6]], writes=[Bl])
        nlam = lsb[0:64, 5:6]
        P.op("dve", lambda E: E.tensor_scalar(out=lsb[0:64, 6:7], in0=dgain_sb[0:64, 2 * l:2 * l + 1], scalar1=1.0 - lam_init, scalar2=None, op0=ALU.mult), reads=[B_const], writes=[Bl])
        for q4 in range(4):
            P.dma("sp", lambda E, q4=q4: E.dma_start(out=va[:, 16 * q4:16 * q4 + 16, :], in_=fva[2048 * q4:2048 * (q4 + 1), :].rearrange("(b p) d -> p b d", p=128)), writes=[Bva])
        cnt = {"p": 0, "o": 0}

        def finalize(po, Bpo, which):
            i = cnt["o"] % 2
            cnt["o"] += 1
            P.op("act", lambda E: E.activation(out=osb[i][0:65, :], in_=po[0:65, :], func=AF.Copy), reads=[Bpo], writes=[Bosb[i]])
            P.op("dve", lambda E: E.reciprocal(rl[i][64:65, :], osb[i][64:65, :]), reads=[Bosb[i]], writes=[Brl[i]])
            pb, Bpb = ps[6 + i], Bps[6 + i]
            mmx(pb[0:64, :], onesf[64:65, 0:64], rl[i][64:65, :], True, True, [Brl[i], B_const], [Bpb])
            dst, Bd = (on, Bon) if which == 0 else (on2, Bon2)
            P.op("dve", lambda E: E.tensor_tensor(out=dst[0:64, :], in0=osb[i][0:64, :], in1=pb[0:64, :], op=ALU.mult), reads=[Bosb[i], Bpb], writes=[Bd])

        for kind in range(2):
            if kind == 1:
                for q4 in range(4):
                    P.dma("sp", lambda E, q4=q4: E.dma_start(out=va[:, 16 * q4:16 * q4 + 16, :], in_=dva[2048 * q4:2048 * (q4 + 1), :].rearrange("(b p) d -> p b d", p=128)), writes=[Bva])
            nrow = 70 if kind == 0 else 64
            for h in range(4):
                b = h % 2
                ksrc, qsrc = (fka, fqa) if kind == 0 else (dkT, dqT)
                P.dma("sp", lambda E: E.dma_start(out=kt[b][0:nrow, :], in_=ksrc[h, :, 0:NTP]), writes=[Bkq[b]])
                P.dma("sp", lambda E: E.dma_start(out=qt[b][0:nrow, :], in_=qsrc[h, :, 0:NTP]), writes=[Bkq[b]])
                for qi in range(16):
                    q0 = qi * 512
                    nkb = 4 * qi + 4
                    maps = [(0, nrow)] if kind == 0 else [(0, 32), (32, 64)]
                    pos = [(ps[4], Bps[4]), (ps[5], Bps[5])]
                    if kind == 0:
                        pos = [pos[qi % 2]]
                    for kb in range(nkb):
                        r = kb - 4 * qi
                        c_lo = 128 * r if r > 0 else 0
                        ncol = 512 - c_lo
                        for mi, (r0, r1) in enumerate(maps):
                            j = cnt["p"] % 4
                            cnt["p"] += 1
                            pS, BpS = ps[j], Bps[j]
                            mmx(pS[:, 0:ncol], kt[b][r0:r1, kb * 128:(kb + 1) * 128], qt[b][r0:r1, q0 + c_lo:q0 + 512], True, True, [Bkq[b]], [BpS])
                            P.op("act", lambda E: E.activation(out=pt[j][:, 0:ncol], in_=pS[:, 0:ncol], func=AF.Exp), reads=[BpS], writes=[Bpt[j]])
                            if r >= 0:
                                P.op("pool", lambda E: E.tensor_tensor(out=pt[j][:, 0:128], in0=pt[j][:, 0:128], in1=tri_bf[:, :], op=ALU.mult), reads=[Bpt[j], B_const], writes=[Bpt[j]])
                            po, Bpo = pos[mi]
                            mmx(po[0:65, c_lo:512], va[:, kb, h * 65:(h + 1) * 65], pt[j][:, 0:ncol], kb == 0, kb == nkb - 1, [Bva, Bpt[j]], [Bpo])
                    if kind == 0:
                        finalize(pos[0][0], pos[0][1], 0)
                        P.op("act", lambda E: E.activation(out=obf[:, :], in_=on[0:64, :], func=AF.Copy), reads=[Bon], writes=[Bobf])
                        P.dma("sp", lambda E: E.dma_start(out=ofT[h * 64:(h + 1) * 64, q0:q0 + 512], in_=obf[:, :]), reads=[Bobf])
                    else:
                        finalize(pos[0][0], pos[0][1], 0)
                        finalize(pos[1][0], pos[1][1], 1)
                        P.op("dve", lambda E: E.scalar_tensor_tensor(out=on[0:64, :], in0=on2[0:64, :], scalar=nlam, in1=on[0:64, :], op0=ALU.mult, op1=ALU.add), reads=[Bon, Bon2, Bl], writes=[Bon])
                        P.op("dve", lambda E: E.tensor_tensor(out=obf[:, :], in0=on[0:64, :], in1=on[0:64, :], op=ALU.mult), reads=[Bon], writes=[Bobf])
                        pb, Bpb = ps[6], Bps[6]
                        mmx(pb[0:64, :], ones_bf[0:64, 0:64], obf[:, :], True, True, [Bobf, B_const], [Bpb])
                        P.op("act", lambda E: E.activation(out=on2[0:64, :], in_=pb[0:64, :], func=AF.Sqrt, scale=1.0 / 64, bias=1e-6), reads=[Bpb], writes=[Bon2])
                        P.op("dve", lambda E: E.reciprocal(on2[0:64, :], on2[0:64, :]), reads=[Bon2], writes=[Bon2])
                        P.op("dve", lambda E: E.scalar_tensor_tensor(out=obf[:, :], in0=on[0:64, :], scalar=lsb[0:64, 6:7], in1=on2[0:64, :], op0=ALU.mult, op1=ALU.mult), reads=[Bon, Bon2, Bl], writes=[Bobf])
                        P.dma("sp", lambda E: E.dma_start(out=odT[h * 64:(h + 1) * 64, q0:q0 + 512], in_=obf[:, :]), reads=[Bobf])

    def token_phase(l, do_c, do_a, final=False):
        with ExitStack() as es:
            _token_phase(es, l, do_c, do_a, final)

    def _token_phase(es, l, do_c, do_a, final):
        def sb(name, shape, dt):
            _UID[0] += 1
            return es.enter_context(nc.sbuf_tensor(f"{name}_{_UID[0]}", shape, dt))

        xs = sb(f"xs{l}", [128, 8, 1024], F32)
        hs = sb(f"hs{l}", [128, 8, 1024], BF16)
        at = sb(f"at{l}", [128, NFC, 1024], BF16)
        wst = [sb(f"wst{l}{i}", [128, 2816], F32) for i in range(2)]
        wbf = [sb(f"wbf{l}{i}", [128, 2816], BF16) for i in range(2)]
        Bwst = [Buf() for _ in range(2)]
        Bwbf = [Buf() for _ in range(2)]
        tmp = [sb(f"tmp{l}{i}", [128, 512], F32) for i in range(4)]
        Btmp = [Buf() for _ in range(4)]
        stg = [sb(f"stg{l}{i}", [128, 512], BF16) for i in range(3)]
        Bstg = [Buf() for _ in range(3)]
        cs = sb(f"cs{l}", [128, 2, 1024], F32)
        Bcs = Buf()
        wtm_sb = sb(f"wtm{l}", [128, 8, 768], BF16)
        Bwtm = Buf()
        xp = sb(f"xp{l}", [128, 1024 + 8], F32)
        Bxp = Buf()
        carry = sb(f"carry{l}", [128, 6, 3], F32)
        Bcarry = Buf()
        cw = sb(f"cw{l}", [128, 24], F32)
        ccar = sb(f"ccar{l}", [4, 2], F32)
        csp = sb(f"csp{l}", [4, 8, 512], BF16)
        Bccar = Buf()
        Bcsp = Buf()
        sm = sb(f"sm{l}", [128, 4, 512], F32)
        Bsm = Buf()
        ones3 = sb(f"ones3{l}", [4, 3, 512], BF16)
        tmv = sb(f"tmv{l}", [128, 4, 65], BF16)
        Btmv = Buf()
        rr = {"w": 0, "tmp": 0, "stg": 0, "ps": 0}
        gt = sb(f"gt{l}", [128, 3, 1024], BF16)
        Bgt = Buf()

        P.dma("sp", lambda E: E.dma_start(out=cw[:, :], in_=convw[min(l, DEPTH - 1)]), writes=[B_const])
        P.op("dve", lambda E: E.memset(carry[:, :, :], 0.0), writes=[Bcarry])
        P.op("dve", lambda E: E.memset(ccar[:, :], 0.0), writes=[Bccar])
        P.op("dve", lambda E: E.memset(ones3[:, :, :], 1.0), writes=[B_const])
        P.op("dve", lambda E: E.memset(tmv[:, :, :], 1.0), writes=[Btmv])
        sp2 = sb(f"sp2{l}", [128, 2], F32)
        P.op("dve", lambda E: E.tensor_scalar(out=sp2[0:4, 0:1], in0=smallp_sb[0:4, 4 * min(l, DEPTH - 1):4 * min(l, DEPTH - 1) + 1], scalar1=-1.0, scalar2=None, op0=ALU.mult), reads=[B_const], writes=[B_const])
        P.op("act", lambda E: E.activation(out=sp2[64:68, 1:2], in_=smallp_sb[64:68, 4 * min(l, DEPTH - 1) + 2:4 * min(l, DEPTH - 1) + 3], func=AF.Exp), reads=[B_const], writes=[B_const])
        P.op("dve", lambda E: E.tensor_scalar(out=sp2[64:68, 1:2], in0=sp2[64:68, 1:2], scalar1=-1.0, scalar2=None, op0=ALU.mult), reads=[B_const], writes=[B_const])

        def load_w(src_ap, n):
            i = rr["w"]
            rr["w"] ^= 1
            P.dma("sp", lambda E: E.dma_start(out=wst[i][:, 0:n], in_=src_ap), writes=[Bwst[i]])
            P.op("pool", lambda E: E.tensor_copy(wbf[i][:, 0:n], wst[i][:, 0:n]), reads=[Bwst[i]], writes=[Bwbf[i]])
            return wbf[i], Bwbf[i]

        def next_ps():
            i = rr["ps"]
            rr["ps"] = (i + 1) % 8
            return ps[i], Bps[i]

        def next_tmp():
            i = rr["tmp"]
            rr["tmp"] = (i + 1) % 4
            return tmp[i], Btmp[i]

        def next_stg():
            i = rr["stg"]
            rr["stg"] = (i + 1) % 3
            return stg[i], Bstg[i]

        def mm(out, lhsT, rhs, start, stop, reads, writes):
            P.op("pe", lambda E: E.matmul(out, lhsT=lhsT, rhs=rhs, start=start, stop=stop), reads=reads, writes=writes)

        if stage == 0.2:
            return
        for (t0, tn, is_s) in (TILES[:ntiles] if ntiles > 0 else TILES[ntiles:]):
            subs = subtiles(tn)
            Bx = [[Buf() for _ in subs] for _ in range(8)]
            Bh = [Buf() for _ in subs]
            Bat = [[Buf() for _ in subs] for _ in range(NFC)]
            xsrc = xT0 if (l == 0 and not do_c) else xT
            P.dma("sp", lambda E: E.dma_start(out=xs[:, :, 0:tn], in_=xsrc[:, t0:t0 + tn].rearrange("(k p) t -> p k t", p=128)),
                  writes=[b for r in Bx for b in r])

            def rmsnorm(gcol, mode=0):
                for si, (s0, sn) in enumerate(subs):
                    pn, Bpn = next_ps()
                    for kc in range(8):
                        sq, Bsq = next_stg()
                        P.op("dve", lambda E, kc=kc, sq=sq: E.tensor_tensor(out=sq[:, 0:sn], in0=xs[:, kc, s0:s0 + sn], in1=xs[:, kc, s0:s0 + sn], op=ALU.mult),
                             reads=[Bx[kc][si]], writes=[Bsq])
                        mm(pn[:, 0:sn], ones_bf[:, :], sq[:, 0:sn], kc == 0, kc == 7, [Bsq, B_const], [Bpn])
                    if mode == 1:
                        continue
                    rs, Brs = next_tmp()
                    P.op("act", lambda E: E.activation(out=rs[:, 0:sn], in_=pn[:, 0:sn], func=AF.Sqrt, scale=1.0 / D, bias=1e-6), reads=[Bpn], writes=[Brs])
                    if mode == 2:
                        continue
                    P.op("dve", lambda E: E.reciprocal(rs[:, 0:sn], rs[:, 0:sn]), reads=[Brs], writes=[Brs])
                    for kc in range(8):
                        P.op("dve", lambda E, kc=kc: E.scalar_tensor_tensor(out=hs[:, kc, s0:s0 + sn], in0=xs[:, kc, s0:s0 + sn], scalar=gains_sb[:, gcol + kc:gcol + kc + 1],
                                                                              in1=rs[:, 0:sn], op0=ALU.mult, op1=ALU.mult),
                             reads=[Bx[kc][si], Brs, B_const], writes=[Bh[si]])

            def ffn(which, lw):
                for fc in range(NFC):
                    w, Bw = load_w(wi_r[which][lw, fc], 2048)
                    for si, (s0, sn) in enumerate(subs):
                        pg, Bpg = next_ps()
                        pu, Bpu = next_ps()
                        for kc in range(8):
                            mm(pg[:, 0:sn], w[:, kc * 256:kc * 256 + 128], hs[:, kc, s0:s0 + sn], kc == 0, kc == 7, [Bw, Bh[si]], [Bpg])
                        for kc in range(8):
                            mm(pu[:, 0:sn], w[:, kc * 256 + 128:kc * 256 + 256], hs[:, kc, s0:s0 + sn], kc == 0, kc == 7, [Bw, Bh[si]], [Bpu])
                        sg, Bsg = next_tmp()
                        P.op("act", lambda E: E.activation(out=sg[:, 0:sn], in_=pg[:, 0:sn], func=AF.Silu), reads=[Bpg], writes=[Bsg])
                        P.op("dve", lambda E: E.tensor_tensor(out=at[:, fc, s0:s0 + sn], in0=sg[:, 0:sn], in1=pu[:, 0:sn], op=ALU.mult),
                             reads=[Bsg, Bpu], writes=[Bat[fc][si]])
                for dc in range(8):
                    w, Bw = load_w(wo_r[which][lw, dc], NFC * 128)
                    for si, (s0, sn) in enumerate(subs):
                        po, Bpo = next_ps()
                        for fc in range(NFC):
                            mm(po[:, 0:sn], w[:, fc * 128:(fc + 1) * 128], at[:, fc, s0:s0 + sn], fc == 0, fc == NFC - 1, [Bw, Bat[fc][si]], [Bpo])
                        P.op("dve", lambda E: E.scalar_tensor_tensor(out=xs[:, dc, s0:s0 + sn], in0=po[:, 0:sn], scalar=0.5, in1=xs[:, dc, s0:s0 + sn], op0=ALU.mult, op1=ALU.add),
                             reads=[Bpo, Bx[dc][si]], writes=[Bx[dc][si]])

            def proj_fm(cc, si, s0, sn, w, Bw):
                pp, Bpp = next_ps()
                for kc in range(8):
                    mm(pp[:, 0:sn], w[:, kc * 128:(kc + 1) * 128], hs[:, kc, s0:s0 + sn], kc == 0, kc == 7, [Bw, Bh[si]], [Bpp])
                return pp, Bpp

            def phase_a():
                g0 = (l * 3) * 8
                if stage == 0.3:
                    return
                if stage == 0.5:
                    rmsnorm(g0, mode=1)
                    return
                if stage == 0.6:
                    rmsnorm(g0, mode=2)
                    return
                rmsnorm(g0)
                if stage == 1:
                    return
                ffn(0, l)
                if stage == 2:
                    return
                rmsnorm(g0 + 8)
                P.dma("sp", lambda E: E.dma_start(out=cs[:, :, 0:tn], in_=cossin[:, :, t0:t0 + tn].rearrange("c p t -> p c t")), writes=[Bcs])
                for cc in range(NFM):
                    if fmlist is not None and cc not in fmlist:
                        continue
                    if cc in (6, 7, 10, 11):
                        continue
                    w, Bw = load_w(wfm[l, cc], 1024)
                    if cc in (4, 5, 8, 9):
                        w2, Bw2 = load_w(wfm[l, cc + 2], 1024)
                    for si, (s0, sn) in enumerate(subs):
                        c0 = t0 + s0
                        pp, Bpp = proj_fm(cc, si, s0, sn, w, Bw)
                        if cc in (0, 1):
                            st, Bst = next_stg()
                            P.op("act", lambda E: E.activation(out=st[:, 0:sn], in_=pp[:, 0:sn], func=AF.Copy, scale=0.125), reads=[Bpp], writes=[Bst])
                            for hh in range(2):
                                P.dma("sp", lambda E, hh=hh: E.dma_start(out=fqa[2 * cc + hh, 0:64, c0:c0 + sn], in_=st[64 * hh:64 * hh + 64, 0:sn]), reads=[Bst])
                        elif cc in (2, 3):
                            st, Bst = next_stg()
                            tf, Btf = next_tmp()
                            P.op("act", lambda E: E.activation(out=st[:, 0:sn], in_=pp[:, 0:sn], func=AF.Copy), reads=[Bpp], writes=[Bst])
                            P.op("act", lambda E: E.activation(out=tf[:, 0:sn], in_=pp[:, 0:sn], func=AF.Copy), reads=[Bpp], writes=[Btf])
                            for hh in range(2):
                                P.dma("sp", lambda E, hh=hh: E.dma_start(out=fka[2 * (cc - 2) + hh, 0:64, c0:c0 + sn], in_=st[64 * hh:64 * hh + 64, 0:sn]), reads=[Bst])
                            P.dma("sp", lambda E: E.dma_start(out=fkT_o[l, (cc - 2) * 128:(cc - 1) * 128, c0:c0 + sn], in_=tf[:, 0:sn]), reads=[Btf])
                        elif cc in (4, 5, 8, 9):
                            pp2, Bpp2 = proj_fm(cc + 2, si, s0, sn, w2, Bw2)
                            ta, Bta = next_tmp()
                            tb, Btb = next_tmp()
                            P.op("dve", lambda E: E.tensor_tensor(out=ta[:, 0:sn], in0=pp[:, 0:sn], in1=cs[:, 0, s0:s0 + sn], op=ALU.mult), reads=[Bpp, Bcs], writes=[Bta])
                            P.op("dve", lambda E: E.tensor_tensor(out=tb[:, 0:sn], in0=pp2[:, 0:sn], in1=cs[:, 1, s0:s0 + sn], op=ALU.mult), reads=[Bpp2, Bcs], writes=[Btb])
                            P.op("pool", lambda E: E.tensor_tensor(out=ta[:, 0:sn], in0=ta[:, 0:sn], in1=tb[:, 0:sn], op=ALU.add), reads=[Bta, Btb], writes=[Bta])
                            st, Bst = next_stg()
                            isq = cc in (4, 5)
                            P.op("act", lambda E: E.activation(out=st[:, 0:sn], in_=ta[:, 0:sn], func=AF.Copy, scale=(32 ** -0.5) if isq else 1.0), reads=[Bta], writes=[Bst])
                            dst = dqT if isq else dkT
                            hb = 2 * (cc - (4 if isq else 8))
                            for hh in range(2):
                                P.dma("sp", lambda E, hh=hh: E.dma_start(out=dst[hb + hh, :, c0:c0 + sn], in_=st[64 * hh:64 * hh + 64, 0:sn]), reads=[Bst])
                            if not isq:
                                P.dma("sp", lambda E: E.dma_start(out=dkT_o[l, (cc - 8) * 128:(cc - 7) * 128, c0:c0 + sn], in_=ta[:, 0:sn]), reads=[Bta])
                        elif 12 <= cc <= 17:
                            j = cc - 12
                            if not is_s:
                                if si == 0:
                                    P.op("dve", lambda E: E.tensor_copy(xp[:, 0:3], carry[:, j, :]), reads=[Bcarry], writes=[Bxp])
                                P.op("act", lambda E: E.activation(out=xp[:, 3 + s0:3 + s0 + sn], in_=pp[:, 0:sn], func=AF.Copy), reads=[Bpp], writes=[Bxp])
                                if si == len(subs) - 1:
                                    P.op("dve", lambda E: E.tensor_copy(carry[:, j, :], xp[:, tn:tn + 3]), reads=[Bxp], writes=[Bcarry])
                                    if t0 + tn == NTP:
                                        P.dma("sp", lambda E: E.dma_start(out=cvp_o[l, j * 128:(j + 1) * 128, :], in_=carry[:, j, :]), reads=[Bcarry])
                                ac, Bac = next_tmp()
                                P.op("dve", lambda E: E.tensor_scalar(out=ac[:, 0:sn], in0=xp[:, s0:s0 + sn], scalar1=cw[:, 4 * j:4 * j + 1], scalar2=None, op0=ALU.mult), reads=[Bxp, B_const], writes=[Bac])
                                for i in range(1, 4):
                                    P.op("dve", lambda E, i=i: E.scalar_tensor_tensor(out=ac[:, 0:sn], in0=xp[:, s0 + i:s0 + i + sn], scalar=cw[:, 4 * j + i:4 * j + i + 1], in1=ac[:, 0:sn], op0=ALU.mult, op1=ALU.add),
                                         reads=[Bxp, Bac, B_const], writes=[Bac])
                            else:
                                xv = xp[:, 0:NBS * 7].rearrange("p (b s) -> p b s", s=7)
                                P.dma("sp", lambda E: E.dma_start(out=xv[:, :, 0:3], in_=convs_in[l, j * 128:(j + 1) * 128, :].rearrange("p (b s) -> p b s", s=3)), writes=[Bxp])
                                P.op("act", lambda E: E.activation(out=xv[:, :, 3:7], in_=pp[:, 0:sn].rearrange("p (b s) -> p b s", s=4), func=AF.Copy), reads=[Bpp], writes=[Bxp])
                                P.dma("sp", lambda E: E.dma_start(out=cvs_o[l, j * 128:(j + 1) * 128, :].rearrange("p (b s) -> p b s", s=3), in_=xv[:, :, 4:7]), reads=[Bxp])
                                ac, Bac = next_tmp()
                                acv = ac[:, 0:sn].rearrange("p (b s) -> p b s", s=4)
                                P.op("dve", lambda E: E.tensor_scalar(out=acv, in0=xv[:, :, 0:4], scalar1=cw[:, 4 * j:4 * j + 1], scalar2=None, op0=ALU.mult), reads=[Bxp, B_const], writes=[Bac])
                                for i in range(1, 4):
                                    P.op("dve", lambda E, i=i: E.scalar_tensor_tensor(out=acv, in0=xv[:, :, i:i + 4], scalar=cw[:, 4 * j + i:4 * j + i + 1], in1=acv, op0=ALU.mult, op1=ALU.add),
                                         reads=[Bxp, Bac, B_const], writes=[Bac])
                            P.op("act", lambda E: E.activation(out=ac[:, 0:sn], in_=ac[:, 0:sn], func=AF.Silu), reads=[Bac], writes=[Bac])
                            P.dma("sp", lambda E: E.dma_start(out=cqkvT[j * 128:(j + 1) * 128, c0:c0 + sn], in_=ac[:, 0:sn]), reads=[Bac])
                        elif cc == 18:
                            pc = l * 4
                            P.op("act", lambda E: E.activation(out=sm[0:4, 0, 0:sn], in_=pp[0:4, 0:sn], func=AF.Exp, scale=-1.0, bias=sp2[0:4, 0:1]), reads=[Bpp, B_const], writes=[Bsm])
                            P.op("act", lambda E: E.activation(out=sm[64:68, 0, 0:sn], in_=pp[64:68, 0:sn], func=AF.Exp, bias=smallp_sb[64:68, pc + 1:pc + 2]), reads=[Bpp, B_const], writes=[Bsm])
                            P.op("act", lambda E: E.activation(out=sm[0:4, 1, 0:sn], in_=sm[0:4, 0, 0:sn], func=AF.Ln, bias=1.0), reads=[Bsm], writes=[Bsm])
                            P.op("act", lambda E: E.activation(out=sm[64:68, 1, 0:sn], in_=sm[64:68, 0, 0:sn], func=AF.Ln, bias=1.0), reads=[Bsm], writes=[Bsm])
                            P.op("act", lambda E: E.activation(out=sm[32:36, 1, 0:sn], in_=pp[32:36, 0:sn], func=AF.Sigmoid), reads=[Bpp], writes=[Bsm])
                            P.op("dve", lambda E: E.tensor_scalar(out=sm[0:4, 2, 0:sn], in0=sm[0:4, 1, 0:sn], scalar1=-1.0, scalar2=None, op0=ALU.mult), reads=[Bsm], writes=[Bsm])
                            P.op("dve", lambda E: E.tensor_scalar(out=sm[64:68, 2, 0:sn], in0=sm[64:68, 1, 0:sn], scalar1=sp2[64:68, 1:2], scalar2=None, op0=ALU.mult), reads=[Bsm, B_const], writes=[Bsm])
                            P.dma("sp", lambda E: E.dma_start(out=lfT_o[l, :, c0:c0 + sn], in_=sm[0:4, 2, 0:sn]), reads=[Bsm])
                            P.dma("sp", lambda E: E.dma_start(out=bgT[0:4, c0:c0 + sn], in_=sm[32:36, 1, 0:sn]), reads=[Bsm])
                            P.dma("sp", lambda E: E.dma_start(out=bgT[4:8, c0:c0 + sn], in_=sm[64:68, 2, 0:sn]), reads=[Bsm])
                            if is_s:
                                P.dma("sp", lambda E: E.dma_start(out=lfs[:, 0:sn], in_=sm[0:4, 2, 0:sn]), reads=[Bsm])
                            else:
                                P.op("dve", lambda E: E.tensor_tensor_scan(out=sm[0:4, 3, 0:sn], data0=ones3[0:4, 0, 0:sn], data1=sm[0:4, 2, 0:sn], initial=ccar[0:4, 0:1], op0=ALU.mult, op1=ALU.add),
                                     reads=[Bsm, Bccar, B_const], writes=[Bsm])
                                P.op("dve", lambda E: E.tensor_copy(ccar[0:4, 0:1], sm[0:4, 3, sn - 1:sn]), reads=[Bsm], writes=[Bccar])
                                P.op("dve", lambda E: E.tensor_copy(csp[0:4, 0, 0:sn], sm[0:4, 3, 0:sn]), reads=[Bsm], writes=[Bcsp])
                                P.op("dve", lambda E: E.tensor_tensor(out=sm[0:4, 0, 0:sn], in0=sm[0:4, 3, 0:sn], in1=csp[0:4, 0, 0:sn], op=ALU.subtract), reads=[Bsm, Bcsp], writes=[Bsm])
                                P.op("dve", lambda E: E.tensor_copy(csp[0:4, 1, 0:sn], sm[0:4, 0, 0:sn]), reads=[Bsm], writes=[Bcsp])
                                P.op("dve", lambda E: E.tensor_tensor(out=sm[0:4, 1, 0:sn], in0=sm[0:4, 0, 0:sn], in1=csp[0:4, 1, 0:sn], op=ALU.subtract), reads=[Bsm, Bcsp], writes=[Bsm])
                                P.op("dve", lambda E: E.tensor_copy(csp[0:4, 2, 0:sn], sm[0:4, 1, 0:sn]), reads=[Bsm], writes=[Bcsp])
                                P.op("dve", lambda E: E.tensor_scalar(out=csp[0:4, 3:6, 0:sn], in0=csp[0:4, 0:3, 0:sn], scalar1=-1.0, scalar2=None, op0=ALU.mult), reads=[Bcsp], writes=[Bcsp])
                                P.dma("sp", lambda E: E.dma_start(out=fqa[:, 64:67, c0:c0 + sn], in_=csp[0:4, 0:3, 0:sn]), reads=[Bcsp])
                                P.dma("sp", lambda E: E.dma_start(out=fka[:, 67:70, c0:c0 + sn], in_=csp[0:4, 3:6, 0:sn]), reads=[Bcsp])
                                P.dma("sp", lambda E: E.dma_start(out=fqa[:, 67:70, c0:c0 + sn], in_=ones3[0:4, :, 0:sn]), reads=[B_const])
                                P.dma("sp", lambda E: E.dma_start(out=fka[:, 64:67, c0:c0 + sn], in_=ones3[0:4, :, 0:sn]), reads=[B_const])
                        elif cc >= 43:
                            tz, Btz = next_tmp()
                            P.op("act", lambda E: E.activation(out=tz[:, 0:sn], in_=pp[:, 0:sn], func=AF.Silu), reads=[Bpp], writes=[Btz])
                            P.dma("sp", lambda E: E.dma_start(out=zT[(cc - 43) * 128:(cc - 42) * 128, c0:c0 + sn], in_=tz[:, 0:sn]), reads=[Btz])
                        else:
                            st, Bst = next_stg()
                            P.op("act", lambda E: E.activation(out=st[:, 0:sn], in_=pp[:, 0:sn], func=AF.Sigmoid), reads=[Bpp], writes=[Bst])
                            P.dma("sp", lambda E: E.dma_start(out=gatesT[(cc - 19) * 128:(cc - 18) * 128, c0:c0 + sn], in_=st[:, 0:sn]), reads=[Bst])
                if stage == 3:
                    return
                for half in range(2):
                    P.dma("sp", lambda E, half=half: E.dma_start(out=wst[half][:, 0:3072], in_=wtm[l, :, :].rearrange("p (k c) -> p k c", c=768)[:, 4 * half:4 * half + 4, :]) if False else
                          E.dma_start(out=wst[half][:, 0:2816], in_=wtm[l, :, half * 2816:(half + 1) * 2816]), writes=[Bwst[half]])
                    P.op("pool", lambda E, half=half: E.tensor_copy(wtm_sb[:, :, :].rearrange("p k c -> p (k c)")[:, half * 2816:(half + 1) * 2816], wst[half][:, 0:2816]), reads=[Bwst[half]], writes=[Bwtm])
                P.dma("sp", lambda E: E.dma_start(out=wst[0][:, 0:512], in_=wtm[l, :, 5632:6144]), writes=[Bwst[0]])
                P.op("pool", lambda E: E.tensor_copy(wtm_sb[:, :, :].rearrange("p k c -> p (k c)")[:, 5632:6144], wst[0][:, 0:512]), reads=[Bwst[0]], writes=[Bwtm])
                for b0 in range(0, tn, 128):
                    si = b0 // 512
                    r0 = t0 + b0
                    pa, Bpa = next_ps()
                    pz, Bpz = next_ps()
                    for kc in range(8):
                        mm(pa[:, 0:512], hs[:, kc, b0:b0 + 128], wtm_sb[:, kc, 0:512], kc == 0, kc == 7, [Bh[si], Bwtm], [Bpa])
                    for kc in range(8):
                        mm(pz[:, 0:256], hs[:, kc, b0:b0 + 128], wtm_sb[:, kc, 512:768], kc == 0, kc == 7, [Bh[si], Bwtm], [Bpz])
                    for vi, (dst_o, dst_a) in enumerate(((fv_o, fva), (dv_o, dva))):
                        tf, Btf = next_tmp()
                        P.op("act", lambda E, vi=vi, tf=tf: E.activation(out=tf[:, 0:256], in_=pa[:, vi * 256:vi * 256 + 256], func=AF.Copy), reads=[Bpa], writes=[Btf])
                        P.dma("sp", lambda E, tf=tf, dst_o=dst_o: E.dma_start(out=dst_o[l, r0:r0 + 128, :], in_=tf[:, 0:256]), reads=[Btf])
                        P.op("dve", lambda E, tf=tf: E.tensor_copy(tmv[:, :, 0:64], tf[:, 0:256].rearrange("p (h d) -> p h d", d=64)), reads=[Btf], writes=[Btmv])
                        P.dma("sp", lambda E, dst_a=dst_a: E.dma_start(out=dst_a[r0:r0 + 128, :], in_=tmv[:, :, :].rearrange("p h d -> p (h d)")), reads=[Btmv])
                    tf, Btf = next_tmp()
                    P.op("act", lambda E, tf=tf: E.activation(out=tf[:, 0:256], in_=pz[:, 0:256], func=AF.Silu), reads=[Bpz], writes=[Btf])
                    P.dma("sp", lambda E, tf=tf: E.dma_start(out=zs[r0:r0 + 128, :], in_=tf[:, 0:256]), reads=[Btf])

            def phase_c(lc):
                ob = at
                Bob = Buf()
                for bi, src in enumerate((ofT, odT, ogT)):
                    P.dma("sp", lambda E, bi=bi, src=src: E.dma_start(out=ob[:, 2 * bi:2 * bi + 2, 0:tn], in_=src[:, t0:t0 + tn].rearrange("(k p) t -> p k t", p=128)), writes=[Bob])
                Bm = [Buf() for _ in subs]
                for dc in range(8):
                    w, Bw = load_w(wbr[lc, dc], 768)
                    P.dma("sp", lambda E: E.dma_start(out=gt[:, :, 0:tn], in_=gatesT[:, t0:t0 + tn].rearrange("(b k p) t -> k p b t", p=128, k=8)[dc]), writes=[Bgt])
                    for si, (s0, sn) in enumerate(subs):
                        ma, Bma = next_tmp()
                        for bi in range(3):
                            pp, Bpp = next_ps()
                            for k2 in range(2):
                                mm(pp[:, 0:sn], w[:, (bi * 2 + k2) * 128:(bi * 2 + k2 + 1) * 128], ob[:, 2 * bi + k2, s0:s0 + sn], k2 == 0, k2 == 1, [Bw, Bob], [Bpp])
                            if bi == 0:
                                P.op("dve", lambda E: E.tensor_tensor(out=ma[:, 0:sn], in0=pp[:, 0:sn], in1=gt[:, 0, s0:s0 + sn], op=ALU.mult), reads=[Bpp, Bgt], writes=[Bma])
                            else:
                                t2, Bt2 = next_tmp()
                                P.op("dve", lambda E, bi=bi: E.tensor_tensor(out=t2[:, 0:sn], in0=pp[:, 0:sn], in1=gt[:, bi, s0:s0 + sn], op=ALU.mult), reads=[Bpp, Bgt], writes=[Bt2])
                                if bi == 1:
                                    P.op("pool", lambda E: E.tensor_tensor(out=ma[:, 0:sn], in0=ma[:, 0:sn], in1=t2[:, 0:sn], op=ALU.add), reads=[Bma, Bt2], writes=[Bma])
                                else:
                                    P.op("pool", lambda E: E.tensor_tensor(out=hs[:, dc, s0:s0 + sn], in0=ma[:, 0:sn], in1=t2[:, 0:sn], op=ALU.add), reads=[Bma, Bt2], writes=[Bm[si]])
                for dc in range(8):
                    w, Bw = load_w(wout_r[lc, dc], 1024)
                    for si, (s0, sn) in enumerate(subs):
                        pp, Bpp = next_ps()
                        for kc in range(8):
                            mm(pp[:, 0:sn], w[:, kc * 128:(kc + 1) * 128], hs[:, kc, s0:s0 + sn], kc == 0, kc == 7, [Bw, Bm[si]], [Bpp])
                        P.op("dve", lambda E: E.tensor_tensor(out=xs[:, dc, s0:s0 + sn], in0=pp[:, 0:sn], in1=xs[:, dc, s0:s0 + sn], op=ALU.add), reads=[Bpp, Bx[dc][si]], writes=[Bx[dc][si]])
                for si in range(len(subs)):
                    Bh[si].r.extend(Bm[si].r)
                    Bh[si].w = list(Bm[si].w)
                rmsnorm((lc * 3 + 2) * 8)
                ffn(1, lc)

            def final_norm():
                gcol = DEPTH * 24
                for si, (s0, sn) in enumerate(subs):
                    pn, Bpn = next_ps()
                    for kc in range(8):
                        sq, Bsq = next_stg()
                        P.op("dve", lambda E, kc=kc, sq=sq: E.tensor_tensor(out=sq[:, 0:sn], in0=xs[:, kc, s0:s0 + sn], in1=xs[:, kc, s0:s0 + sn], op=ALU.mult), reads=[Bx[kc][si]], writes=[Bsq])
                        mm(pn[:, 0:sn], ones_bf[:, :], sq[:, 0:sn], kc == 0, kc == 7, [Bsq, B_const], [Bpn])
                    rs, Brs = xp, Bxp
                    P.op("act", lambda E: E.activation(out=rs[:, 0:sn], in_=pn[:, 0:sn], func=AF.Sqrt, scale=1.0 / D, bias=1e-6), reads=[Bpn], writes=[Brs])
                    P.op("dve", lambda E: E.reciprocal(rs[:, 0:sn], rs[:, 0:sn]), reads=[Brs], writes=[Brs])
                    for kc in range(8):
                        yo, Byo = next_tmp()
                        P.op("dve", lambda E, kc=kc, yo=yo: E.scalar_tensor_tensor(out=yo[:, 0:sn], in0=xs[:, kc, s0:s0 + sn], scalar=gains_sb[:, gcol + kc:gcol + kc + 1], in1=rs[:, 0:sn], op0=ALU.mult, op1=ALU.mult),
                             reads=[Bx[kc][si], Brs, B_const], writes=[Byo])
                        P.dma("sp", lambda E, kc=kc, yo=yo: E.dma_start(out=yT[kc * 128:(kc + 1) * 128, t0 + s0:t0 + s0 + sn], in_=yo[:, 0:sn]), reads=[Byo])

            if do_c:
                phase_c(l - 1)
            if final:
                final_norm()
                continue
            if do_a:
                phase_a()
            P.dma("sp", lambda E: E.dma_start(out=xT[:, t0:t0 + tn].rearrange("(k p) t -> p k t", p=128), in_=xs[:, :, 0:tn]), reads=[b for r in Bx for b in r])

    if stage == 6:
        with ExitStack() as es:
            attn_sample(es, 0)
    elif stage != 0.1:
        token_phase(0, False, True)
        if stage >= 5:
            for l in range(1, DEPTH):
                P.barrier()
                mixers(l - 1)
                P.barrier()
                token_phase(l, True, True)
            P.barrier()
            mixers(DEPTH - 1)
            P.barrier()
            token_phase(DEPTH, True, False, final=True)
    P.emit()
    return nc


OFF = {}
_o = 0
for _n, _w in (("fq", 256), ("fk", 256), ("fv", 256), ("ff", 4), ("dq", 256), ("dk", 256), ("dv", 256), ("cqkv", 768), ("cb", 4), ("ca", 4), ("cz", 256), ("gates", 3072)):
    OFF[_n] = _o
    _o += _w


def _fm_cols():
    cols = np.full((NFM, 128), -1, np.int64)
    ar = np.arange(128)
    swp = (ar // 32) * 32 + (ar % 32 + 16) % 32
    for i in range(2):
        cols[i] = OFF["fq"] + 128 * i + ar
        cols[2 + i] = OFF["fk"] + 128 * i + ar
        cols[4 + i] = OFF["dq"] + 128 * i + ar
        cols[6 + i] = OFF["dq"] + 128 * i + swp
        cols[8 + i] = OFF["dk"] + 128 * i + ar
        cols[10 + i] = OFF["dk"] + 128 * i + swp
    for i in range(6):
        cols[12 + i] = OFF["cqkv"] + 128 * i + ar
    cols[18, 0:4] = OFF["ff"] + np.arange(4)
    cols[18, 32:36] = OFF["cb"] + np.arange(4)
    cols[18, 64:68] = OFF["ca"] + np.arange(4)
    for i in range(24):
        cols[19 + i] = OFF["gates"] + 128 * i + ar
    for i in range(2):
        cols[43 + i] = OFF["cz"] + 128 * i + ar
    return cols


def _prep_shared(inp):
    f = np.float32
    sh = {}
    for i, nm in enumerate(("ffn1", "ffn2")):
        wi = np.asarray(inp[nm + "_wi"], f)
        sh[f"wi{i}"] = np.ascontiguousarray(wi.reshape(DEPTH, 8, 128, 2, NFC, 128).transpose(0, 4, 2, 1, 3, 5)).reshape(DEPTH, NFC, 128, 2048)
        wo = np.asarray(inp[nm + "_wo"], f)
        sh[f"wo{i}"] = np.ascontiguousarray(wo.reshape(DEPTH, NFC, 128, 8, 128).transpose(0, 3, 2, 1, 4)).reshape(DEPTH, 8, 128, NFC * 128)
    g = np.zeros((128, DEPTH * 24 + 8), f)
    for l in range(DEPTH):
        for n, nm in enumerate(("norm_ffn1", "norm_mix", "norm_ffn2")):
            g[:, (l * 3 + n) * 8:(l * 3 + n) * 8 + 8] = np.asarray(inp[nm], f)[l].reshape(8, 128).T
    g[:, DEPTH * 24:] = np.asarray(inp["norm_final"], f).reshape(8, 128).T
    sh["gains"] = g
    w_in = np.asarray(inp["w_in"], f)
    cols = _fm_cols()
    wpad = np.concatenate([w_in, np.zeros((DEPTH, D, 1), f)], axis=2)
    wf = wpad[:, :, cols.reshape(-1)].reshape(DEPTH, 8, 128, NFM, 128)
    sh["wfm"] = np.ascontiguousarray(wf.transpose(0, 3, 2, 1, 4)).reshape(DEPTH, NFM, 128, 1024)
    tmc = np.concatenate([OFF["fv"] + np.arange(256), OFF["dv"] + np.arange(256), OFF["cz"] + np.arange(256)])
    wt = w_in[:, :, tmc].reshape(DEPTH, 8, 128, 768)
    sh["wtm"] = np.ascontiguousarray(wt.transpose(0, 2, 1, 3)).reshape(DEPTH, 128, 8 * 768)
    pos = np.concatenate([np.arange(NTP), np.tile(2048 + np.arange(4), NBS)]).astype(f)
    r = np.arange(128)
    inv = (np.float32(10000.0) ** (-(np.arange(0, 32, 2).astype(f)) / np.float32(32))).astype(f)
    ang = (pos[None, :] * inv[(r % 32) % 16][:, None]).astype(f)
    sgn = np.where((r % 32) < 16, -1.0, 1.0).astype(f)[:, None]
    sh["cossin"] = np.stack([np.cos(ang).astype(f), (np.sin(ang).astype(f) * sgn).astype(f)]).astype(f)
    sp = np.zeros((128, DEPTH * 4), f)
    for l in range(DEPTH):
        sp[0:4, 4 * l] = np.asarray(inp["fox_f_bias"], f)[l]
        sp[64:68, 4 * l + 1] = np.asarray(inp["delta_dt_bias"], f)[l]
        sp[64:68, 4 * l + 2] = np.asarray(inp["delta_A_log"], f)[l]
    sh["smallp"] = sp
    wb = np.asarray(inp["w_branch"], f)
    sh["wbr"] = np.ascontiguousarray(wb.reshape(DEPTH, 3, 2, 128, 8, 128).transpose(0, 4, 3, 1, 2, 5)).reshape(DEPTH, 8, 128, 768)
    wo_ = np.asarray(inp["w_out"], f)
    sh["wout_r"] = np.ascontiguousarray(wo_.reshape(DEPTH, 8, 128, 8, 128).transpose(0, 3, 2, 1, 4)).reshape(DEPTH, 8, 128, 1024)
    cst = np.zeros((128, 452), f)
    cst[0:64, 450] = 1.0
    cst[64:128, 451] = 1.0
    cst[0:64, 322:386] = np.eye(64, dtype=f)
    cst[64:128, 386:450] = np.eye(64, dtype=f)
    cst[:, 320] = ((np.arange(128) % 64) < 32).astype(f)
    cst[:, 321] = ((np.arange(128) % 64) >= 32).astype(f)
    ar = np.arange(128)
    cst[:, 0:128] = (ar[:, None] <= ar[None, :]).astype(f)
    cst[:, 128:192] = np.concatenate([np.eye(64, dtype=f), np.eye(64, dtype=f)])
    cst[:, 192:320] = ((ar[:, None] // 64) == (ar[None, :] // 64)).astype(f)
    sh["consts"] = cst
    sel = np.zeros((8, 4, 128), f)
    for which in range(2):
        for pr in range(2):
            for p in range(128):
                sel[which * 4 + 2 * pr + p // 64, which * 2 + pr, p] = 1.0
    sh["sel_in"] = sel.reshape(8, 512)
    sh["dlam"] = np.asarray(inp["diff_lambda"], f).reshape(DEPTH, 1, 128)
    sh["dnorm_row"] = np.asarray(inp["diff_norm"], f).reshape(DEPTH, 1, 64)
    pa = np.ones((DEPTH, NPOOL * 128, 1036), f)
    pa[:, :, 0:256] = np.asarray(inp["cache_fox_k"], f).reshape(DEPTH, NPOOL * 128, 256)
    pa[:, :, 256:512] = np.asarray(inp["cache_diff_k"], f).reshape(DEPTH, NPOOL * 128, 256)
    pa[:, :, 512:772].reshape(DEPTH, NPOOL * 128, 4, 65)[..., 0:64] = np.asarray(inp["cache_fox_v"], f).reshape(DEPTH, NPOOL * 128, 4, 64)
    pa[:, :, 772:1032].reshape(DEPTH, NPOOL * 128, 4, 65)[..., 0:64] = np.asarray(inp["cache_diff_v"], f).reshape(DEPTH, NPOOL * 128, 4, 64)
    pa[:, :, 1032:1036] = np.asarray(inp["cache_fox_logf"], f).reshape(DEPTH, NPOOL * 128, 4)
    for i in range(DEPTH):
        sh[f"pool_all{i}"] = pa[i]
    dg = np.zeros((128, DEPTH * 2), f)
    for l in range(DEPTH):
        dg[:, 2 * l] = np.tile(np.asarray(inp["diff_norm"], f)[l], 2)
        dg[:, 2 * l + 1] = np.tile(np.asarray(inp["delta_norm"], f)[l], 2)
    sh["dgain"] = dg
    cw = np.asarray(inp["delta_conv_w"], f)
    sh["convw"] = np.ascontiguousarray(cw.reshape(DEPTH, 4, 6, 128).transpose(0, 3, 2, 1)).reshape(DEPTH, 128, 24)
    return sh


def _prep_core(inp, c):
    f = np.float32
    m = {}
    xs = np.asarray(inp["x_sample"], f)[NBS * c:NBS * (c + 1)].reshape(NS, D)
    m["xT0"] = np.ascontiguousarray(np.concatenate([np.asarray(inp["x_prompt"], f)[c], xs], axis=0).T)
    sc = np.asarray(inp["state_conv"], f)[:, NBS * c:NBS * (c + 1)]
    m["convs_in"] = np.ascontiguousarray(sc.transpose(0, 3, 1, 2)).reshape(DEPTH, 768, NBS * 3)
    m["ptab"] = np.ascontiguousarray(np.asarray(inp["page_table"], np.int32)[NBS * c:NBS * (c + 1)]).reshape(1, NBS * 16)
    sd = np.asarray(inp["state_delta"], f)[:, NBS * c:NBS * (c + 1)]
    m["sdel_in"] = np.ascontiguousarray(sd.reshape(DEPTH, NBS, 2, 2, 64, 64).transpose(0, 1, 3, 4, 2, 5)).reshape(DEPTH, NBS, 128, 128)
    return m


_CACHE = {}
_UID = [0]


def kernel(**inp):
    if "nc" not in _CACHE:
        _CACHE["nc"] = build(*_CACHE.get("args", ()))
    nc = _CACHE["nc"]
    sh = _prep_shared(inp)
    in_maps = []
    for c in range(NCORES):
        m = dict(sh)
        m.update(_prep_core(inp, c))
        in_maps.append(m)
    res = run_bass_kernel_spmd(nc, in_maps, core_ids=list(range(NCORES)))
    R = res.results
    _CACHE["R"] = R
    f = np.float32

    def tm(name, c):
        return R[c][name]

    outs = {}
    yT = [R[c]["yT"] for c in range(NCORES)]
    y_prompt = np.stack([yT[c][:, :NTP].T for c in range(NCORES)])
    y_sample = np.concatenate([yT[c][:, NTP:].T.reshape(NBS, 4, D) for c in range(NCORES)])

    def fm_out(name, shp_tail):
        p = np.stack([R[c][name][:, :, :NTP].transpose(0, 2, 1) for c in range(NCORES)], axis=1)
        s = np.concatenate([R[c][name][:, :, NTP:].transpose(0, 2, 1).reshape(DEPTH, NBS, 4, -1) for c in range(NCORES)], axis=1)
        return p.reshape((DEPTH, NCORES, NTP) + shp_tail), s.reshape((DEPTH, NCORES * NBS, 4) + shp_tail)

    def tm_out(name, shp_tail):
        p = np.stack([R[c][name][:, :NTP] for c in range(NCORES)], axis=1)
        s = np.concatenate([R[c][name][:, NTP:].reshape(DEPTH, NBS, 4, -1) for c in range(NCORES)], axis=1)
        return p.reshape((DEPTH, NCORES, NTP) + shp_tail), s.reshape((DEPTH, NCORES * NBS, 4) + shp_tail)

    fk_p, fk_s = fm_out("fkT_o", (4, 64))
    fv_p, fv_s = tm_out("fv_o", (4, 64))
    lf_p, lf_s = fm_out("lfT_o", (4,))
    dk_p, dk_s = fm_out("dkT_o", (4, 2, 32))
    dv_p, dv_s = tm_out("dv_o", (4, 64))
    cv_p = np.stack([R[c]["cvp_o"].transpose(0, 2, 1) for c in range(NCORES)], axis=1)
    cv_s = np.concatenate([R[c]["cvs_o"].reshape(DEPTH, 768, NBS, 3).transpose(0, 2, 3, 1) for c in range(NCORES)], axis=1)
    ds_p = np.stack([R[c]["dsp_o"].reshape(DEPTH, 2, 64, 2, 64).transpose(0, 3, 1, 2, 4).reshape(DEPTH, 4, 64, 64) for c in range(NCORES)], axis=1)
    ds_s = np.concatenate([R[c]["dss_o"].reshape(DEPTH, NBS, 2, 64, 2, 64).transpose(0, 1, 4, 2, 3, 5).reshape(DEPTH, NBS, 4, 64, 64) for c in range(NCORES)], axis=1)
    return tuple(np.ascontiguousarray(a, dtype=f) for a in (y_prompt, y_sample, fk_p, fk_s, fv_p, fv_s, lf_p, lf_s, dk_p, dk_s, dv_p, dv_s, ds_p, ds_s, cv_p, cv_s))
```
